# Optimizing a Trainium2 kernel written in Bass

```python
import jax
import jax.numpy as jnp
from jax import lax
import numpy as np

D_MODEL = 1024
BATCH = 4
SEQ = 4096
DEPTH = 2
DEC_BATCH = 128
DEC_SEQ = 8
PAST_LEN = 16384
PAGE_SIZE = 128

RET_HEADS = 8
RET_DK = 64
RET_DV = 128
RET_CHUNK = 128
SWA_HEADS = 8
SWA_KV_HEADS = 2
SWA_GROUP = SWA_HEADS // SWA_KV_HEADS
SWA_HD = 64
WINDOW = 128
MEM_HEADS = 4
MEM_HD = 128
N_MEM = 256
D_FF = 4 * D_MODEL
ROPE_THETA = 10000.0
LN_EPS = 1e-5
GN_EPS = 1e-5
ALPHA = (2 * DEPTH) ** 0.25
BETA = (8 * DEPTH) ** -0.25

RET_QK_W = RET_HEADS * RET_DK
RET_V_W = RET_HEADS * RET_DV
SWA_Q_W = SWA_HEADS * SWA_HD
SWA_KV_W = SWA_KV_HEADS * SWA_HD
MEM_W = MEM_HEADS * MEM_HD
IN_WIDTHS = (RET_QK_W, RET_QK_W, RET_V_W, RET_V_W, SWA_Q_W, SWA_KV_W, SWA_KV_W, MEM_W, D_MODEL, D_MODEL, D_MODEL)
IN_IS_VALUE = (False, False, True, False, False, False, True, False, False, False, False)
IN_W = sum(IN_WIDTHS)
IN_OFFSETS = [int(o) for o in np.cumsum(IN_WIDTHS)[:-1]]

kernel_name = 'hybrid_retention_sinkswa_memxattn_step'

F32 = jnp.float32


def layer_norm(x, g, b):
    xf = x.astype(F32)
    mu = jnp.mean(xf, -1, keepdims=True)
    var = jnp.mean(jnp.square(xf - mu), -1, keepdims=True)
    return ((xf - mu) * lax.rsqrt(var + LN_EPS) * g.astype(F32) + b.astype(F32)).astype(x.dtype)


def rope(x, pos):
    half = x.shape[-1] // 2
    inv = jnp.power(ROPE_THETA, -jnp.arange(half, dtype=F32) / half)
    ang = pos.astype(F32)[:, None] * inv[None, :]
    cos = jnp.cos(ang)[:, None, :]
    sin = jnp.sin(ang)[:, None, :]
    xf = x.astype(F32)
    x1, x2 = xf[..., :half], xf[..., half:]
    return jnp.concatenate([x1 * cos - x2 * sin, x2 * cos + x1 * sin], -1).astype(x.dtype)


def ret_log_gamma():
    return jnp.log1p(-jnp.exp2(-5.0 - jnp.arange(RET_HEADS, dtype=F32)))


def retention_chunk(q, k, v, s0):
    L = q.shape[1]
    lg = ret_log_gamma()
    idx = jnp.arange(L, dtype=F32)
    diff = idx[:, None] - idx[None, :]
    decay = jnp.where(diff >= 0, jnp.exp(lg[:, None, None] * jnp.maximum(diff, 0.0)), 0.0)
    qf = q.astype(F32)
    kf = k.astype(F32) * (RET_DK ** -0.5)
    vf = v.astype(F32)
    s0 = s0.astype(F32)
    scores = jnp.einsum('bihd,bjhd->bhij', qf, kf) * decay
    o = jnp.einsum('bhij,bjhe->bihe', scores, vf)
    o = o + jnp.einsum('bihd,bhde->bihe', qf, s0) * jnp.exp(lg[None, :] * (idx[:, None] + 1.0))[None, :, :, None]
    w_end = jnp.exp(lg[None, :] * (L - 1.0 - idx[:, None]))
    s_new = jnp.exp(lg * L)[None, :, None, None] * s0 + jnp.einsum('bjhd,bjhe,jh->bhde', kf, vf, w_end)
    return o, s_new


def retention_prompt(q, k, v):
    B, S = q.shape[:2]
    n = S // RET_CHUNK

    def blocks(t):
        return jnp.moveaxis(t.reshape(B, n, RET_CHUNK, *t.shape[2:]), 1, 0)

    def step(s, qkv):
        o, s = retention_chunk(qkv[0], qkv[1], qkv[2], s)
        return s, o

    s0 = jnp.zeros((B, RET_HEADS, RET_DK, RET_DV), F32)
    s_end, o = lax.scan(step, s0, (blocks(q), blocks(k), blocks(v)))
    return jnp.moveaxis(o, 0, 1).reshape(B, S, RET_HEADS, RET_DV), s_end


def group_norm(o, g):
    mu = jnp.mean(o, -1, keepdims=True)
    var = jnp.mean(jnp.square(o - mu), -1, keepdims=True)
    return (o - mu) * lax.rsqrt(var + GN_EPS) * g.astype(F32)


def band_mask(tq, tk, off):
    i = jnp.arange(tq)[:, None]
    j = jnp.arange(tk)[None, :]
    return (j <= i + off) & (j > i + off - WINDOW)


def sink_window_attend(q, k, v, valid, sinks):
    s = jnp.einsum('bnqhgd,bnkhd->bnhgqk', q.astype(F32), k.astype(F32)) * (SWA_HD ** -0.5)
    s = jnp.where(valid[None, :, None, None], s, -jnp.inf)
    sink = jnp.broadcast_to(sinks.astype(F32).reshape(SWA_KV_HEADS, SWA_GROUP)[None, None, :, :, None, None],
                            s.shape[:-1] + (1,))
    p = jax.nn.softmax(jnp.concatenate([s, sink], -1), axis=-1)[..., :-1]
    return jnp.einsum('bnhgqk,bnkhd->bnqhgd', p, v.astype(F32))


def swa_prompt(q, k, v, sinks):
    B, S = q.shape[:2]
    n = S // WINDOW
    qb = q.reshape(B, n, WINDOW, SWA_KV_HEADS, SWA_GROUP, SWA_HD)
    kb = k.reshape(B, n, WINDOW, SWA_KV_HEADS, SWA_HD)
    vb = v.reshape(B, n, WINDOW, SWA_KV_HEADS, SWA_HD)

    def with_prev(t):
        prev = jnp.concatenate([jnp.zeros_like(t[:, :1]), t[:, :-1]], axis=1)
        return jnp.concatenate([prev, t], axis=2)

    j = jnp.arange(2 * WINDOW)
    not_pad = (jnp.arange(n)[:, None, None] > 0) | (j >= WINDOW)[None, None, :]
    valid = band_mask(WINDOW, 2 * WINDOW, WINDOW)[None] & not_pad
    o = sink_window_attend(qb, with_prev(kb), with_prev(vb), valid, sinks)
    return o.reshape(B, S, SWA_KV_HEADS, SWA_GROUP, SWA_HD)


def swa_sample(q, k_ctx, v_ctx, sinks):
    T = q.shape[1]
    lb = k_ctx.shape[1] - T
    valid = band_mask(T, lb + T, lb)[None]
    o = sink_window_attend(q[:, None], k_ctx[:, None], v_ctx[:, None], valid, sinks)
    return o[:, 0]


def mem_kv(mem, w):
    B, M, _ = mem.shape
    k, v = jnp.split(mem @ w, 2, axis=-1)
    return k.reshape(B, M, MEM_HEADS, MEM_HD), v.reshape(B, M, MEM_HEADS, MEM_HD)


def mem_attend(q, km, vm):
    s = jnp.einsum('bthd,bmhd->bhtm', q.astype(F32), km.astype(F32)) * (MEM_HD ** -0.5)
    p = jax.nn.softmax(s, axis=-1)
    return jnp.einsum('bhtm,bmhd->bthd', p, vm.astype(F32))


def in_proj(x, w_in, pos):
    B, T, _ = x.shape
    rq, rk, rv, rg, sq, sk, sv, mq, g_ret, g_swa, g_mem = jnp.split(x @ w_in, IN_OFFSETS, axis=-1)
    rq = rope(rq.reshape(B, T, RET_HEADS, RET_DK), pos)
    rk = rope(rk.reshape(B, T, RET_HEADS, RET_DK), pos)
    rv = rv.reshape(B, T, RET_HEADS, RET_DV)
    sq = rope(sq.reshape(B, T, SWA_HEADS, SWA_HD), pos).reshape(B, T, SWA_KV_HEADS, SWA_GROUP, SWA_HD)
    sk = rope(sk.reshape(B, T, SWA_KV_HEADS, SWA_HD), pos)
    sv = sv.reshape(B, T, SWA_KV_HEADS, SWA_HD)
    mq = mq.reshape(B, T, MEM_HEADS, MEM_HD)
    return rq, rk, rv, rg, sq, sk, sv, mq, g_ret, g_swa, g_mem


def finish_layer(x, ret_o, rg, g_ret, swa_o, g_swa, mem_o, g_mem,
                 gn_g, w_br_ret, w_br_swa, w_br_mem, w_out, ln1_g, ln1_b, w_up, w_down, ln2_g, ln2_b):
    B, T, _ = x.shape
    dt = x.dtype
    ret_b = (jax.nn.silu(rg.astype(F32)) * group_norm(ret_o, gn_g).reshape(B, T, RET_V_W)).astype(dt) @ w_br_ret
    swa_b = swa_o.reshape(B, T, SWA_Q_W).astype(dt) @ w_br_swa
    mem_b = mem_o.reshape(B, T, MEM_W).astype(dt) @ w_br_mem
    merged = jax.nn.sigmoid(g_ret) * ret_b + jax.nn.sigmoid(g_swa) * swa_b + jax.nn.sigmoid(g_mem) * mem_b
    x = layer_norm(ALPHA * x + merged @ w_out, ln1_g, ln1_b)
    h = jnp.square(jax.nn.relu(x @ w_up))
    return layer_norm(ALPHA * x + h @ w_down, ln2_g, ln2_b)


def setup_inputs(seed: int = 0) -> dict:
    key = jax.random.key(seed)
    ks = jax.random.split(key, 24)

    def nrm(k, shape, s=1.0):
        return jax.random.normal(k, shape, F32) * s

    buf = min(WINDOW, PAST_LEN)
    col_scale = jnp.asarray(np.concatenate(
        [np.full((w,), BETA if isv else 1.0, np.float32) for w, isv in zip(IN_WIDTHS, IN_IS_VALUE)]))
    mem_scale = jnp.asarray(np.concatenate([np.ones((MEM_W,), np.float32), np.full((MEM_W,), BETA, np.float32)]))
    return {
        'x_prompt': nrm(ks[0], (BATCH, SEQ, D_MODEL)),
        'x_sample': nrm(ks[1], (DEC_BATCH, DEC_SEQ, D_MODEL)),
        'state_ret': nrm(ks[2], (DEPTH, DEC_BATCH, RET_HEADS, RET_DK, RET_DV), 0.5),
        'cache_swa_k': nrm(ks[3], (DEPTH, DEC_BATCH, buf, SWA_KV_HEADS, SWA_HD)),
        'cache_swa_v': nrm(ks[4], (DEPTH, DEC_BATCH, buf, SWA_KV_HEADS, SWA_HD), BETA),
        'cache_mem_k': nrm(ks[5], (DEPTH, DEC_BATCH, N_MEM, MEM_HEADS, MEM_HD)),
        'cache_mem_v': nrm(ks[6], (DEPTH, DEC_BATCH, N_MEM, MEM_HEADS, MEM_HD), BETA),
        'mem_prompt': nrm(ks[7], (BATCH, N_MEM, D_MODEL)),
        'w_in': nrm(ks[8], (DEPTH, D_MODEL, IN_W), D_MODEL ** -0.5) * col_scale,
        'w_br_ret': nrm(ks[9], (DEPTH, RET_V_W, D_MODEL), RET_V_W ** -0.5 * BETA),
        'w_br_swa': nrm(ks[10], (DEPTH, SWA_Q_W, D_MODEL), SWA_Q_W ** -0.5 * BETA),
        'w_br_mem': nrm(ks[11], (DEPTH, MEM_W, D_MODEL), MEM_W ** -0.5 * BETA),
        'w_out': nrm(ks[12], (DEPTH, D_MODEL, D_MODEL), D_MODEL ** -0.5 * BETA),
        'w_mem_kv': nrm(ks[13], (DEPTH, D_MODEL, 2 * MEM_W), D_MODEL ** -0.5) * mem_scale,
        'attn_sinks': nrm(ks[14], (DEPTH, SWA_HEADS), 0.5),
        'ret_gn_g': 1.0 + nrm(ks[15], (DEPTH, RET_HEADS, RET_DV), 0.02),
        'ln1_g': 1.0 + nrm(ks[16], (DEPTH, D_MODEL), 0.02),
        'ln1_b': nrm(ks[17], (DEPTH, D_MODEL), 0.02),
        'w_up': nrm(ks[18], (DEPTH, D_MODEL, D_FF), D_MODEL ** -0.5),
        'w_down': nrm(ks[19], (DEPTH, D_FF, D_MODEL), D_FF ** -0.5 * BETA),
        'ln2_g': 1.0 + nrm(ks[20], (DEPTH, D_MODEL), 0.02),
        'ln2_b': nrm(ks[21], (DEPTH, D_MODEL), 0.02),
    }


def reference(x_prompt, x_sample, state_ret, cache_swa_k, cache_swa_v, cache_mem_k, cache_mem_v, mem_prompt,
              w_in, w_br_ret, w_br_swa, w_br_mem, w_out, w_mem_kv, attn_sinks, ret_gn_g,
              ln1_g, ln1_b, w_up, w_down, ln2_g, ln2_b):
    pos_p = jnp.arange(x_prompt.shape[1], dtype=jnp.int32)
    pos_s = PAST_LEN + jnp.arange(x_sample.shape[1], dtype=jnp.int32)
    xp, xs = x_prompt, x_sample
    ret_p, swk_p, swv_p, mk_p, mv_p = [], [], [], [], []
    ret_s, swk_s, swv_s = [], [], []
    for l in range(DEPTH):
        lw = (ret_gn_g[l], w_br_ret[l], w_br_swa[l], w_br_mem[l], w_out[l],
              ln1_g[l], ln1_b[l], w_up[l], w_down[l], ln2_g[l], ln2_b[l])
        rq, rk, rv, rg, sq, sk, sv, mq, g_r, g_s, g_m = in_proj(xp, w_in[l], pos_p)
        ret_o, s_p = retention_prompt(rq, rk, rv)
        swa_o = swa_prompt(sq, sk, sv, attn_sinks[l])
        mk, mv = mem_kv(mem_prompt, w_mem_kv[l])
        mem_o = mem_attend(mq, mk, mv)
        xp = finish_layer(xp, ret_o, rg, g_r, swa_o, g_s, mem_o, g_m, *lw)
        nbp = min(WINDOW, sk.shape[1])
        ret_p.append(s_p)
        swk_p.append(sk[:, -nbp:])
        swv_p.append(sv[:, -nbp:])
        mk_p.append(mk)
        mv_p.append(mv)
        rq, rk, rv, rg, sq, sk, sv, mq, g_r, g_s, g_m = in_proj(xs, w_in[l], pos_s)
        ret_o, s_s = retention_chunk(rq, rk, rv, state_ret[l])
        k_ctx = jnp.concatenate([cache_swa_k[l].astype(sk.dtype), sk], axis=1)
        v_ctx = jnp.concatenate([cache_swa_v[l].astype(sv.dtype), sv], axis=1)
        swa_o = swa_sample(sq, k_ctx, v_ctx, attn_sinks[l])
        mem_o = mem_attend(mq, cache_mem_k[l], cache_mem_v[l])
        xs = finish_layer(xs, ret_o, rg, g_r, swa_o, g_s, mem_o, g_m, *lw)
        nbs = min(WINDOW, k_ctx.shape[1])
        ret_s.append(s_s)
        swk_s.append(k_ctx[:, -nbs:])
        swv_s.append(v_ctx[:, -nbs:])
    return (xp, xs, jnp.stack(ret_p), jnp.stack(swk_p), jnp.stack(swv_p), jnp.stack(mk_p), jnp.stack(mv_p),
            jnp.stack(ret_s), jnp.stack(swk_s), jnp.stack(swv_s))
```

```python
import numpy as np
import concourse.bass as bass
import concourse.mybir as mybir
from concourse.bass_utils import run_bass_kernel_spmd

F32 = mybir.dt.float32
BF16 = mybir.dt.bfloat16
AF = mybir.ActivationFunctionType
ALU = mybir.AluOpType

D = 1024
DEPTH = 2
SEQ = 4096
BATCH = 4
DEC_BATCH = 128
DEC_SEQ = 8
PAST_LEN = 16384
RH, RDK, RDV = 8, 64, 128
SH, SKV, SHD = 8, 2, 64
MH, MHD, NMEM = 4, 128, 256
DFF = 4096
IN_W = 7424
ALPHA = (2 * DEPTH) ** 0.25
LN_EPS = 1e-5
GN_EPS = 1e-5
NEG = -2000.0
NCORES = 8
SB_PER_CORE = DEC_BATCH // NCORES


def _dsize(dt):
    return mybir.dt.size(dt)


class Tracker:
    def __init__(self, nc):
        self.nc = nc
        self.ops = []
        self.psum_last = {}
        self.hist = {}
        self.phase = 0
        self.eng_objs = {"pe": nc.tensor, "dve": nc.vector, "act": nc.scalar,
                         "pool": nc.gpsimd, "sp": nc.sync}

    @staticmethod
    def box(ap):
        t = ap.tensor
        name = t.name
        dims = ap.ap
        esz = _dsize(ap.dtype)
        space = str(ap.space)
        if "DRAM" in space.upper() or "HBM" in space.upper() or "dram" in space:
            lo = ap.offset
            hi = lo + sum((c - 1) * s for s, c in dims if c > 0)
            return name, (0, 0, lo * esz, (hi + 1) * esz), True
        fstride = dims[0][0]
        p0 = ap.offset // fstride
        f0 = ap.offset % fstride
        p1 = p0 + dims[0][1] - 1
        f1 = f0 + sum((c - 1) * s for s, c in dims[1:])
        return name, (p0, p1, f0 * esz, (f1 + 1) * esz), False

    @staticmethod
    def overlap(a, b):
        return not (a[1] < b[0] or b[1] < a[0] or a[3] <= b[2] or b[3] <= a[2])

    @staticmethod
    def contains(a, b):
        return a[0] <= b[0] and a[1] >= b[1] and a[2] <= b[2] and a[3] >= b[3]

    def add(self, eng, fn, w=(), r=(), dma_key=None, untracked=(), inc=None):
        opid = len(self.ops)
        deps = set()
        is_dma = dma_key is not None
        rboxes = [self.box(a) for a in r]
        wboxes = [self.box(a) for a in w]
        psum_names = set()
        for name, bx, isdram in rboxes + wboxes:
            if name.startswith("PB") or name.startswith("PT"):
                psum_names.add(name)
        for name in psum_names:
            last = self.psum_last.setdefault(name, {})
            for e2, o2 in last.items():
                if e2 != eng:
                    deps.add(o2)
            last[eng] = opid
        rboxes = [b for b in rboxes if b[0] not in psum_names]
        wboxes = [b for b in wboxes if b[0] not in psum_names]
        for name, bx, isdram in rboxes:
            if name in untracked:
                continue
            h = self.hist.setdefault(name, [])
            for (b2, o2, isw, e2, d2) in h:
                if isw and self.overlap(bx, b2):
                    deps.add(o2)
        for name, bx, isdram in wboxes:
            if name in untracked:
                continue
            h = self.hist.setdefault(name, [])
            for (b2, o2, isw, e2, d2) in h:
                if self.overlap(bx, b2):
                    deps.add(o2)
        for name, bx, isdram in rboxes:
            if name in untracked:
                continue
            h = self.hist[name]
            if not is_dma:
                h[:] = [e for e in h if not ((not e[2]) and e[3] == eng and (not e[4])
                                             and self.contains(bx, e[0]))]
            h.append((bx, opid, False, eng, is_dma))
        for name, bx, isdram in wboxes:
            if name in untracked:
                continue
            h = self.hist[name]
            h[:] = [e for e in h if not self.contains(bx, e[0])]
            h.append((bx, opid, True, eng, is_dma))
        deps.discard(opid)
        self.ops.append(dict(eng=eng, fn=fn, deps=deps, dma_key=dma_key, phase=self.phase,
                             signal=False, inc_override=inc))
        return opid

    def emit(self, final_wait_ops):
        nc = self.nc
        ops = self.ops
        for o in ops:
            if o["eng"] == "pe" and o["dma_key"] is None:
                o["deps"] = {d for d in o["deps"]
                             if not (ops[d]["eng"] == "pe" and ops[d]["dma_key"] is None)}
        for o in ops:
            for d in o["deps"]:
                ops[d]["signal"] = True
        for d in final_wait_ops:
            ops[d]["signal"] = True
        for o in ops:
            if o["dma_key"] is not None:
                o["signal"] = True
        sem_keys = []
        counts = {}
        for o in ops:
            if not o["signal"]:
                continue
            if o["dma_key"] is not None:
                k = ("dma", o["dma_key"])
                inc = 16 if o["inc_override"] is None else o["inc_override"]
            else:
                k = (o["eng"], o["phase"])
                inc = 1
            if k not in counts:
                counts[k] = 0
                sem_keys.append(k)
            counts[k] += inc
            o["sem"] = k
            o["val"] = counts[k]
            o["inc"] = inc
        rng = bass.get_kernel_semaphore_range()
        assert len(sem_keys) <= len(rng) - 2, f"too many semaphores: {len(sem_keys)}"
        sems = {}
        self._sem_cms = []
        for k in sem_keys:
            cm = nc.semaphore(("s_%s_%s" % k).replace("-", "_"))
            s = cm.__enter__()
            self._sem_cms.append(cm)
            sems[k] = s
        waited = {}
        nwait = 0
        issued = {}
        for o in ops:
            eobj = self.eng_objs[o["eng"]]
            need = {}
            for d in o["deps"]:
                od = ops[d]
                k = od["sem"]
                if od["dma_key"] is not None:
                    need[k] = max(need.get(k, 0), issued[k])
                else:
                    need[k] = max(need.get(k, 0), od["val"])
            for k, v in need.items():
                wk = (o["eng"], k)
                if waited.get(wk, 0) >= v:
                    continue
                eobj.wait_ge(sems[k], v)
                waited[wk] = v
                nwait += 1
            ins = o["fn"]()
            if o["signal"]:
                ins.then_inc(sems[o["sem"]], o["inc"])
                if o["dma_key"] is not None:
                    issued[o["sem"]] = o["val"]
        need = {}
        for d in final_wait_ops:
            od = ops[d]
            need[od["sem"]] = max(need.get(od["sem"], 0), od["val"])
        for k, v in need.items():
            nc.sync.wait_ge(sems[k], v)
        self.stats = dict(nops=len(ops), nwait=nwait, nsem=len(sem_keys),
                          maxcount=max(counts.values()) if counts else 0)

    def close(self):
        for cm in reversed(self._sem_cms):
            cm.__exit__(None, None, None)


def ret_gammas():
    return (1.0 - np.exp2(-5.0 - np.arange(RH, dtype=np.float64)))


def host_consts():
    c = {}
    c["ident"] = np.eye(128, dtype=np.float32)
    half = 32
    inv = np.power(np.float32(10000.0), -np.arange(half, dtype=np.float32) / np.float32(half)).astype(np.float32)
    pos = np.arange(SEQ, dtype=np.float32)
    ang = (pos[:, None] * inv[None, :]).astype(np.float32)
    cosp = np.cos(ang).astype(np.float32).reshape(SEQ // 128, 128, half).transpose(1, 0, 2)
    sinp = np.sin(ang).astype(np.float32).reshape(SEQ // 128, 128, half).transpose(1, 0, 2)
    c["cosp"] = np.ascontiguousarray(cosp)
    c["sinp"] = np.ascontiguousarray(sinp)
    c["nsinp"] = np.ascontiguousarray(-sinp)
    poss = (PAST_LEN + (np.arange(128) % DEC_SEQ)).astype(np.float32)
    angs = (poss[:, None] * inv[None, :]).astype(np.float32)
    c["coss"] = np.cos(angs).astype(np.float32).reshape(128, 1, half)
    c["sins"] = np.sin(angs).astype(np.float32).reshape(128, 1, half)
    c["nsins"] = (-c["sins"]).astype(np.float32)
    g = ret_gammas()
    lg = np.log(g)
    j = np.arange(128)[:, None]
    i = np.arange(128)[None, :]
    dec = np.zeros((128, RH, 128), np.float64)
    decs = np.zeros((128, RH, 128), np.float64)
    for h in range(RH):
        dec[:, h, :] = np.where(i >= j, np.exp(lg[h] * np.maximum(i - j, 0)), 0.0) * RDK ** -0.5
        same = (i // DEC_SEQ) == (j // DEC_SEQ)
        decs[:, h, :] = np.where((i >= j) & same, np.exp(lg[h] * np.maximum(i - j, 0)), 0.0) * RDK ** -0.5
    c["decp"] = dec.astype(np.float32)
    c["decs"] = decs.astype(np.float32)
    gq = np.zeros((128, 4, 128), np.float64)
    gqs = np.zeros((128, 4, 128), np.float64)
    gl = np.zeros((128, 4), np.float64)
    gls = np.zeros((128, 4), np.float64)
    ii = np.arange(128)
    for pr in range(4):
        for hh in range(2):
            h = 2 * pr + hh
            gq[hh * 64:(hh + 1) * 64, pr, :] = np.exp(lg[h] * (ii + 1.0))[None, :]
            gqs[hh * 64:(hh + 1) * 64, pr, :] = np.exp(lg[h] * ((ii % DEC_SEQ) + 1.0))[None, :]
            gl[hh * 64:(hh + 1) * 64, pr] = np.exp(lg[h] * 128.0)
            gls[hh * 64:(hh + 1) * 64, pr] = np.exp(lg[h] * float(DEC_SEQ))
    c["gqp"] = gq.astype(np.float32)
    c["gqs"] = gqs.astype(np.float32)
    c["glp"] = gl.astype(np.float32)
    c["gls"] = gls.astype(np.float32)
    wend = np.exp(lg[None, :] * (127.0 - ii[:, None])) * RDK ** -0.5
    wends = np.exp(lg[None, :] * (DEC_SEQ - 1.0 - (ii[:, None] % DEC_SEQ))) * RDK ** -0.5
    c["wendp"] = wend.astype(np.float32)
    c["wends"] = wends.astype(np.float32)
    own = np.where(j <= i, 0.0, NEG)
    prev = np.where(j > i, 0.0, NEG)
    c["mown"] = np.tile(own, (1, 4)).astype(np.float32)
    c["mprev"] = np.tile(prev, (1, 4)).astype(np.float32)
    same = (i // DEC_SEQ) == (j // DEC_SEQ)
    news = np.where(same & (j <= i), 0.0, NEG)
    c["mnews"] = np.tile(news, (1, 4)).astype(np.float32)
    kk = np.arange(128)[:, None]
    t8 = np.arange(DEC_SEQ)[None, :]
    mc = np.where(kk >= t8 + 1, 0.0, NEG)
    c["mcache"] = np.tile(mc, (1, 64)).astype(np.float32)
    bm = (np.arange(128)[:, None] // DEC_SEQ == np.arange(SB_PER_CORE)[None, :]).astype(np.float32)
    c["bmrow"] = bm
    return c


CONST_SHAPES = None


class Builder:
    def __init__(self, cfg):
        self.cfg = cfg
        self.nc = bass.Bass("TRN2", target_bir_lowering=False)
        self.T = Tracker(self.nc)
        self.cms = []
        self.final_ops = []

    def sb(self, name, shape, dt):
        cm = self.nc.sbuf_tensor(name, list(shape), dt)
        t = cm.__enter__()
        self.cms.append(cm)
        return t

    def ps(self, name, shape, dt):
        cm = self.nc.psum_tensor(name, list(shape), dt)
        t = cm.__enter__()
        self.cms.append(cm)
        return t

    def din(self, name, shape, dt=F32):
        return self.nc.dram_tensor(name, list(shape), dt, kind="ExternalInput").ap()

    def dout(self, name, shape, dt=F32):
        return self.nc.dram_tensor(name, list(shape), dt, kind="ExternalOutput").ap()

    def mm(self, out, lhsT, rhs, start, stop, skip=False):
        nc = self.nc
        if skip:
            return self.T.add("pe", lambda: nc.tensor.matmul(out, lhsT=lhsT, rhs=rhs, start=start, stop=stop,
                                                             skip_group_check=True),
                              w=[out], r=[lhsT, rhs, out])
        return self.T.add("pe", lambda: nc.tensor.matmul(out, lhsT=lhsT, rhs=rhs, start=start, stop=stop),
                          w=[out], r=[lhsT, rhs] + ([] if start else [out]))

    def trf(self, out, in_, identf):
        nc = self.nc
        return self.T.add("pe", lambda: nc.tensor.transpose(out=out, in_=in_, identity=identf),
                          w=[out], r=[in_, identf])

    def tr(self, out, in_):
        nc = self.nc
        ident = self.identb[:]
        return self.T.add("pe", lambda: nc.tensor.transpose(out=out, in_=in_, identity=ident),
                          w=[out], r=[in_, ident])

    def act(self, out, in_, func, scale=1.0, bias=None):
        nc = self.nc
        r = [in_]
        kw = {}
        if not isinstance(scale, float):
            r.append(scale)
        if bias is not None:
            kw["bias"] = bias
            if not isinstance(bias, float):
                r.append(bias)
        return self.T.add("act", lambda: nc.scalar.activation(out=out, in_=in_, func=func, scale=scale, **kw),
                          w=[out], r=r)

    def tt(self, out, in0, in1, op, eng="dve"):
        nc = self.nc
        e = nc.vector if eng == "dve" else nc.gpsimd
        return self.T.add(eng, lambda: e.tensor_tensor(out=out, in0=in0, in1=in1, op=op), w=[out], r=[in0, in1])

    def ts(self, out, in0, s1, op0, s2=None, op1=None, eng="dve"):
        nc = self.nc
        e = nc.vector if eng == "dve" else nc.gpsimd
        r = [in0] + [s for s in (s1, s2) if s is not None and not isinstance(s, float)]
        if op1 is None:
            return self.T.add(eng, lambda: e.tensor_scalar(out=out, in0=in0, scalar1=s1, scalar2=None, op0=op0),
                              w=[out], r=r)
        return self.T.add(eng, lambda: e.tensor_scalar(out=out, in0=in0, scalar1=s1, scalar2=s2, op0=op0, op1=op1),
                          w=[out], r=r)

    def stt(self, out, in0, scalar, in1, op0, op1):
        nc = self.nc
        r = [in0, in1] + ([] if isinstance(scalar, float) else [scalar])
        return self.T.add("dve", lambda: nc.vector.scalar_tensor_tensor(out=out, in0=in0, scalar=scalar, in1=in1,
                                                                         op0=op0, op1=op1), w=[out], r=r)

    def cp(self, out, in_, eng="dve"):
        nc = self.nc
        if eng == "act":
            return self.T.add("act", lambda: nc.scalar.copy(out=out, in_=in_), w=[out], r=[in_])
        e = nc.vector if eng == "dve" else nc.gpsimd
        return self.T.add(eng, lambda: e.tensor_copy(out=out, in_=in_), w=[out], r=[in_])

    def memset(self, ap, val, eng="dve"):
        nc = self.nc
        e = nc.vector if eng == "dve" else nc.gpsimd
        return self.T.add(eng, lambda: e.memset(ap, val), w=[ap])

    def dma(self, out, in_, key, q="sp", final=False):
        nc = self.nc
        e = {"sp": nc.sync, "pool": nc.gpsimd, "act": nc.scalar}[q]
        if key is None or not key.startswith("!"):
            on, inn = out.tensor.name, in_.tensor.name
            key = ("st_" + inn) if on in self.dram_names else ("ld_" + on)
        op = self.T.add(q, lambda: e.dma_start(out=out, in_=in_), w=[out], r=[in_], dma_key=key,
                        untracked=self.untracked)
        if final:
            self.final_ops.append(op)
        return op

    def bnstats(self, out, in_):
        nc = self.nc
        return self.T.add("dve", lambda: nc.vector.bn_stats(out=out, in_=in_), w=[out], r=[in_])

    def bnaggr(self, out, in_):
        nc = self.nc
        return self.T.add("dve", lambda: nc.vector.bn_aggr(out=out, in_=in_), w=[out], r=[in_])

    def recip(self, out, in_):
        nc = self.nc
        return self.T.add("dve", lambda: nc.vector.reciprocal(out=out, in_=in_), w=[out], r=[in_])


def V(t, off, dims):
    f = 1
    for s in t.shape[1:]:
        f *= s
    return bass.AP(t, off, [[f, 128]] + [list(d) for d in dims])


def VP(t, p0, pn, off, dims):
    f = 1
    for s in t.shape[1:]:
        f *= s
    return bass.AP(t, p0 * f + off, [[f, pn]] + [list(d) for d in dims])


def AV(view, off, dims, p0=0, pn=128):
    f = view.ap[0][0]
    base = view.offset
    return bass.AP(view.tensor, base + p0 * f + off, [[f, pn]] + [list(d) for d in dims])


def build(cfg):
    nsc = cfg["nsc"]
    nlay = cfg["nlay"]
    stage = cfg.get("stage", 99)
    sstage = cfg.get("sstage", 99)
    dbgname = cfg.get("dbg", None)
    do_sample = cfg["sample"]
    B = Builder(cfg)
    nc = B.nc
    T = B.T
    NTOK = nsc * 512
    x_p = B.din("x_p", [NTOK, D])
    mem_p = B.din("mem_p", [NMEM, D])
    x_s = B.din("x_s", [128, D])
    st_ret = B.din("st_ret", [DEPTH, SB_PER_CORE, RH, RDK, RDV])
    c_sk = B.din("c_sk", [DEPTH, SB_PER_CORE, 128, 128])
    c_sv = B.din("c_sv", [DEPTH, SB_PER_CORE, 128, 128])
    c_mk = B.din("c_mk", [DEPTH, SB_PER_CORE, NMEM, 512])
    c_mv = B.din("c_mv", [DEPTH, SB_PER_CORE, NMEM, 512])
    w_in = B.din("w_in", [DEPTH, D, IN_W])
    w_br_ret = B.din("w_br_ret", [DEPTH, 1024, D])
    w_br_swa = B.din("w_br_swa", [DEPTH, 512, D])
    w_br_mem = B.din("w_br_mem", [DEPTH, 512, D])
    w_out = B.din("w_out", [DEPTH, D, D])
    w_mem_kv = B.din("w_mem_kv", [DEPTH, D, 1024])
    sinks = B.din("sinks", [DEPTH, SH])
    gn_g = B.din("gn_g", [DEPTH, 1024])
    ln1_g = B.din("ln1_g", [DEPTH, D])
    ln1_b = B.din("ln1_b", [DEPTH, D])
    w_up = B.din("w_up", [DEPTH, D, DFF])
    w_down = B.din("w_down", [DEPTH, DFF, D])
    ln2_g = B.din("ln2_g", [DEPTH, D])
    ln2_b = B.din("ln2_b", [DEPTH, D])
    own = dict(w_in=B.din("wo_in", [D, IN_W]), w_br_ret=B.din("wo_br_ret", [1024, D]),
               w_br_swa=B.din("wo_br_swa", [512, D]), w_br_mem=B.din("wo_br_mem", [512, D]),
               w_out=B.din("wo_out", [D, D]), w_mem_kv=B.din("wo_mem_kv", [D, 1024]),
               sinks=B.din("o_sinks", [1, SH]), gn_g=B.din("o_gn_g", [1024]), ln1_g=B.din("o_ln1_g", [D]),
               ln1_b=B.din("o_ln1_b", [D]), w_up=B.din("wo_up", [D, DFF]), w_down=B.din("wo_down", [DFF, D]),
               ln2_g=B.din("o_ln2_g", [D]), ln2_b=B.din("o_ln2_b", [D]))
    own_names = [v.tensor.name for v in own.values()]
    sinks_full = sinks
    w_in = [w_in[0], w_in[1], own["w_in"]]
    w_br_ret = [w_br_ret[0], w_br_ret[1], own["w_br_ret"]]
    w_br_swa = [w_br_swa[0], w_br_swa[1], own["w_br_swa"]]
    w_br_mem = [w_br_mem[0], w_br_mem[1], own["w_br_mem"]]
    w_out = [w_out[0], w_out[1], own["w_out"]]
    w_mem_kv = [w_mem_kv[0], w_mem_kv[1], own["w_mem_kv"]]
    w_up = [w_up[0], w_up[1], own["w_up"]]
    w_down = [w_down[0], w_down[1], own["w_down"]]
    gn_g = [gn_g[0], gn_g[1], own["gn_g"]]
    ln1_g = [ln1_g[0], ln1_g[1], own["ln1_g"]]
    ln1_b = [ln1_b[0], ln1_b[1], own["ln1_b"]]
    ln2_g = [ln2_g[0], ln2_g[1], own["ln2_g"]]
    ln2_b = [ln2_b[0], ln2_b[1], own["ln2_b"]]
    NSLOT = nsc + 2
    sel_d = B.din("sel", [128, 2])
    flag_d = B.din("flagneg", [128, 512])
    r_cos = B.din("r_cos", [128, NSLOT * 4, 32])
    r_sin = B.din("r_sin", [128, NSLOT * 4, 32])
    r_nsin = B.din("r_nsin", [128, NSLOT * 4, 32])
    cc_src = [nc.dram_tensor("cc_src%d" % i, [512, D], F32, kind="Internal").ap() for i in range(2)]
    cc_dst = [nc.dram_tensor("cc_dst%d" % i, [2 * 512, D], F32, kind="Internal").ap() for i in range(2)]
    hc = host_consts()
    cd = {k: B.din("c_" + k, list(v.shape)) for k, v in hc.items()}
    y_p = B.dout("y_p", [NSLOT * 512, D])
    y_s = B.dout("y_s", [128, D])
    rs_p = B.dout("rs_p", [2, RH, RDK, RDV])
    sk_p = B.dout("sk_p", [2, 128, 128])
    sv_p = B.dout("sv_p", [2, 128, 128])
    mk_p = B.dout("mk_p", [NMEM, 512])
    mv_p = B.dout("mv_p", [NMEM, 512])
    rs_s = B.dout("rs_s", [DEPTH, SB_PER_CORE, RH, RDK, RDV])
    sk_s = B.dout("sk_s", [DEPTH, SB_PER_CORE, 128, 128])
    sv_s = B.dout("sv_s", [DEPTH, SB_PER_CORE, 128, 128])
    B.dram_names = set(t.tensor.name for t in [y_p, y_s, rs_p, sk_p, sv_p, mk_p, mv_p, rs_s, sk_s, sv_s])
    B.untracked = set(n.tensor.name for n in
                      [x_p, mem_p, x_s, st_ret, c_sk, c_sv, c_mk, c_mv, w_in[0], w_br_ret[0], w_br_swa[0], w_br_mem[0],
                       w_out[0],
                       w_mem_kv[0], sinks, gn_g[0], ln1_g[0], ln1_b[0], w_up[0], w_down[0], ln2_g[0], ln2_b[0],
                       sel_d, flag_d, r_cos, r_sin, r_nsin]
                      + list(cd.values())) | set(own_names)

    sb = B.sb
    B.identb = sb("identb", [128, 128], BF16)
    onesb = sb("onesb", [128, 128], BF16)
    X = sb("X", [128, 4, D], F32)
    XT = sb("XT", [128, 8, 512], BF16)
    Xb = sb("Xb", [128, D], BF16)
    ARENA = sb("ARENA", [128, 32 * 512], BF16)
    HT = ARENA[:].rearrange("p (a b) -> p a b", a=32)
    GT = ARENA[:, 0:24 * 512].rearrange("p (a b) -> p a b", a=24)
    MQT = ARENA[:, 24 * 512:28 * 512].rearrange("p (a b) -> p a b", a=4)
    RQT = ARENA[:, 28 * 512:32 * 512].rearrange("p (a b) -> p a b", a=4)
    PV_ = dict(HT=HT, GT=GT, MQT=MQT, RQT=RQT)
    SV_ = dict(HT=ARENA[:, 0:32 * 128].rearrange("p (a b) -> p a b", a=32),
               GT=ARENA[:, 0:24 * 128].rearrange("p (a b) -> p a b", a=24),
               MQT=ARENA[:, 24 * 128:28 * 128].rearrange("p (a b) -> p a b", a=4),
               RQT=ARENA[:, 28 * 128:32 * 128].rearrange("p (a b) -> p a b", a=4))
    a0 = 32 * 128
    RKwX = ARENA[:, a0:a0 + 4096].rearrange("p (r c) -> p r c", r=8)
    S0bg = ARENA[:, a0 + 4096:a0 + 6144].rearrange("p (b r e) -> p b r e", b=4, r=4)
    Ec = ARENA[:, a0 + 6144:a0 + 7168].rearrange("p (b g c) -> p b g c", b=16, g=2)
    KcT = ARENA[:, a0 + 7168:a0 + 9216].rearrange("p (b k) -> p b k", b=16)
    Kraw = ARENA[:, a0 + 9216:a0 + 11264].rearrange("p (b k) -> p b k", b=16)
    SQTs = ARENA[:, a0 + 11264:a0 + 12288].rearrange("p (g c) -> p g c", g=2)
    S0g = X[:, 1:3, :].rearrange("p a (b e) -> p (a b) e", e=128).rearrange("p (b r) e -> p b r e", b=4)
    Vcp = X[:, 3, :].bitcast(BF16).rearrange("p (b g h c) -> p b g h c", b=4, g=2, h=2)
    identF = sb("identF", [128, 128], F32)
    BMR = sb("BMR", [128, 16], F32)
    RKm = sb("RKm", [128, 4, 8, 128], BF16)
    SQT = sb("SQT", [128, 4, 512], BF16)
    SKm = sb("SKm", [128, 5, 2, 128], BF16)
    RKw = sb("RKw", [128, 4, 512], BF16)
    Vb = sb("Vb", [128, 4, 1024], BF16)
    MT = Vb[:].rearrange("p a (k t) -> p (a k) t", k=2)
    SRG = sb("SRG", [128, 4, 1024], BF16)
    SVp = sb("SVp", [128, 5, 2, 2, 128], BF16)
    ONp = sb("ONp", [128, 2, 128], BF16)
    GRT = sb("GRT", [128, 8, 512], BF16)
    SWOT = sb("SWOT", [128, 4, 512], BF16)
    MOT = sb("MOT", [128, 4, 512], BF16)
    T1 = sb("T1", [128, 512], F32)
    T2 = sb("T2", [128, 512], F32)
    RF = sb("RF", [128, 512], F32)
    RQb = sb("RQb", [128, 512], BF16)
    ST = sb("ST", [128, 8, 128], BF16)
    RQs2 = sb("RQs2", [128, 8, 128], BF16)
    Y = sb("Y", [128, 8, 128], F32)
    GR = sb("GR", [128, 1024], BF16)
    STAT = sb("STAT", [128, 8, 6], F32)
    MV = sb("MV", [128, 8, 2], F32)
    E = sb("E", [128, 4, 512], BF16)
    MEMT = E[:].rearrange("p a (k m) -> p (a k) m", k=2)
    identf = T2[:, 0:128]
    MSKf = T1
    DEN = sb("DEN", [128, 512], F32)
    LNB = sb("LNB", [128, 2, D], F32)
    GNG = sb("GNG", [128, 1024], F32)
    NW = 3
    WR = [sb("WR%d" % i, [128, 8, 512], BF16) for i in range(NW)]
    ROPE = sb("ROPE", [128, 3, 4, 32], F32)
    DEC = sb("DEC", [128, 8, 128], F32)
    GQ = sb("GQ", [128, 4, 128], F32)
    WEND = sb("WEND", [128, 8], F32)
    GL = sb("GL", [128, 4], F32)
    MOWN = sb("MOWN", [128, 512], BF16)
    MPREV = sb("MPREV", [128, 512], BF16)
    SINKE = sb("SINKE", [128, 3, 4], F32)
    Sst = [sb("Sst", [128, 4, 128], F32)] * 3
    Sbf = [sb("Sbf", [128, 4, 128], BF16)] * 3
    SKTprev = [sb("SKTp", [128, 2, 128], BF16)] * 3
    SVprev = [sb("SVpv", [128, 2, 2, 128], BF16)] * 3
    MKT = [sb("MKT", [128, 4, 256], BF16)] * 3
    MVb = [sb("MVb", [128, 2, 512], BF16)] * 3
    SEL = sb("SEL", [128, 2], F32)
    XbN = sb("XbN", [128, 4, D], BF16)
    XbS = Xb[:].bitcast(F32)
    FLAGN = sb("FLAGN", [128, 512], BF16)
    SKf = sb("SKf", [128, 256], F32)
    PB = [B.ps("PB%d" % i, [128, 512], F32) for i in range(6)]
    PT = [B.ps("PT%d" % i, [128, 1024], BF16) for i in range(2)]

    B.dma(identf, cd["ident"], "c0")
    B.cp(B.identb[:], identf)
    B.memset(onesb[:], 1.0)
    B.memset(ONp[:], 0.0)
    B.memset(ONp[:, 0, 0:64], 1.0)
    B.memset(ONp[:, 1, 64:128], 1.0)
    B.memset(SVp[:], 0.0)
    B.memset(RKm[:], 0.0)
    B.memset(SKm[:], 0.0)
    B.memset(RQs2[:], 0.0)
    for l in range(1):
        B.memset(SVprev[l][:], 0.0)
        B.memset(SKTprev[l][:], 0.0)
        B.memset(Sst[l][:], 0.0)
        B.memset(Sbf[l][:], 0.0)
    B.dma(SEL[:], sel_d, None)
    B.dma(MSKf[:], flag_d, None)
    B.cp(FLAGN[:], MSKf[:])
    B.dma(MSKf[:], cd["mown"], "c0")
    B.cp(MOWN[:], MSKf[:])
    B.dma(MSKf[:], cd["mprev"], "c0")
    B.cp(MPREV[:], MSKf[:])
    SKRAW = sb("SKRAW", [128, 3, SH], F32)
    B.dma(SKRAW[:, 0:2, :], bass.AP(sinks_full.tensor, 0, [[0, 128], [SH, DEPTH], [1, SH]]), "c0")
    B.dma(SKRAW[:, 2, :], bass.AP(own["sinks"].tensor, 0, [[0, 128], [1, SH]]), "c0")
    for hh in range(2):
        B.act(SINKE[hh * 64:(hh + 1) * 64, :, :],
              AV(SKRAW[:], hh, [[SH, 3], [2, 4]], p0=hh * 64, pn=64), AF.Exp)

    def load_prompt_consts():
        B.dma(DEC[:], cd["decp"], "c0")
        B.dma(GQ[:], cd["gqp"], "c0")
        B.dma(WEND[:], cd["wendp"], "c0")
        B.dma(GL[:], cd["glp"], "c0")

    wstate = dict(idx=0, plan=[])

    def wplan_for_pass(l, first):
        pl = []
        if first:
            pl.append((w_mem_kv[l][:, 0:512], 8, 512))
            pl.append((w_mem_kv[l][:, 512:1024], 8, 512))
        for j in range(7):
            pl.append((w_in[l][:, j * 512:(j + 1) * 512], 8, 512))
        pl.append((w_in[l][:, 3584:3840], 8, 256))
        for jj in range(7):
            pl.append((w_in[l][:, 3840 + jj * 512:3840 + (jj + 1) * 512], 8, 512))
        for cb in range(2):
            pl.append((w_br_ret[l][:, cb * 512:(cb + 1) * 512], 8, 512))
            pl.append(((w_br_swa[l][:, cb * 512:(cb + 1) * 512], w_br_mem[l][:, cb * 512:(cb + 1) * 512]), 8, 512))
        for cb in range(2):
            pl.append((w_out[l][:, cb * 512:(cb + 1) * 512], 8, 512))
        for jb in range(8):
            pl.append((w_up[l][:, jb * 512:(jb + 1) * 512], 8, 512))
        for cb in range(2):
            for rb in range(4):
                pl.append((w_down[l][rb * 1024:(rb + 1) * 1024, cb * 512:(cb + 1) * 512], 8, 512))
        return pl

    def w_issue(i):
        src, nkt, ncols = wstate["plan"][i]
        slot = WR[i % NW]
        if isinstance(src, tuple):
            for half, s in enumerate(src):
                B.dma(slot[:, half * 4:(half + 1) * 4, 0:ncols], s.rearrange("(kt p) c -> p kt c", p=128),
                      "w%d" % (i % NW), q="pool")
        else:
            B.dma(slot[:, 0:nkt, 0:ncols], src.rearrange("(kt p) c -> p kt c", p=128), "w%d" % (i % NW), q="pool")

    PF = 1

    def w_next():
        i = wstate["idx"]
        if i == 0:
            for k in range(min(PF, len(wstate["plan"]))):
                w_issue(k)
        if i + PF < len(wstate["plan"]):
            w_issue(i + PF)
        wstate["idx"] = i + 1
        return WR[i % NW]

    def transposes_to(dst_view_fn, src_fn, n, ptile, evac_eng="dve"):
        for k in range(n):
            B.tr(ptile[:, k * 128:(k + 1) * 128], src_fn(k))
        B.cp(dst_view_fn, ptile[:, 0:n * 128].rearrange("p (a b) -> p a b", a=n), eng=evac_eng)

    def rope_block(ps, nheads, tabidx, out_ap_f32=None, out_ap_bf=None, sample=False, perm=False):
        w = nheads * 64
        cosb = AV(ROPE[:], (0 * 4 + tabidx) * 32, [[0, nheads], [0, 2], [1, 32]])
        sinb = AV(ROPE[:], (1 * 4 + tabidx) * 32, [[0, nheads], [1, 32]])
        nsinb = AV(ROPE[:], (2 * 4 + tabidx) * 32, [[0, nheads], [1, 32]])
        psv = ps.rearrange("p (h t f) -> p h t f", h=nheads, t=2)
        t1v = T1[:, 0:w].rearrange("p (h t f) -> p h t f", h=nheads, t=2)
        t2v = T2[:, 0:w].rearrange("p (h t f) -> p h t f", h=nheads, t=2)
        B.tt(t1v, psv, cosb, ALU.mult)
        B.tt(t2v[:, :, 0, :], psv[:, :, 1, :], nsinb, ALU.mult)
        B.tt(t2v[:, :, 1, :], psv[:, :, 0, :], sinb, ALU.mult)
        if out_ap_f32 is not None:
            B.tt(out_ap_f32, T1[:, 0:w], T2[:, 0:w], ALU.add, eng="pool")
            if out_ap_bf is not None:
                B.cp(out_ap_bf, out_ap_f32, eng="act")
        elif perm:
            B.tt(out_ap_bf, T1[:, 0:w].rearrange("p (s t d) -> p s t d", s=2, t=4),
                 T2[:, 0:w].rearrange("p (s t d) -> p s t d", s=2, t=4), ALU.add, eng="pool")
        else:
            B.tt(out_ap_bf, T1[:, 0:w], T2[:, 0:w], ALU.add, eng="pool")

    def layer_norm_chunk(xc, gslot=0):
        for a in range(2):
            B.bnstats(STAT[:, a, :], xc[:, a * 512:(a + 1) * 512])
        B.bnaggr(MV[:, 0, :], STAT[:, 0:2, :].rearrange("p a b -> p (a b)"))
        B.ts(MV[:, 1, 0:1], MV[:, 0, 1:2], LN_EPS, ALU.add)
        B.act(MV[:, 1, 0:1], MV[:, 1, 0:1], AF.Ln)
        B.act(MV[:, 1, 0:1], MV[:, 1, 0:1], AF.Exp, scale=-0.5)
        B.ts(xc, xc, MV[:, 0, 0:1], ALU.subtract, MV[:, 1, 0:1], ALU.mult)
        B.tt(xc, xc, LNB[:, 0, :], ALU.mult)
        B.tt(xc, xc, LNB[:, 1, :], ALU.add)

    def group_norm_gate(ops, c, nch_tok):
        for h in range(8):
            B.bnstats(STAT[:, h, :], ops[h // 4][:, (h % 4) * 128:(h % 4 + 1) * 128])
        for h in range(8):
            B.bnaggr(MV[:, h, :], STAT[:, h, :])
        B.ts(MV[:, :, 1], MV[:, :, 1], GN_EPS, ALU.add)
        B.act(MV[:, :, 1], MV[:, :, 1], AF.Ln)
        B.act(MV[:, :, 1], MV[:, :, 1], AF.Exp, scale=-0.5)
        for half in range(2):
            pv = ops[half][:].rearrange("p (h e) -> p h e", h=4)
            B.tt(Y[:, half * 4:(half + 1) * 4, :], pv,
                 AV(MV[:], half * 8, [[2, 4], [0, 128]]), ALU.subtract)
        B.tt(Y[:], Y[:], AV(MV[:], 1, [[2, 8], [0, 128]]), ALU.mult)
        B.tt(Y[:], Y[:], GNG[:].rearrange("p (h e) -> p h e", h=8), ALU.mult)
        B.tt(GR[:], Y[:].rearrange("p h e -> p (h e)"), SRG[:, c, :], ALU.mult)
        transposes_to(GRT[:, :, c * 128:(c + 1) * 128], lambda k: GR[:, k * 128:(k + 1) * 128], 8, PT[0],
                      evac_eng="act")


    EM = sb("EM", [128, 2, 512], BF16)

    M = dict(PV_)

    def bcast_row(dram_ap_row, n):
        return bass.AP(dram_ap_row.tensor, dram_ap_row.offset, [[0, 128], [1, n]])

    def compute_memT():
        for mb in range(2):
            for half in range(2):
                B.dma(T1[:], mem_p[mb * 128:(mb + 1) * 128, half * 512:(half + 1) * 512], "x")
                B.cp(Xb[:, 0:512], T1[:], eng="act")
                transposes_to(MEMT[:, half * 4:(half + 1) * 4, mb * 128:(mb + 1) * 128],
                              lambda k: Xb[:, k * 128:(k + 1) * 128], 4, PT[half])

    def mem_kv(l):
        import os
        SUB = int(os.environ.get("SUB", "99"))
        compute_memT()
        Wk = w_next()
        for mb in range(2):
            ps = PB[mb]
            for kt in range(8):
                B.mm(ps[:], MEMT[:, kt, mb * 128:(mb + 1) * 128], Wk[:, kt, :], kt == 0, kt == 7)
            if SUB < 1:
                continue
            B.cp(T1[:], ps[:], eng="act")
            if SUB < 2:
                continue
            B.dma(mk_p[mb * 128:(mb + 1) * 128, :], T1[:], "o_mk", final=True)
            if SUB < 3:
                continue
            B.cp(RQb[:], ps[:])
            transposes_to(MKT[l][:, :, mb * 128:(mb + 1) * 128], lambda k: RQb[:, k * 128:(k + 1) * 128], 4, PT[mb])
        if SUB < 4:
            return
        Wv = w_next()
        for mb in range(2):
            ps = PB[2 + mb]
            for kt in range(8):
                B.mm(ps[:], MEMT[:, kt, mb * 128:(mb + 1) * 128], Wv[:, kt, :], kt == 0, kt == 7)
            B.cp(T2[:], ps[:], eng="act")
            B.dma(mv_p[mb * 128:(mb + 1) * 128, :], T2[:], "o_mv", final=True)
            B.cp(MVb[l][:, mb, :], ps[:])

    def load_layer_params(l):
        B.dma(GNG[:], bcast_row(gn_g[l], 1024), "prm")
        B.dma(LNB[:, 0, :], bcast_row(ln1_g[l], D), "prm")
        B.dma(LNB[:, 1, :], bcast_row(ln1_b[l], D), "prm")

    def xT_phase(nch):
        for c in range(nch):
            B.cp(Xb[:], X[:, c, :], eng="act")
            transposes_to(XT[:, :, c * 128:(c + 1) * 128], lambda k: Xb[:, k * 128:(k + 1) * 128], 8, PT[c % 2])

    def in_proj(l, nch, sample, last_chunk_out=None, mid_hook=None):
        NTk = nch * 128
        for j in range(8):
            W = w_next()
            ncols = 512 if j < 7 else 256
            for c in range(nch):
                ps = PB[(j * nch + c) % 3]
                for kt in range(8):
                    B.mm(ps[:, 0:ncols], XT[:, kt, c * 128:(c + 1) * 128], W[:, kt, 0:ncols], kt == 0, kt == 7)
                tab = 0 if sample else c
                if j == 0:
                    rope_block(ps[:], 8, tab, out_ap_bf=RQb[:])
                    transposes_to(M["RQT"][:, :, c * 128:(c + 1) * 128], lambda k: RQb[:, k * 128:(k + 1) * 128], 4, PT[1])
                elif j == 1:
                    rope_block(ps[:], 8, tab, out_ap_f32=RF[:], out_ap_bf=RQb[:])
                    for k in range(4):
                        B.tr(PT[1][:, k * 128:(k + 1) * 128], RQb[:, k * 128:(k + 1) * 128])
                    for hh in range(2):
                        B.cp(AV(RKm[:], (c * 8 + hh) * 128, [[256, 4], [1, 128]], p0=hh * 64, pn=64),
                             PT[1][hh * 64:(hh + 1) * 64, 0:512].rearrange("p (a b) -> p a b", a=4))
                    B.tt(RKw[:, c, :].rearrange("p (h d) -> p h d", h=8), RF[:].rearrange("p (h d) -> p h d", h=8),
                         AV(WEND[:], 0, [[1, 8], [0, 64]]), ALU.mult, eng="pool")
                elif j in (2, 3):
                    B.cp(Vb[:, c, (j - 2) * 512:(j - 1) * 512], ps[:], eng="act")
                elif j in (4, 5):
                    B.act(SRG[:, c, (j - 4) * 512:(j - 3) * 512], ps[:], AF.Silu)
                elif j == 6:
                    rope_block(ps[:], 8, tab, out_ap_bf=AV(RQb[:], 0, [[64, 2], [128, 4], [1, 64]]), perm=True)
                    transposes_to(SQT[:, c, :].rearrange("p (a b) -> p a b", a=4),
                                  lambda k: RQb[:, k * 128:(k + 1) * 128], 4, PT[1])
                else:
                    rope_block(ps[:, 0:128], 2, tab, out_ap_f32=SKf[:, 0:128], out_ap_bf=RQb[:, 0:128])
                    B.tr(PT[1][:, 0:128], RQb[:, 0:128])
                    for g in range(2):
                        B.cp(SKm[g * 64:(g + 1) * 64, 1 + c, g, :], PT[1][g * 64:(g + 1) * 64, 0:128])
                    B.cp(SKf[:, 128:256], ps[:, 128:256], eng="act")
                    for g in range(2):
                        for hh in range(2):
                            B.cp(SVp[:, 1 + c, g, hh, hh * 64:(hh + 1) * 64], SKf[:, 128 + g * 64:128 + (g + 1) * 64])
                    if last_chunk_out is not None and c == nch - 1:
                        last_chunk_out()
        if mid_hook is not None:
            mid_hook()
        for jj in range(7):
            W = w_next()
            for t in range(4):
                ti = jj * 4 + t
                ps = PB[ti % 3]
                for kt in range(8):
                    B.mm(ps[:, 0:NTk], W[:, kt, t * 128:(t + 1) * 128], XT[:, kt, 0:NTk], kt == 0, kt == 7)
                if ti < 4:
                    B.cp(M["MQT"][:, ti, 0:NTk], ps[:, 0:NTk])
                else:
                    B.act(M["GT"][:, ti - 4, 0:NTk], ps[:, 0:NTk], AF.Sigmoid)

    def swa_pv_norm(l, c, has_prev, slot_prev, slot_own):
        for pr in range(4):
            g = pr // 2
            for which in range(2):
                dst = (PB[5], PB[0])[which][:, pr * 128:(pr + 1) * 128]
                first = True
                for hh in range(2):
                    t = 2 * (pr % 2) + hh
                    for blk in range(2):
                        if blk == 0 and not has_prev:
                            continue
                        slot = slot_prev if blk == 0 else slot_own
                        lhs = SVp[:, slot, g, hh, :] if which == 0 else ONp[:, hh, :]
                        B.mm(dst, lhs, E[:, g * 2 + blk, t * 128:(t + 1) * 128], first, hh == 1 and blk == 1)
                        first = False
        for pr in range(4):
            B.ts(DEN[:, pr * 128:(pr + 1) * 128], PB[0][:, pr * 128:(pr + 1) * 128], SINKE[:, l, pr:pr + 1], ALU.add)
        B.recip(DEN[:], DEN[:])
        B.tt(SWOT[:, :, c * 128:(c + 1) * 128], PB[5][:].rearrange("p (a i) -> p a i", a=4),
             DEN[:].rearrange("p (a i) -> p a i", a=4), ALU.mult)

    def branches_out_mlp(l, nch, mid_hook=None):
        NTk = nch * 128
        for cb in range(2):
            Wret = w_next()
            Wsm = w_next()
            for o4 in range(4):
                ot = cb * 4 + o4
                bk = (ot % 2) * 3
                pr_, psw, pme = PB[bk], PB[bk + 1], PB[bk + 2]
                for kt in range(8):
                    B.mm(pr_[:, 0:NTk], Wret[:, kt, o4 * 128:(o4 + 1) * 128], GRT[:, kt, 0:NTk], kt == 0, kt == 7)
                for kt in range(4):
                    B.mm(psw[:, 0:NTk], Wsm[:, kt, o4 * 128:(o4 + 1) * 128], SWOT[:, kt, 0:NTk], kt == 0, kt == 3)
                for kt in range(4):
                    B.mm(pme[:, 0:NTk], Wsm[:, 4 + kt, o4 * 128:(o4 + 1) * 128], MOT[:, kt, 0:NTk], kt == 0, kt == 3)
                B.tt(T1[:, 0:NTk], pr_[:, 0:NTk], M["GT"][:, ot, 0:NTk], ALU.mult)
                B.tt(T2[:, 0:NTk], psw[:, 0:NTk], M["GT"][:, 8 + ot, 0:NTk], ALU.mult)
                B.tt(RF[:, 0:NTk], pme[:, 0:NTk], M["GT"][:, 16 + ot, 0:NTk], ALU.mult)
                B.tt(T1[:, 0:NTk], T1[:, 0:NTk], T2[:, 0:NTk], ALU.add)
                B.tt(MT[:, ot, 0:NTk], T1[:, 0:NTk], RF[:, 0:NTk], ALU.add)
        for cb in range(2):
            Wo = w_next()
            for c in range(nch):
                ps = PB[(cb * nch + c) % 6]
                for kt in range(8):
                    B.mm(ps[:], MT[:, kt, c * 128:(c + 1) * 128], Wo[:, kt, :], kt == 0, kt == 7)
                xs = X[:, c, cb * 512:(cb + 1) * 512]
                B.stt(xs, xs, ALPHA, ps[:], ALU.mult, ALU.add)
        for c in range(nch):
            layer_norm_chunk(X[:, c, :], 0)
        B.dma(LNB[:, 0, :], bcast_row(ln2_g[l], D), "prm")
        B.dma(LNB[:, 1, :], bcast_row(ln2_b[l], D), "prm")
        xT_phase(nch)
        for jb in range(8):
            W = w_next()
            for t in range(4):
                ft = jb * 4 + t
                ps = PB[ft % 6]
                R = (T1, T2)[ft % 2]
                for kt in range(8):
                    B.mm(ps[:, 0:NTk], W[:, kt, t * 128:(t + 1) * 128], XT[:, kt, 0:NTk], kt == 0, kt == 7)
                B.act(R[:, 0:NTk], ps[:, 0:NTk], AF.Relu)
                B.tt(M["HT"][:, ft, 0:NTk], R[:, 0:NTk], R[:, 0:NTk], ALU.mult)
        if mid_hook is not None:
            mid_hook()
        for cb in range(2):
            for rb in range(4):
                W = w_next()
                for c in range(nch):
                    for k8 in range(8):
                        B.mm(PB[c][:], M["HT"][:, rb * 8 + k8, c * 128:(c + 1) * 128], W[:, k8, :],
                             rb == 0 and k8 == 0, rb == 3 and k8 == 7)
            for c in range(nch):
                xs = X[:, c, cb * 512:(cb + 1) * 512]
                B.stt(xs, xs, ALPHA, PB[c][:], ALU.mult, ALU.add)
        for c in range(nch):
            layer_norm_chunk(X[:, c, :], 0)

    def prefetch_input_bf16(t):
        for c in range(4):
            for half in range(2):
                dst = XbN[:, c, half * 512:(half + 1) * 512]
                if t < nsc:
                    B.dma(RF[:], x_p[t * 512 + c * 128:t * 512 + (c + 1) * 128, half * 512:(half + 1) * 512], None)
                if t >= 2:
                    B.dma(XbS, cc_dst[t % 2][c * 128:(c + 1) * 128, half * 512:(half + 1) * 512], None)
                if t < 2:
                    B.ts(dst, RF[:], SEL[:, 0:1], ALU.mult)
                elif t < nsc:
                    B.ts(RF[:], RF[:], SEL[:, 0:1], ALU.mult)
                    B.stt(dst, XbS, SEL[:, 1:2], RF[:], ALU.mult, ALU.add)
                else:
                    B.ts(dst, XbS, SEL[:, 1:2], ALU.mult)

    def load_x_fp32(sc):
        if sc < nsc:
            B.dma(X[:], x_p[sc * 512:(sc + 1) * 512, :].rearrange("(c p) d -> p c d", p=128), "x")
            Xf = X[:].rearrange("p c d -> p (c d)")
            B.ts(Xf, Xf, SEL[:, 0:1], ALU.mult)
        else:
            B.memset(X[:], 0.0)
        if sc >= 2:
            gsrc = cc_dst[sc % 2]
            for c in range(4):
                for half in range(2):
                    stg = (T1, T2)[(c * 2 + half) % 2]
                    B.dma(stg[:], gsrc[c * 128:(c + 1) * 128, half * 512:(half + 1) * 512], None)
                    xs = X[:, c, half * 512:(half + 1) * 512]
                    B.stt(xs, stg[:], SEL[:, 1:2], xs, ALU.mult, ALU.add)

    def prompt_pass(sc, l):
        chunk0 = sc * 4
        B.dma(ROPE[:, 0, :, :], r_cos[:, chunk0:chunk0 + 4, :], "rope")
        B.dma(ROPE[:, 1, :, :], r_sin[:, chunk0:chunk0 + 4, :], "rope")
        B.dma(ROPE[:, 2, :, :], r_nsin[:, chunk0:chunk0 + 4, :], "rope")
        load_layer_params(l)
        if sc == 0:
            mem_kv(l)
        if sc == 0:
            prefetch_input_bf16(0)
        B.cp(SKm[:, 0], SKTprev[l][:])
        B.cp(SVp[:, 0], SVprev[l][:])
        for c in range(4):
            transposes_to(XT[:, :, c * 128:(c + 1) * 128], lambda k: XbN[:, c, k * 128:(k + 1) * 128], 8, PT[c % 2])
        ver = {nsc - 1: 0, nsc + 1: 1}.get(sc, None)
        last = ver is not None

        def last_out():
            B.dma(sk_p[ver], SKf[:, 0:128], "o_sk", final=True)
            B.dma(sv_p[ver], SKf[:, 128:256], "o_sk", final=True)
        in_proj(l, 4, False, last_out if last else None, mid_hook=lambda: load_x_fp32(sc))
        S, Sb_ = Sst[l], Sbf[l]
        for c in range(4):
            cs = slice(c * 128, (c + 1) * 128)
            for h in range(8):
                pr, hh = h // 2, h % 2
                ps = PB[3 + h // 4]
                B.mm(ps[:, (h % 4) * 128:(h % 4 + 1) * 128], RKm[:, c, h, :], M["RQT"][:, pr, cs], True, True)
            for half in range(2):
                B.tt(ST[:, half * 4:(half + 1) * 4, :], PB[3 + half][:].rearrange("p (h i) -> p h i", h=4),
                     DEC[:, half * 4:(half + 1) * 4, :], ALU.mult)
            for hh in range(2):
                B.tt(AV(RQs2[:], hh * 128, [[256, 4], [1, 128]], p0=hh * 64, pn=64),
                     M["RQT"][hh * 64:(hh + 1) * 64, :, cs], GQ[hh * 64:(hh + 1) * 64, :, :], ALU.mult)
            obanks = (PB[5], PB[0])
            for h in range(8):
                pr, hh = h // 2, h % 2
                po = obanks[h // 4][:, (h % 4) * 128:(h % 4 + 1) * 128]
                B.mm(po, ST[:, h, :], Vb[:, c, h * 128:(h + 1) * 128], True, False)
                B.mm(po, RQs2[:, h, :], Sb_[:, pr, :], False, True)
            group_norm_gate(obanks, c, 4)
            for pr in range(4):
                ps = PB[1 + pr % 2]
                B.mm(ps[:, 0:256], RKw[:, c, pr * 128:(pr + 1) * 128], Vb[:, c, pr * 256:(pr + 1) * 256], True, True)
                for hh in range(2):
                    sv = S[hh * 64:(hh + 1) * 64, pr, :]
                    B.stt(sv, sv, GL[hh * 64:(hh + 1) * 64, pr:pr + 1],
                          ps[hh * 64:(hh + 1) * 64, hh * 128:(hh + 1) * 128], ALU.mult, ALU.add)
            B.cp(Sb_[:], S[:], eng="act")
            if last and c == 3:
                B.dma(bass.AP(rs_p.tensor, ver * RH * RDK * RDV, [[128, 128], [2 * RDK * RDV, 4], [1, 128]]), S[:],
                      "o_rs", final=True)
            if stage < 6:
                continue
            has_prev = not (sc == 0 and c == 0)
            banks = {(0, 0): PB[1], (0, 1): PB[2], (1, 0): PB[3], (1, 1): PB[4]}
            for g in range(2):
                qv = SQT[:, c, :]
                for blk in range(2):
                    if blk == 0 and not has_prev:
                        continue
                    ps = banks[(g, blk)]
                    B.mm(ps[:], SKm[:, c + blk, g, :], qv, True, False)
                    if blk == 0 and sc == 2 and c == 0:
                        B.mm(ps[:], B.identb[:], FLAGN[:], False, False)
                    B.mm(ps[:], B.identb[:], (MPREV if blk == 0 else MOWN)[:], False, True)
                    B.act(E[:, g * 2 + blk, :], ps[:], AF.Exp, scale=SHD ** -0.5)
            swa_pv_norm(l, c, has_prev, c, c + 1)
            if stage < 7:
                continue
            for blk in range(2):
                ps = PB[1 + blk]
                for h in range(4):
                    B.mm(ps[:, h * 128:(h + 1) * 128], MKT[l][:, h, blk * 128:(blk + 1) * 128], M["MQT"][:, h, cs], True, True)
                B.act(EM[:, blk, :], ps[:], AF.Exp, scale=MHD ** -0.5)
            for h in range(4):
                for blk in range(2):
                    B.mm(PB[3][:, h * 128:(h + 1) * 128], MVb[l][:, blk, h * 128:(h + 1) * 128],
                         EM[:, blk, h * 128:(h + 1) * 128], blk == 0, blk == 1)
                for blk in range(2):
                    B.mm(PB[4][:, h * 128:(h + 1) * 128], onesb[:], EM[:, blk, h * 128:(h + 1) * 128], blk == 0, blk == 1)
            B.recip(DEN[:], PB[4][:])
            B.tt(MOT[:, :, cs], PB[3][:].rearrange("p (a i) -> p a i", a=4),
                 DEN[:].rearrange("p (a i) -> p a i", a=4), ALU.mult)
        B.cp(SKTprev[l][:], SKm[:, 4])
        B.cp(SVprev[l][:], SVp[:, 4])
        if stage < 8:
            return
        branches_out_mlp(l, 4, mid_hook=(lambda: prefetch_input_bf16(sc + 1)) if sc + 1 < NSLOT else None)
        B.dma(y_p[sc * 512:(sc + 1) * 512, :].rearrange("(c p) d -> p c d", p=128), X[:], "o_y", final=True)
        if sc < nsc:
            k = sc % 2
            B.dma(cc_src[k].rearrange("(c p) d -> p c d", p=128), X[:], "!ccs%d" % k)
            src_ap, dst_ap = cc_src[k], cc_dst[k]
            if not cfg.get("nocc", False):
              B.T.add("pool", lambda: nc.gpsimd.collective_compute(
                "AllGather", ALU.bypass, replica_groups=[[i, i + 4] for i in range(4)], ins=[src_ap], outs=[dst_ap]),
                w=[dst_ap], r=[src_ap], dma_key="!cc%d" % sc, inc=1)

    def load_sample_consts():
        B.dma(DEC[:], cd["decs"], None)
        B.dma(GQ[:], cd["gqs"], None)
        B.dma(WEND[:], cd["wends"], None)
        B.dma(GL[:], cd["gls"], None)
        B.dma(ROPE[:, 0, 0:1, :], cd["coss"], None)
        B.dma(ROPE[:, 1, 0:1, :], cd["sins"], None)
        B.dma(ROPE[:, 2, 0:1, :], cd["nsins"], None)
        B.dma(BMR[:], cd["bmrow"], None)
        B.dma(identF[:], cd["ident"], None)
        B.dma(T1[:], cd["mnews"], None)
        B.cp(MOWN[:], T1[:])
        B.dma(T1[:], cd["mcache"], None)
        B.cp(MPREV[:], T1[:])

    def reload_prompt_masks():
        B.dma(T1[:], cd["mown"], None)
        B.cp(MOWN[:], T1[:])
        B.dma(T1[:], cd["mprev"], None)
        B.cp(MPREV[:], T1[:])

    def sample_pass(l):
        NB = SB_PER_CORE
        load_layer_params(l)
        if l == 0:
            B.dma(X[:, 0, :], x_s, None)
        B.dma(sk_s[l][:, 0:120, :], c_sk[l][:, 8:128, :], "!cck", final=True)
        B.dma(sv_s[l][:, 0:120, :], c_sv[l][:, 8:128, :], "!ccv", final=True)
        xT_phase(1)

        def new_rows_out():
            for b in range(NB):
                B.dma(sk_s[l, b, 120:128, :], SKf[8 * b:8 * b + 8, 0:128], None, final=True)
                B.dma(sv_s[l, b, 120:128, :], SKf[8 * b:8 * b + 8, 128:256], None, final=True)
        in_proj(l, 1, True, new_rows_out)
        RQT_, MQT_ = M["RQT"], M["MQT"]
        if sstage < 2:
            return
        cs = slice(0, 128)
        for h in range(8):
            pr = h // 2
            ps = PB[3 + h // 4]
            B.mm(ps[:, (h % 4) * 128:(h % 4 + 1) * 128], RKm[:, 0, h, :], RQT_[:, pr, cs], True, True)
        for half in range(2):
            B.tt(ST[:, half * 4:(half + 1) * 4, :], PB[3 + half][:].rearrange("p (h i) -> p h i", h=4),
                 DEC[:, half * 4:(half + 1) * 4, :], ALU.mult)
        for hh in range(2):
            B.tt(AV(RQs2[:], hh * 128, [[256, 4], [1, 128]], p0=hh * 64, pn=64),
                 RQT_[hh * 64:(hh + 1) * 64, :, cs], GQ[hh * 64:(hh + 1) * 64, :, :], ALU.mult)
        ot = (PB[5], PB[0])
        B.memset(ot[0][:], 0.0)
        B.memset(ot[1][:], 0.0)
        for h in range(8):
            B.mm(ot[h // 4][:, (h % 4) * 128:(h % 4 + 1) * 128], Vb[:, 0, h * 128:(h + 1) * 128], ST[:, h, :],
                 False, False, skip=True)
        for grp in range(4):
            b0 = grp * 4
            st_src = bass.AP(st_ret.tensor, (l * NB + b0) * RH * RDK * RDV,
                             [[128, 128], [RH * RDK * RDV, 4], [2 * RDK * RDV, 4], [1, 128]])
            B.dma(S0g, st_src, None)
            B.dma(S0bg, st_src, None, q="pool")
            if grp % 2 == 0:
                rnd = grp // 2
                B.tt(RKwX, AV(RKw[:], 0, [[0, 8], [1, 512]]),
                     AV(BMR[:], rnd * 8, [[1, 8], [0, 512]]), ALU.mult)
            for bl in range(4):
                b = b0 + bl
                for h in range(8):
                    pr = h // 2
                    B.mm(ot[h // 4][:, (h % 4) * 128 + 8 * b:(h % 4) * 128 + 8 * b + 8], S0bg[:, bl, pr, :],
                         RQs2[:, h, 8 * b:8 * b + 8], False, False, skip=True)
                for pr in range(4):
                    ps = PB[1 + pr // 2]
                    B.mm(ps[:, (pr % 2) * 256:(pr % 2 + 1) * 256], RKwX[:, b % 8, pr * 128:(pr + 1) * 128],
                         Vb[:, 0, pr * 256:(pr + 1) * 256], True, True)
                for pr in range(4):
                    ps = PB[1 + pr // 2]
                    for hh in range(2):
                        sv = S0g[hh * 64:(hh + 1) * 64, bl, pr, :]
                        c0 = (pr % 2) * 256 + hh * 128
                        B.stt(sv, sv, GL[hh * 64:(hh + 1) * 64, pr:pr + 1],
                              ps[hh * 64:(hh + 1) * 64, c0:c0 + 128], ALU.mult, ALU.add)
            B.dma(bass.AP(rs_s.tensor, (l * NB + b0) * RH * RDK * RDV,
                          [[128, 128], [RH * RDK * RDV, 4], [2 * RDK * RDV, 4], [1, 128]]), S0g, None, final=True)
        for half in range(2):
            B.cp(Y[:, half * 4:(half + 1) * 4, :], ot[half][:].rearrange("p (h i) -> p h i", h=4), eng="act")
        for h in range(8):
            B.trf(PB[3 + h // 4][:, (h % 4) * 128:(h % 4 + 1) * 128], Y[:, h, :], identF[:])
        group_norm_gate((PB[3], PB[4]), 0, 1)
        if sstage < 3:
            return
        for g in range(2):
            ps = PB[1 + g]
            B.mm(ps[:], SKm[:, 1, g, :], SQT[:, 0, :], True, False)
            B.mm(ps[:], B.identb[:], MOWN[:], False, True)
            B.act(E[:, g * 2 + 1, :], ps[:], AF.Exp, scale=SHD ** -0.5)
        B.memset(SQTs, 0.0)
        for g in range(2):
            B.cp(AV(SQTs, g * 512, [[32, 16], [8, 4], [1, 8]], p0=g * 64, pn=64),
                 AV(SQT[:], 0, [[8, 16], [128, 4], [1, 8]], p0=g * 64, pn=64))
        B.dma(Kraw, c_sk[l].rearrange("b k c -> k b c"), None, q="pool")
        for half in range(2):
            for k in range(8):
                B.tr(PT[half][:, k * 128:(k + 1) * 128], Kraw[:, half * 8 + k, :])
            B.cp(KcT[:, half * 8:(half + 1) * 8, :], PT[half][:].rearrange("p (a b) -> p a b", a=8))
        for half in range(2):
            ps = PB[3 + half]
            B.memset(ps[:], 0.0)
            for b8 in range(8):
                b = half * 8 + b8
                for g in range(2):
                    B.mm(ps[:, (b8 * 2 + g) * 32:(b8 * 2 + g + 1) * 32], KcT[:, b, :],
                         SQTs[:, g, b * 32:(b + 1) * 32], False, False, skip=True)
            B.mm(ps[:], B.identb[:], MPREV[:], False, False, skip=True)
            B.act(Ec[:, half * 8:(half + 1) * 8, :, :].rearrange("p b g c -> p (b g c)"), ps[:], AF.Exp,
                  scale=SHD ** -0.5)
        B.dma(Kraw, c_sv[l].rearrange("b k c -> k b c"), None, q="pool")
        num, den = PB[5], PB[0]
        B.memset(num[:], 0.0)
        B.memset(den[:], 0.0)
        for pr in range(4):
            g = pr // 2
            for hh in range(2):
                t = 2 * (pr % 2) + hh
                B.mm(num[:, pr * 128:(pr + 1) * 128], SVp[:, 1, g, hh, :], E[:, g * 2 + 1, t * 128:(t + 1) * 128],
                     False, False, skip=True)
                B.mm(den[:, pr * 128:(pr + 1) * 128], ONp[:, hh, :], E[:, g * 2 + 1, t * 128:(t + 1) * 128],
                     False, False, skip=True)
        for grp in range(4):
            b0 = grp * 4
            B.memset(Vcp, 0.0)
            for hh in range(2):
                B.cp(AV(Vcp, hh * 128 + hh * 64, [[512, 4], [256, 2], [1, 64]]),
                     AV(Kraw, b0 * 128, [[128, 4], [64, 2], [1, 64]]))
            for bl in range(4):
                b = b0 + bl
                for pr in range(4):
                    g = pr // 2
                    for hh in range(2):
                        t = 2 * (pr % 2) + hh
                        rhs = Ec[:, b, g, t * 8:(t + 1) * 8]
                        B.mm(num[:, pr * 128 + 8 * b:pr * 128 + 8 * b + 8], Vcp[:, bl, g, hh, :], rhs,
                             False, False, skip=True)
                        B.mm(den[:, pr * 128 + 8 * b:pr * 128 + 8 * b + 8], ONp[:, hh, :], rhs,
                             False, False, skip=True)
        for pr in range(4):
            B.ts(DEN[:, pr * 128:(pr + 1) * 128], den[:, pr * 128:(pr + 1) * 128], SINKE[:, l, pr:pr + 1], ALU.add)
        B.recip(DEN[:], DEN[:])
        B.tt(SWOT[:, :, cs], num[:].rearrange("p (a i) -> p a i", a=4),
             DEN[:].rearrange("p (a i) -> p a i", a=4), ALU.mult)
        if sstage < 4:
            return
        Kmraw = Vb[:, 1:3, :].rearrange("p a (k c) -> p (a k) c", k=2)
        KmT = SRG[:, 1:3, :].rearrange("p a (h m) -> p a h m", h=4)
        Vm = RKm[:, 1:3, :, :].rearrange("p a h c -> p (a h c)").rearrange("p (k c) -> p k c", k=4)
        Em = RKw[:, 1, 0:128].rearrange("p (b k h i) -> p b k h i", b=2, k=2, h=4)
        mnum, mden = PB[3], PB[4]
        B.memset(mnum[:], 0.0)
        B.memset(mden[:], 0.0)
        for grp in range(NB // 2):
            b0 = grp * 2
            B.dma(Kmraw, c_mk[l][b0:b0 + 2].rearrange("b (k m) c -> m (b k) c", k=2), None, q="pool")
            B.dma(Vm, c_mv[l][b0:b0 + 2].rearrange("b (k m) c -> m (b k) c", k=2), None, q="pool")
            sc_ps = PB[1 + grp % 2]
            B.memset(sc_ps[:, 0:128], 0.0)
            for bl in range(2):
                b = b0 + bl
                for blk in range(2):
                    for h in range(4):
                        B.tr(PT[bl][:, (h * 2 + blk) * 128:(h * 2 + blk + 1) * 128],
                             Kmraw[:, bl * 2 + blk, h * 128:(h + 1) * 128])
                B.cp(KmT[:, bl, :, :].rearrange("p h m -> p (h m)"), PT[bl][:])
                for blk in range(2):
                    for h in range(4):
                        c0 = ((bl * 2 + blk) * 4 + h) * 8
                        B.mm(sc_ps[:, c0:c0 + 8], KmT[:, bl, h, blk * 128:(blk + 1) * 128],
                             MQT_[:, h, 8 * b:8 * b + 8], False, False, skip=True)
            B.act(Em.rearrange("p b k h i -> p (b k h i)"), sc_ps[:, 0:128], AF.Exp, scale=MHD ** -0.5)
            for bl in range(2):
                b = b0 + bl
                for blk in range(2):
                    for h in range(4):
                        rhs = Em[:, bl, blk, h, :]
                        B.mm(mnum[:, h * 128 + 8 * b:h * 128 + 8 * b + 8], Vm[:, bl * 2 + blk, h * 128:(h + 1) * 128],
                             rhs, False, False, skip=True)
                        B.mm(mden[:, h * 128 + 8 * b:h * 128 + 8 * b + 8], onesb[:], rhs, False, False, skip=True)
        B.recip(DEN[:], mden[:])
        B.tt(MOT[:, :, cs], mnum[:].rearrange("p (a i) -> p a i", a=4),
             DEN[:].rearrange("p (a i) -> p a i", a=4), ALU.mult)
        if sstage < 5:
            return
        branches_out_mlp(l, 1)
        if l == nlay - 1:
            B.dma(y_s, X[:, 0, :], None, final=True)

    passes = []
    if do_sample:
        for l in range(nlay):
            passes.append(("s", 0, l))
    for sc in range(NSLOT):
        passes.append(("p", sc, 2))
    for kind, sc, l in passes:
        wstate["plan"] += wplan_for_pass(l, kind == "p" and sc == 0)
    if do_sample:
        load_sample_consts()
    else:
        load_prompt_consts()
    for pi, (kind, sc, l) in enumerate(passes):
        T.phase = pi
        if kind == "s":
            M.update(SV_)
            sample_pass(l)
            if l == nlay - 1:
                M.update(PV_)
                B.memset(RKm[:], 0.0)
                load_prompt_consts()
                reload_prompt_masks()
        elif not cfg.get("noprompt", False):
            prompt_pass(sc, l)
    if stage >= 99 and sstage >= 99 and not cfg.get("noprompt", False):
        assert wstate["idx"] == len(wstate["plan"]), (wstate["idx"], len(wstate["plan"]))
    if dbgname is not None:
        dbg = B.dout("dbg", [128, 4096])
        DBG = X[:].rearrange("p a b -> p (a b)")
        srcs = dict(XT=XT[:], RQT=RQT, RKm=RKm[:, 0], SQT=SQT[:], SKm=SKm[:], Vb=Vb[:, :, :], SRG=SRG[:], MQT=MQT, GT=GT[:, 0:8, :],
                    GRT=GRT[:], SWOT=SWOT[:], MOT=MOT[:], MT=MT, X=X[:], RKw=RKw[:], MKT=MKT[0][:], MVb=MVb[0][:],
                    MEMT=MEMT, HT=HT[:, 0:8, :], S=Sst[0][:], E=E[:], EM=EM[:])[dbgname]
        n = 1
        for s_ in srcs.shape[1:]:
            n *= s_
        dims = "abcd"[:len(srcs.shape) - 1]
        flat = srcs.rearrange("p %s -> p (%s)" % (" ".join(dims), " ".join(dims))) if len(dims) > 1 else srcs
        if dbgname != "X":
            B.cp(DBG[:, 0:n], flat)
        B.dma(dbg[:, 0:n], DBG[:, 0:n], "o_dbg", final=True)
    T.emit(B.final_ops)
    return B


_CACHE = {}


def _core_inputs(inp, core, nsc, hc):
    f = lambda a: np.ascontiguousarray(np.asarray(a, dtype=np.float32))
    b = core % BATCH
    L = core // BATCH
    sb0 = core * SB_PER_CORE
    sl = slice(sb0, sb0 + SB_PER_CORE)
    nslot = nsc + 2
    m = {
        "x_p": f(inp["x_prompt"][b, :nsc * 512]),
        "mem_p": f(inp["mem_prompt"][b]),
        "x_s": f(np.asarray(inp["x_sample"])[sl].reshape(SB_PER_CORE * DEC_SEQ, D)),
        "st_ret": f(np.asarray(inp["state_ret"])[:, sl]),
        "c_sk": f(np.asarray(inp["cache_swa_k"])[:, sl].reshape(DEPTH, SB_PER_CORE, 128, 128)),
        "c_sv": f(np.asarray(inp["cache_swa_v"])[:, sl].reshape(DEPTH, SB_PER_CORE, 128, 128)),
        "c_mk": f(np.asarray(inp["cache_mem_k"])[:, sl].reshape(DEPTH, SB_PER_CORE, NMEM, 512)),
        "c_mv": f(np.asarray(inp["cache_mem_v"])[:, sl].reshape(DEPTH, SB_PER_CORE, NMEM, 512)),
        "sinks": f(inp["attn_sinks"]),
        "gn_g": f(np.asarray(inp["ret_gn_g"]).reshape(DEPTH, 1024)),
    }
    chunk_of = []
    for s in range(nslot):
        for c in range(4):
            ch = (4 * s + c) if L == 0 else (4 * (s - 2) + c)
            if ch < 0 or ch >= nsc * 4:
                ch = 0
            chunk_of.append(ch)
    for nm, key in (("r_cos", "cosp"), ("r_sin", "sinp"), ("r_nsin", "nsinp")):
        m[nm] = np.ascontiguousarray(hc[key][:, chunk_of, :])
    sel = np.zeros((128, 2), np.float32)
    sel[:, L] = 1.0
    m["sel"] = sel
    m["flagneg"] = np.full((128, 512), NEG if L == 1 else 0.0, np.float32)
    return m


_OWN = (("wo_in", "w_in"), ("wo_br_ret", "w_br_ret"), ("wo_br_swa", "w_br_swa"), ("wo_br_mem", "w_br_mem"),
        ("wo_out", "w_out"), ("wo_mem_kv", "w_mem_kv"), ("wo_up", "w_up"), ("wo_down", "w_down"),
        ("o_ln1_g", "ln1_g"), ("o_ln1_b", "ln1_b"), ("o_ln2_g", "ln2_g"), ("o_ln2_b", "ln2_b"))
_FULL = ("w_in", "w_br_ret", "w_br_swa", "w_br_mem", "w_out", "w_mem_kv", "ln1_g", "ln1_b", "w_up", "w_down",
         "ln2_g", "ln2_b")


def run_cores(inp, cfg, cores):
    key = tuple(sorted(cfg.items()))
    if key not in _CACHE:
        _CACHE[key] = build(cfg)
    B = _CACHE[key]
    hc = host_consts()
    f = lambda a: np.ascontiguousarray(np.asarray(a, dtype=np.float32))
    full = {k: f(inp[k]) for k in _FULL}
    ownl = []
    for L in range(DEPTH):
        d = {dst: np.ascontiguousarray(full[srck][L]) for dst, srck in _OWN}
        d["o_sinks"] = f(inp["attn_sinks"])[L:L + 1]
        d["o_gn_g"] = f(np.asarray(inp["ret_gn_g"]).reshape(DEPTH, 1024))[L]
        ownl.append(d)
    in_maps = []
    for core in cores:
        m = _core_inputs(inp, core, cfg["nsc"], hc)
        m.update(full)
        m.update(ownl[core // BATCH])
        for k, v in hc.items():
            m["c_" + k] = v
        in_maps.append(m)
    res = run_bass_kernel_spmd(B.nc, in_maps, core_ids=list(range(len(cores))))
    return res.results


def assemble(r, nsc):
    S = nsc * 512
    y_p = np.stack([r[BATCH + b]["y_p"][2 * 512:2 * 512 + S] for b in range(BATCH)]).astype(np.float32)
    y_s = np.concatenate([r[c]["y_s"].reshape(SB_PER_CORE, DEC_SEQ, D) for c in range(NCORES)]).astype(np.float32)

    def per_layer(name, shape, versioned):
        out = []
        for L in range(DEPTH):
            row = []
            for b in range(BATCH):
                a = r[L * BATCH + b][name]
                a = a[L] if versioned else a
                row.append(np.asarray(a).reshape(shape))
            out.append(np.stack(row))
        return np.stack(out).astype(np.float32)
    rs_p = per_layer("rs_p", (RH, RDK, RDV), True)
    sk_p = per_layer("sk_p", (128, SKV, SHD), True)
    sv_p = per_layer("sv_p", (128, SKV, SHD), True)
    mk_p = per_layer("mk_p", (NMEM, MH, MHD), False)
    mv_p = per_layer("mv_p", (NMEM, MH, MHD), False)
    rs_s = np.concatenate([r[c]["rs_s"] for c in range(NCORES)], axis=1).astype(np.float32)
    sk_s = np.concatenate([r[c]["sk_s"].reshape(DEPTH, SB_PER_CORE, 128, SKV, SHD) for c in range(NCORES)],
                          axis=1).astype(np.float32)
    sv_s = np.concatenate([r[c]["sv_s"].reshape(DEPTH, SB_PER_CORE, 128, SKV, SHD) for c in range(NCORES)],
                          axis=1).astype(np.float32)
    return (y_p, y_s, rs_p, sk_p, sv_p, mk_p, mv_p, rs_s, sk_s, sv_s)


def kernel(**inp):
    cfg = dict(nsc=SEQ // 512, nlay=DEPTH, sample=True)
    r = run_cores(inp, cfg, list(range(NCORES)))
    return assemble(r, cfg["nsc"])
```

```python
import numpy as np
import concourse.bass as bass
import concourse.mybir as mybir
from concourse.bass_utils import run_bass_kernel_spmd

F32 = mybir.dt.float32
BF16 = mybir.dt.bfloat16
AF = mybir.ActivationFunctionType
ALU = mybir.AluOpType

D = 1024
DEPTH = 2
SEQ = 4096
BATCH = 4
DEC_BATCH = 128
DEC_SEQ = 8
PAST_LEN = 16384
RH, RDK, RDV = 8, 64, 128
SH, SKV, SHD = 8, 2, 64
MH, MHD, NMEM = 4, 128, 256
DFF = 4096
IN_W = 7424
ALPHA = (2 * DEPTH) ** 0.25
LN_EPS = 1e-5
GN_EPS = 1e-5
NEG = -2000.0
NCORES = 8
SB_PER_CORE = DEC_BATCH // NCORES


def _dsize(dt):
    return mybir.dt.size(dt)


class Tracker:
    def __init__(self, nc):
        self.nc = nc
        self.ops = []
        self.psum_last = {}
        self.hist = {}
        self.phase = 0
        self.eng_objs = {"pe": nc.tensor, "dve": nc.vector, "act": nc.scalar,
                         "pool": nc.gpsimd, "sp": nc.sync}

    @staticmethod
    def box(ap):
        t = ap.tensor
        name = t.name
        dims = ap.ap
        esz = _dsize(ap.dtype)
        space = str(ap.space)
        if "DRAM" in space.upper() or "HBM" in space.upper() or "dram" in space:
            lo = ap.offset
            hi = lo + sum((c - 1) * s for s, c in dims if c > 0)
            return name, (0, 0, lo * esz, (hi + 1) * esz), True
        fstride = dims[0][0]
        p0 = ap.offset // fstride
        f0 = ap.offset % fstride
        p1 = p0 + dims[0][1] - 1
        f1 = f0 + sum((c - 1) * s for s, c in dims[1:])
        return name, (p0, p1, f0 * esz, (f1 + 1) * esz), False

    @staticmethod
    def overlap(a, b):
        return not (a[1] < b[0] or b[1] < a[0] or a[3] <= b[2] or b[3] <= a[2])

    @staticmethod
    def contains(a, b):
        return a[0] <= b[0] and a[1] >= b[1] and a[2] <= b[2] and a[3] >= b[3]

    def add(self, eng, fn, w=(), r=(), dma_key=None, untracked=(), inc=None):
        opid = len(self.ops)
        deps = set()
        is_dma = dma_key is not None
        rboxes = [self.box(a) for a in r]
        wboxes = [self.box(a) for a in w]
        psum_names = set()
        for name, bx, isdram in rboxes + wboxes:
            if name.startswith("PB") or name.startswith("PT"):
                psum_names.add(name)
        for name in psum_names:
            last = self.psum_last.setdefault(name, {})
            for e2, o2 in last.items():
                if e2 != eng:
                    deps.add(o2)
            last[eng] = opid
        rboxes = [b for b in rboxes if b[0] not in psum_names]
        wboxes = [b for b in wboxes if b[0] not in psum_names]
        for name, bx, isdram in rboxes:
            if name in untracked:
                continue
            h = self.hist.setdefault(name, [])
            for (b2, o2, isw, e2, d2) in h:
                if isw and self.overlap(bx, b2):
                    deps.add(o2)
        for name, bx, isdram in wboxes:
            if name in untracked:
                continue
            h = self.hist.setdefault(name, [])
            for (b2, o2, isw, e2, d2) in h:
                if self.overlap(bx, b2):
                    deps.add(o2)
        for name, bx, isdram in rboxes:
            if name in untracked:
                continue
            h = self.hist[name]
            if not is_dma:
                h[:] = [e for e in h if not ((not e[2]) and e[3] == eng and (not e[4])
                                             and self.contains(bx, e[0]))]
            h.append((bx, opid, False, eng, is_dma))
        for name, bx, isdram in wboxes:
            if name in untracked:
                continue
            h = self.hist[name]
            h[:] = [e for e in h if not self.contains(bx, e[0])]
            h.append((bx, opid, True, eng, is_dma))
        deps.discard(opid)
        self.ops.append(dict(eng=eng, fn=fn, deps=deps, dma_key=dma_key, phase=self.phase,
                             signal=False, inc_override=inc))
        return opid

    def emit(self, final_wait_ops):
        nc = self.nc
        ops = self.ops
        for o in ops:
            if o["eng"] == "pe" and o["dma_key"] is None:
                o["deps"] = {d for d in o["deps"]
                             if not (ops[d]["eng"] == "pe" and ops[d]["dma_key"] is None)}
        for o in ops:
            for d in o["deps"]:
                ops[d]["signal"] = True
        for d in final_wait_ops:
            ops[d]["signal"] = True
        for o in ops:
            if o["dma_key"] is not None:
                o["signal"] = True
        sem_keys = []
        counts = {}
        for o in ops:
            if not o["signal"]:
                continue
            if o["dma_key"] is not None:
                k = ("dma", o["dma_key"])
                inc = 16 if o["inc_override"] is None else o["inc_override"]
            else:
                k = (o["eng"], o["phase"])
                inc = 1
            if k not in counts:
                counts[k] = 0
                sem_keys.append(k)
            counts[k] += inc
            o["sem"] = k
            o["val"] = counts[k]
            o["inc"] = inc
        rng = bass.get_kernel_semaphore_range()
        assert len(sem_keys) <= len(rng) - 2, f"too many semaphores: {len(sem_keys)}"
        sems = {}
        self._sem_cms = []
        for k in sem_keys:
            cm = nc.semaphore(("s_%s_%s" % k).replace("-", "_"))
            s = cm.__enter__()
            self._sem_cms.append(cm)
            sems[k] = s
        waited = {}
        nwait = 0
        issued = {}
        for o in ops:
            eobj = self.eng_objs[o["eng"]]
            need = {}
            for d in o["deps"]:
                od = ops[d]
                k = od["sem"]
                if od["dma_key"] is not None:
                    need[k] = max(need.get(k, 0), issued[k])
                else:
                    need[k] = max(need.get(k, 0), od["val"])
            for k, v in need.items():
                wk = (o["eng"], k)
                if waited.get(wk, 0) >= v:
                    continue
                eobj.wait_ge(sems[k], v)
                waited[wk] = v
                nwait += 1
            ins = o["fn"]()
            if o["signal"]:
                ins.then_inc(sems[o["sem"]], o["inc"])
                if o["dma_key"] is not None:
                    issued[o["sem"]] = o["val"]
        need = {}
        for d in final_wait_ops:
            od = ops[d]
            need[od["sem"]] = max(need.get(od["sem"], 0), od["val"])
        for k, v in need.items():
            nc.sync.wait_ge(sems[k], v)
        self.stats = dict(nops=len(ops), nwait=nwait, nsem=len(sem_keys),
                          maxcount=max(counts.values()) if counts else 0)

    def close(self):
        for cm in reversed(self._sem_cms):
            cm.__exit__(None, None, None)


def ret_gammas():
    return (1.0 - np.exp2(-5.0 - np.arange(RH, dtype=np.float64)))


def host_consts():
    c = {}
    c["ident"] = np.eye(128, dtype=np.float32)
    half = 32
    inv = np.power(np.float32(10000.0), -np.arange(half, dtype=np.float32) / np.float32(half)).astype(np.float32)
    pos = np.arange(SEQ, dtype=np.float32)
    ang = (pos[:, None] * inv[None, :]).astype(np.float32)
    cosp = np.cos(ang).astype(np.float32).reshape(SEQ // 128, 128, half).transpose(1, 0, 2)
    sinp = np.sin(ang).astype(np.float32).reshape(SEQ // 128, 128, half).transpose(1, 0, 2)
    c["cosp"] = np.ascontiguousarray(cosp)
    c["sinp"] = np.ascontiguousarray(sinp)
    c["nsinp"] = np.ascontiguousarray(-sinp)
    poss = (PAST_LEN + (np.arange(128) % DEC_SEQ)).astype(np.float32)
    angs = (poss[:, None] * inv[None, :]).astype(np.float32)
    c["coss"] = np.cos(angs).astype(np.float32).reshape(128, 1, half)
    c["sins"] = np.sin(angs).astype(np.float32).reshape(128, 1, half)
    c["nsins"] = (-c["sins"]).astype(np.float32)
    g = ret_gammas()
    lg = np.log(g)
    j = np.arange(128)[:, None]
    i = np.arange(128)[None, :]
    dec = np.zeros((128, RH, 128), np.float64)
    decs = np.zeros((128, RH, 128), np.float64)
    for h in range(RH):
        dec[:, h, :] = np.where(i >= j, np.exp(lg[h] * np.maximum(i - j, 0)), 0.0) * RDK ** -0.5
        same = (i // DEC_SEQ) == (j // DEC_SEQ)
        decs[:, h, :] = np.where((i >= j) & same, np.exp(lg[h] * np.maximum(i - j, 0)), 0.0) * RDK ** -0.5
    c["decp"] = dec.astype(np.float32)
    c["decs"] = decs.astype(np.float32)
    gq = np.zeros((128, 4, 128), np.float64)
    gqs = np.zeros((128, 4, 128), np.float64)
    gl = np.zeros((128, 4), np.float64)
    gls = np.zeros((128, 4), np.float64)
    ii = np.arange(128)
    for pr in range(4):
        for hh in range(2):
            h = 2 * pr + hh
            gq[hh * 64:(hh + 1) * 64, pr, :] = np.exp(lg[h] * (ii + 1.0))[None, :]
            gqs[hh * 64:(hh + 1) * 64, pr, :] = np.exp(lg[h] * ((ii % DEC_SEQ) + 1.0))[None, :]
            gl[hh * 64:(hh + 1) * 64, pr] = np.exp(lg[h] * 128.0)
            gls[hh * 64:(hh + 1) * 64, pr] = np.exp(lg[h] * float(DEC_SEQ))
    c["gqp"] = gq.astype(np.float32)
    c["gqs"] = gqs.astype(np.float32)
    c["glp"] = gl.astype(np.float32)
    c["gls"] = gls.astype(np.float32)
    wend = np.exp(lg[None, :] * (127.0 - ii[:, None])) * RDK ** -0.5
    wends = np.exp(lg[None, :] * (DEC_SEQ - 1.0 - (ii[:, None] % DEC_SEQ))) * RDK ** -0.5
    c["wendp"] = wend.astype(np.float32)
    c["wends"] = wends.astype(np.float32)
    own = np.where(j <= i, 0.0, NEG)
    prev = np.where(j > i, 0.0, NEG)
    c["mown"] = np.tile(own, (1, 4)).astype(np.float32)
    c["mprev"] = np.tile(prev, (1, 4)).astype(np.float32)
    same = (i // DEC_SEQ) == (j // DEC_SEQ)
    news = np.where(same & (j <= i), 0.0, NEG)
    c["mnews"] = np.tile(news, (1, 4)).astype(np.float32)
    kk = np.arange(128)[:, None]
    t8 = np.arange(DEC_SEQ)[None, :]
    mc = np.where(kk >= t8 + 1, 0.0, NEG)
    c["mcache"] = np.tile(mc, (1, 64)).astype(np.float32)
    bm = (np.arange(128)[:, None] // DEC_SEQ == np.arange(SB_PER_CORE)[None, :]).astype(np.float32)
    c["bmrow"] = bm
    return c


CONST_SHAPES = None


class Builder:
    def __init__(self, cfg):
        self.cfg = cfg
        self.nc = bass.Bass("TRN2", target_bir_lowering=False)
        self.T = Tracker(self.nc)
        self.cms = []
        self.final_ops = []

    def sb(self, name, shape, dt):
        cm = self.nc.sbuf_tensor(name, list(shape), dt)
        t = cm.__enter__()
        self.cms.append(cm)
        return t

    def ps(self, name, shape, dt):
        cm = self.nc.psum_tensor(name, list(shape), dt)
        t = cm.__enter__()
        self.cms.append(cm)
        return t

    def din(self, name, shape, dt=F32):
        return self.nc.dram_tensor(name, list(shape), dt, kind="ExternalInput").ap()

    def dout(self, name, shape, dt=F32):
        return self.nc.dram_tensor(name, list(shape), dt, kind="ExternalOutput").ap()

    def mm(self, out, lhsT, rhs, start, stop, skip=False):
        nc = self.nc
        if skip:
            return self.T.add("pe", lambda: nc.tensor.matmul(out, lhsT=lhsT, rhs=rhs, start=start, stop=stop,
                                                             skip_group_check=True),
                              w=[out], r=[lhsT, rhs, out])
        return self.T.add("pe", lambda: nc.tensor.matmul(out, lhsT=lhsT, rhs=rhs, start=start, stop=stop),
                          w=[out], r=[lhsT, rhs] + ([] if start else [out]))

    def trf(self, out, in_, identf):
        nc = self.nc
        return self.T.add("pe", lambda: nc.tensor.transpose(out=out, in_=in_, identity=identf),
                          w=[out], r=[in_, identf])

    def tr(self, out, in_):
        nc = self.nc
        ident = self.identb[:]
        return self.T.add("pe", lambda: nc.tensor.transpose(out=out, in_=in_, identity=ident),
                          w=[out], r=[in_, ident])

    def act(self, out, in_, func, scale=1.0, bias=None):
        nc = self.nc
        r = [in_]
        kw = {}
        if not isinstance(scale, float):
            r.append(scale)
        if bias is not None:
            kw["bias"] = bias
            if not isinstance(bias, float):
                r.append(bias)
        return self.T.add("act", lambda: nc.scalar.activation(out=out, in_=in_, func=func, scale=scale, **kw),
                          w=[out], r=r)

    def tt(self, out, in0, in1, op, eng="dve"):
        nc = self.nc
        e = nc.vector if eng == "dve" else nc.gpsimd
        return self.T.add(eng, lambda: e.tensor_tensor(out=out, in0=in0, in1=in1, op=op), w=[out], r=[in0, in1])

    def ts(self, out, in0, s1, op0, s2=None, op1=None, eng="dve"):
        nc = self.nc
        e = nc.vector if eng == "dve" else nc.gpsimd
        r = [in0] + [s for s in (s1, s2) if s is not None and not isinstance(s, float)]
        if op1 is None:
            return self.T.add(eng, lambda: e.tensor_scalar(out=out, in0=in0, scalar1=s1, scalar2=None, op0=op0),
                              w=[out], r=r)
        return self.T.add(eng, lambda: e.tensor_scalar(out=out, in0=in0, scalar1=s1, scalar2=s2, op0=op0, op1=op1),
                          w=[out], r=r)

    def stt(self, out, in0, scalar, in1, op0, op1):
        nc = self.nc
        r = [in0, in1] + ([] if isinstance(scalar, float) else [scalar])
        return self.T.add("dve", lambda: nc.vector.scalar_tensor_tensor(out=out, in0=in0, scalar=scalar, in1=in1,
                                                                         op0=op0, op1=op1), w=[out], r=r)

    def cp(self, out, in_, eng="dve"):
        nc = self.nc
        if eng == "act":
            return self.T.add("act", lambda: nc.scalar.copy(out=out, in_=in_), w=[out], r=[in_])
        e = nc.vector if eng == "dve" else nc.gpsimd
        return self.T.add(eng, lambda: e.tensor_copy(out=out, in_=in_), w=[out], r=[in_])

    def memset(self, ap, val, eng="dve"):
        nc = self.nc
        e = nc.vector if eng == "dve" else nc.gpsimd
        return self.T.add(eng, lambda: e.memset(ap, val), w=[ap])

    def dma(self, out, in_, key, q="sp", final=False):
        nc = self.nc
        e = {"sp": nc.sync, "pool": nc.gpsimd, "act": nc.scalar}[q]
        if key is None or not key.startswith("!"):
            on, inn = out.tensor.name, in_.tensor.name
            key = ("st_" + inn) if on in self.dram_names else ("ld_" + on)
        op = self.T.add(q, lambda: e.dma_start(out=out, in_=in_), w=[out], r=[in_], dma_key=key,
                        untracked=self.untracked)
        if final:
            self.final_ops.append(op)
        return op

    def bnstats(self, out, in_):
        nc = self.nc
        return self.T.add("dve", lambda: nc.vector.bn_stats(out=out, in_=in_), w=[out], r=[in_])

    def bnaggr(self, out, in_):
        nc = self.nc
        return self.T.add("dve", lambda: nc.vector.bn_aggr(out=out, in_=in_), w=[out], r=[in_])

    def recip(self, out, in_):
        nc = self.nc
        return self.T.add("dve", lambda: nc.vector.reciprocal(out=out, in_=in_), w=[out], r=[in_])


def V(t, off, dims):
    f = 1
    for s in t.shape[1:]:
        f *= s
    return bass.AP(t, off, [[f, 128]] + [list(d) for d in dims])


def VP(t, p0, pn, off, dims):
    f = 1
    for s in t.shape[1:]:
        f *= s
    return bass.AP(t, p0 * f + off, [[f, pn]] + [list(d) for d in dims])


def AV(view, off, dims, p0=0, pn=128):
    f = view.ap[0][0]
    base = view.offset
    return bass.AP(view.tensor, base + p0 * f + off, [[f, pn]] + [list(d) for d in dims])


def build(cfg):
    nsc = cfg["nsc"]
    nlay = cfg["nlay"]
    stage = cfg.get("stage", 99)
    sstage = cfg.get("sstage", 99)
    dbgname = cfg.get("dbg", None)
    do_sample = cfg["sample"]
    B = Builder(cfg)
    nc = B.nc
    T = B.T
    NTOK = nsc * 512
    x_p = B.din("x_p", [NTOK, D])
    mem_p = B.din("mem_p", [NMEM, D])
    x_s = B.din("x_s", [128, D])
    st_ret = B.din("st_ret", [DEPTH, SB_PER_CORE, RH, RDK, RDV])
    c_sk = B.din("c_sk", [DEPTH, SB_PER_CORE, 128, 128])
    c_sv = B.din("c_sv", [DEPTH, SB_PER_CORE, 128, 128])
    c_mk = B.din("c_mk", [DEPTH, SB_PER_CORE, NMEM, 512])
    c_mv = B.din("c_mv", [DEPTH, SB_PER_CORE, NMEM, 512])
    w_in = B.din("w_in", [DEPTH, D, IN_W])
    w_br_ret = B.din("w_br_ret", [DEPTH, 1024, D])
    w_br_swa = B.din("w_br_swa", [DEPTH, 512, D])
    w_br_mem = B.din("w_br_mem", [DEPTH, 512, D])
    w_out = B.din("w_out", [DEPTH, D, D])
    w_mem_kv = B.din("w_mem_kv", [DEPTH, D, 1024])
    sinks = B.din("sinks", [DEPTH, SH])
    gn_g = B.din("gn_g", [DEPTH, 1024])
    ln1_g = B.din("ln1_g", [DEPTH, D])
    ln1_b = B.din("ln1_b", [DEPTH, D])
    w_up = B.din("w_up", [DEPTH, D, DFF])
    w_down = B.din("w_down", [DEPTH, DFF, D])
    ln2_g = B.din("ln2_g", [DEPTH, D])
    ln2_b = B.din("ln2_b", [DEPTH, D])
    own = dict(w_in=B.din("wo_in", [D, IN_W]), w_br_ret=B.din("wo_br_ret", [1024, D]),
               w_br_swa=B.din("wo_br_swa", [512, D]), w_br_mem=B.din("wo_br_mem", [512, D]),
               w_out=B.din("wo_out", [D, D]), w_mem_kv=B.din("wo_mem_kv", [D, 1024]),
               sinks=B.din("o_sinks", [1, SH]), gn_g=B.din("o_gn_g", [1024]), ln1_g=B.din("o_ln1_g", [D]),
               ln1_b=B.din("o_ln1_b", [D]), w_up=B.din("wo_up", [D, DFF]), w_down=B.din("wo_down", [DFF, D]),
               ln2_g=B.din("o_ln2_g", [D]), ln2_b=B.din("o_ln2_b", [D]))
    own_names = [v.tensor.name for v in own.values()]
    sinks_full = sinks
    w_in = [w_in[0], w_in[1], own["w_in"]]
    w_br_ret = [w_br_ret[0], w_br_ret[1], own["w_br_ret"]]
    w_br_swa = [w_br_swa[0], w_br_swa[1], own["w_br_swa"]]
    w_br_mem = [w_br_mem[0], w_br_mem[1], own["w_br_mem"]]
    w_out = [w_out[0], w_out[1], own["w_out"]]
    w_mem_kv = [w_mem_kv[0], w_mem_kv[1], own["w_mem_kv"]]
    w_up = [w_up[0], w_up[1], own["w_up"]]
    w_down = [w_down[0], w_down[1], own["w_down"]]
    gn_g = [gn_g[0], gn_g[1], own["gn_g"]]
    ln1_g = [ln1_g[0], ln1_g[1], own["ln1_g"]]
    ln1_b = [ln1_b[0], ln1_b[1], own["ln1_b"]]
    ln2_g = [ln2_g[0], ln2_g[1], own["ln2_g"]]
    ln2_b = [ln2_b[0], ln2_b[1], own["ln2_b"]]
    NSLOT = nsc + 2
    sel_d = B.din("sel", [128, 2])
    flag_d = B.din("flagneg", [128, 512])
    r_cos = B.din("r_cos", [128, NSLOT * 4, 32])
    r_sin = B.din("r_sin", [128, NSLOT * 4, 32])
    r_nsin = B.din("r_nsin", [128, NSLOT * 4, 32])
    cc_src = [nc.dram_tensor("cc_src%d" % i, [512, D], F32, kind="Internal").ap() for i in range(2)]
    cc_dst = [nc.dram_tensor("cc_dst%d" % i, [2 * 512, D], F32, kind="Internal").ap() for i in range(2)]
    hc = host_consts()
    cd = {k: B.din("c_" + k, list(v.shape)) for k, v in hc.items()}
    y_p = B.dout("y_p", [NSLOT * 512, D])
    y_s = B.dout("y_s", [128, D])
    rs_p = B.dout("rs_p", [2, RH, RDK, RDV])
    sk_p = B.dout("sk_p", [2, 128, 128])
    sv_p = B.dout("sv_p", [2, 128, 128])
    mk_p = B.dout("mk_p", [NMEM, 512])
    mv_p = B.dout("mv_p", [NMEM, 512])
    rs_s = B.dout("rs_s", [DEPTH, SB_PER_CORE, RH, RDK, RDV])
    sk_s = B.dout("sk_s", [DEPTH, SB_PER_CORE, 128, 128])
    sv_s = B.dout("sv_s", [DEPTH, SB_PER_CORE, 128, 128])
    B.dram_names = set(t.tensor.name for t in [y_p, y_s, rs_p, sk_p, sv_p, mk_p, mv_p, rs_s, sk_s, sv_s])
    B.untracked = set(n.tensor.name for n in
                      [x_p, mem_p, x_s, st_ret, c_sk, c_sv, c_mk, c_mv, w_in[0], w_br_ret[0], w_br_swa[0], w_br_mem[0],
                       w_out[0],
                       w_mem_kv[0], sinks, gn_g[0], ln1_g[0], ln1_b[0], w_up[0], w_down[0], ln2_g[0], ln2_b[0],
                       sel_d, flag_d, r_cos, r_sin, r_nsin]
                      + list(cd.values())) | set(own_names)

    sb = B.sb
    B.identb = sb("identb", [128, 128], BF16)
    onesb = sb("onesb", [128, 128], BF16)
    X = sb("X", [128, 4, D], F32)
    XT = sb("XT", [128, 8, 512], BF16)
    Xb = sb("Xb", [128, D], BF16)
    ARENA = sb("ARENA", [128, 32 * 512], BF16)
    HT = ARENA[:].rearrange("p (a b) -> p a b", a=32)
    GT = ARENA[:, 0:24 * 512].rearrange("p (a b) -> p a b", a=24)
    MQT = ARENA[:, 24 * 512:28 * 512].rearrange("p (a b) -> p a b", a=4)
    RQT = ARENA[:, 28 * 512:32 * 512].rearrange("p (a b) -> p a b", a=4)
    PV_ = dict(HT=HT, GT=GT, MQT=MQT, RQT=RQT)
    SV_ = dict(HT=ARENA[:, 0:32 * 128].rearrange("p (a b) -> p a b", a=32),
               GT=ARENA[:, 0:24 * 128].rearrange("p (a b) -> p a b", a=24),
               MQT=ARENA[:, 24 * 128:28 * 128].rearrange("p (a b) -> p a b", a=4),
               RQT=ARENA[:, 28 * 128:32 * 128].rearrange("p (a b) -> p a b", a=4))
    a0 = 32 * 128
    RKwX = ARENA[:, a0:a0 + 4096].rearrange("p (r c) -> p r c", r=8)
    S0bg = ARENA[:, a0 + 4096:a0 + 6144].rearrange("p (b r e) -> p b r e", b=4, r=4)
    Ec = ARENA[:, a0 + 6144:a0 + 7168].rearrange("p (b g c) -> p b g c", b=16, g=2)
    KcT = ARENA[:, a0 + 7168:a0 + 9216].rearrange("p (b k) -> p b k", b=16)
    Kraw = ARENA[:, a0 + 9216:a0 + 11264].rearrange("p (b k) -> p b k", b=16)
    SQTs = ARENA[:, a0 + 11264:a0 + 12288].rearrange("p (g c) -> p g c", g=2)
    S0g = X[:, 1:3, :].rearrange("p a (b e) -> p (a b) e", e=128).rearrange("p (b r) e -> p b r e", b=4)
    Vcp = X[:, 3, :].bitcast(BF16).rearrange("p (b g h c) -> p b g h c", b=4, g=2, h=2)
    identF = sb("identF", [128, 128], F32)
    BMR = sb("BMR", [128, 16], F32)
    RKm = sb("RKm", [128, 4, 8, 128], BF16)
    SQT = sb("SQT", [128, 4, 512], BF16)
    SKm = sb("SKm", [128, 5, 2, 128], BF16)
    RKw = sb("RKw", [128, 4, 512], BF16)
    Vb = sb("Vb", [128, 4, 1024], BF16)
    MT = Vb[:].rearrange("p a (k t) -> p (a k) t", k=2)
    SRG = sb("SRG", [128, 4, 1024], BF16)
    SVp = sb("SVp", [128, 5, 2, 2, 128], BF16)
    ONp = sb("ONp", [128, 2, 128], BF16)
    GRT = sb("GRT", [128, 8, 512], BF16)
    SWOT = sb("SWOT", [128, 4, 512], BF16)
    MOT = sb("MOT", [128, 4, 512], BF16)
    T1 = sb("T1", [128, 512], F32)
    T2 = sb("T2", [128, 512], F32)
    RF = sb("RF", [128, 512], F32)
    RQb = sb("RQb", [128, 512], BF16)
    ST = sb("ST", [128, 8, 128], BF16)
    RQs2 = sb("RQs2", [128, 8, 128], BF16)
    Y = sb("Y", [128, 8, 128], F32)
    GR = sb("GR", [128, 1024], BF16)
    STAT = sb("STAT", [128, 8, 6], F32)
    MV = sb("MV", [128, 8, 2], F32)
    E = sb("E", [128, 4, 512], BF16)
    MEMT = E[:].rearrange("p a (k m) -> p (a k) m", k=2)
    identf = T2[:, 0:128]
    MSKf = T1
    DEN = sb("DEN", [128, 512], F32)
    LNB = sb("LNB", [128, 2, D], F32)
    GNG = sb("GNG", [128, 1024], F32)
    NW = 3
    WR = [sb("WR%d" % i, [128, 8, 512], BF16) for i in range(NW)]
    ROPE = sb("ROPE", [128, 3, 4, 32], F32)
    DEC = sb("DEC", [128, 8, 128], F32)
    GQ = sb("GQ", [128, 4, 128], F32)
    WEND = sb("WEND", [128, 8], F32)
    GL = sb("GL", [128, 4], F32)
    MOWN = sb("MOWN", [128, 512], BF16)
    MPREV = sb("MPREV", [128, 512], BF16)
    SINKE = sb("SINKE", [128, 3, 4], F32)
    Sst = [sb("Sst", [128, 4, 128], F32)] * 3
    Sbf = [sb("Sbf", [128, 4, 128], BF16)] * 3
    SKTprev = [sb("SKTp", [128, 2, 128], BF16)] * 3
    SVprev = [sb("SVpv", [128, 2, 2, 128], BF16)] * 3
    MKT = [sb("MKT", [128, 4, 256], BF16)] * 3
    MVb = [sb("MVb", [128, 2, 512], BF16)] * 3
    SEL = sb("SEL", [128, 2], F32)
    XbN = sb("XbN", [128, 4, D], BF16)
    XbS = Xb[:].bitcast(F32)
    FLAGN = sb("FLAGN", [128, 512], BF16)
    SKf = sb("SKf", [128, 256], F32)
    PB = [B.ps("PB%d" % i, [128, 512], F32) for i in range(6)]
    PT = [B.ps("PT%d" % i, [128, 1024], BF16) for i in range(2)]

    B.dma(identf, cd["ident"], "c0")
    B.cp(B.identb[:], identf)
    B.memset(onesb[:], 1.0)
    B.memset(ONp[:], 0.0)
    B.memset(ONp[:, 0, 0:64], 1.0)
    B.memset(ONp[:, 1, 64:128], 1.0)
    B.memset(SVp[:], 0.0)
    B.memset(RKm[:], 0.0)
    B.memset(SKm[:], 0.0)
    B.memset(RQs2[:], 0.0)
    for l in range(1):
        B.memset(SVprev[l][:], 0.0)
        B.memset(SKTprev[l][:], 0.0)
        B.memset(Sst[l][:], 0.0)
        B.memset(Sbf[l][:], 0.0)
    B.dma(SEL[:], sel_d, None)
    B.dma(MSKf[:], flag_d, None)
    B.cp(FLAGN[:], MSKf[:])
    B.dma(MSKf[:], cd["mown"], "c0")
    B.cp(MOWN[:], MSKf[:])
    B.dma(MSKf[:], cd["mprev"], "c0")
    B.cp(MPREV[:], MSKf[:])
    SKRAW = sb("SKRAW", [128, 3, SH], F32)
    B.dma(SKRAW[:, 0:2, :], bass.AP(sinks_full.tensor, 0, [[0, 128], [SH, DEPTH], [1, SH]]), "c0")
    B.dma(SKRAW[:, 2, :], bass.AP(own["sinks"].tensor, 0, [[0, 128], [1, SH]]), "c0")
    for hh in range(2):
        B.act(SINKE[hh * 64:(hh + 1) * 64, :, :],
              AV(SKRAW[:], hh, [[SH, 3], [2, 4]], p0=hh * 64, pn=64), AF.Exp)

    def load_prompt_consts():
        B.dma(DEC[:], cd["decp"], "c0")
        B.dma(GQ[:], cd["gqp"], "c0")
        B.dma(WEND[:], cd["wendp"], "c0")
        B.dma(GL[:], cd["glp"], "c0")

    wstate = dict(idx=0, plan=[])

    def wplan_for_pass(l, first):
        pl = []
        if first:
            pl.append((w_mem_kv[l][:, 0:512], 8, 512))
            pl.append((w_mem_kv[l][:, 512:1024], 8, 512))
        for j in range(7):
            pl.append((w_in[l][:, j * 512:(j + 1) * 512], 8, 512))
        pl.append((w_in[l][:, 3584:3840], 8, 256))
        for jj in range(7):
            pl.append((w_in[l][:, 3840 + jj * 512:3840 + (jj + 1) * 512], 8, 512))
        for cb in range(2):
            pl.append((w_br_ret[l][:, cb * 512:(cb + 1) * 512], 8, 512))
            pl.append(((w_br_swa[l][:, cb * 512:(cb + 1) * 512], w_br_mem[l][:, cb * 512:(cb + 1) * 512]), 8, 512))
        for cb in range(2):
            pl.append((w_out[l][:, cb * 512:(cb + 1) * 512], 8, 512))
        for jb in range(8):
            pl.append((w_up[l][:, jb * 512:(jb + 1) * 512], 8, 512))
        for cb in range(2):
            for rb in range(4):
                pl.append((w_down[l][rb * 1024:(rb + 1) * 1024, cb * 512:(cb + 1) * 512], 8, 512))
        return pl

    def w_issue(i):
        src, nkt, ncols = wstate["plan"][i]
        slot = WR[i % NW]
        if isinstance(src, tuple):
            for half, s in enumerate(src):
                B.dma(slot[:, half * 4:(half + 1) * 4, 0:ncols], s.rearrange("(kt p) c -> p kt c", p=128),
                      "w%d" % (i % NW), q="pool")
        else:
            B.dma(slot[:, 0:nkt, 0:ncols], src.rearrange("(kt p) c -> p kt c", p=128), "w%d" % (i % NW), q="pool")

    PF = 1

    def w_next():
        i = wstate["idx"]
        if i == 0:
            for k in range(min(PF, len(wstate["plan"]))):
                w_issue(k)
        if i + PF < len(wstate["plan"]):
            w_issue(i + PF)
        wstate["idx"] = i + 1
        return WR[i % NW]

    def transposes_to(dst_view_fn, src_fn, n, ptile, evac_eng="dve"):
        for k in range(n):
            B.tr(ptile[:, k * 128:(k + 1) * 128], src_fn(k))
        B.cp(dst_view_fn, ptile[:, 0:n * 128].rearrange("p (a b) -> p a b", a=n), eng=evac_eng)

    def rope_block(ps, nheads, tabidx, out_ap_f32=None, out_ap_bf=None, sample=False, perm=False):
        w = nheads * 64
        cosb = AV(ROPE[:], (0 * 4 + tabidx) * 32, [[0, nheads], [0, 2], [1, 32]])
        sinb = AV(ROPE[:], (1 * 4 + tabidx) * 32, [[0, nheads], [1, 32]])
        nsinb = AV(ROPE[:], (2 * 4 + tabidx) * 32, [[0, nheads], [1, 32]])
        psv = ps.rearrange("p (h t f) -> p h t f", h=nheads, t=2)
        t1v = T1[:, 0:w].rearrange("p (h t f) -> p h t f", h=nheads, t=2)
        t2v = T2[:, 0:w].rearrange("p (h t f) -> p h t f", h=nheads, t=2)
        B.tt(t1v, psv, cosb, ALU.mult)
        B.tt(t2v[:, :, 0, :], psv[:, :, 1, :], nsinb, ALU.mult)
        B.tt(t2v[:, :, 1, :], psv[:, :, 0, :], sinb, ALU.mult)
        if out_ap_f32 is not None:
            B.tt(out_ap_f32, T1[:, 0:w], T2[:, 0:w], ALU.add)
            if out_ap_bf is not None:
                B.cp(out_ap_bf, out_ap_f32, eng="act")
        elif perm:
            B.tt(out_ap_bf, T1[:, 0:w].rearrange("p (s t d) -> p s t d", s=2, t=4),
                 T2[:, 0:w].rearrange("p (s t d) -> p s t d", s=2, t=4), ALU.add)
        else:
            B.tt(out_ap_bf, T1[:, 0:w], T2[:, 0:w], ALU.add)

    def layer_norm_chunk(xc, gslot=0):
        for a in range(2):
            B.bnstats(STAT[:, a, :], xc[:, a * 512:(a + 1) * 512])
        B.bnaggr(MV[:, 0, :], STAT[:, 0:2, :].rearrange("p a b -> p (a b)"))
        B.ts(MV[:, 1, 0:1], MV[:, 0, 1:2], LN_EPS, ALU.add)
        B.act(MV[:, 1, 0:1], MV[:, 1, 0:1], AF.Ln)
        B.act(MV[:, 1, 0:1], MV[:, 1, 0:1], AF.Exp, scale=-0.5)
        B.ts(xc, xc, MV[:, 0, 0:1], ALU.subtract, MV[:, 1, 0:1], ALU.mult)
        B.tt(xc, xc, LNB[:, 0, :], ALU.mult)
        B.tt(xc, xc, LNB[:, 1, :], ALU.add)

    def group_norm_gate(ops, c, nch_tok):
        for h in range(8):
            B.bnstats(STAT[:, h, :], ops[h // 4][:, (h % 4) * 128:(h % 4 + 1) * 128])
        for h in range(8):
            B.bnaggr(MV[:, h, :], STAT[:, h, :])
        B.ts(MV[:, :, 1], MV[:, :, 1], GN_EPS, ALU.add)
        B.act(MV[:, :, 1], MV[:, :, 1], AF.Ln)
        B.act(MV[:, :, 1], MV[:, :, 1], AF.Exp, scale=-0.5)
        for half in range(2):
            pv = ops[half][:].rearrange("p (h e) -> p h e", h=4)
            B.tt(Y[:, half * 4:(half + 1) * 4, :], pv,
                 AV(MV[:], half * 8, [[2, 4], [0, 128]]), ALU.subtract)
        B.tt(Y[:], Y[:], AV(MV[:], 1, [[2, 8], [0, 128]]), ALU.mult)
        B.tt(Y[:], Y[:], GNG[:].rearrange("p (h e) -> p h e", h=8), ALU.mult)
        B.tt(GR[:], Y[:].rearrange("p h e -> p (h e)"), SRG[:, c, :], ALU.mult)
        transposes_to(GRT[:, :, c * 128:(c + 1) * 128], lambda k: GR[:, k * 128:(k + 1) * 128], 8, PT[0],
                      evac_eng="act")


    EM = sb("EM", [128, 2, 512], BF16)

    M = dict(PV_)

    def bcast_row(dram_ap_row, n):
        return bass.AP(dram_ap_row.tensor, dram_ap_row.offset, [[0, 128], [1, n]])

    def compute_memT():
        for mb in range(2):
            for half in range(2):
                B.dma(T1[:], mem_p[mb * 128:(mb + 1) * 128, half * 512:(half + 1) * 512], "x")
                B.cp(Xb[:, 0:512], T1[:], eng="act")
                transposes_to(MEMT[:, half * 4:(half + 1) * 4, mb * 128:(mb + 1) * 128],
                              lambda k: Xb[:, k * 128:(k + 1) * 128], 4, PT[half])

    def mem_kv(l):
        import os
        SUB = int(os.environ.get("SUB", "99"))
        compute_memT()
        Wk = w_next()
        for mb in range(2):
            ps = PB[mb]
            for kt in range(8):
                B.mm(ps[:], MEMT[:, kt, mb * 128:(mb + 1) * 128], Wk[:, kt, :], kt == 0, kt == 7)
            if SUB < 1:
                continue
            B.cp(T1[:], ps[:], eng="act")
            if SUB < 2:
                continue
            B.dma(mk_p[mb * 128:(mb + 1) * 128, :], T1[:], "o_mk", final=True)
            if SUB < 3:
                continue
            B.cp(RQb[:], ps[:])
            transposes_to(MKT[l][:, :, mb * 128:(mb + 1) * 128], lambda k: RQb[:, k * 128:(k + 1) * 128], 4, PT[mb])
        if SUB < 4:
            return
        Wv = w_next()
        for mb in range(2):
            ps = PB[2 + mb]
            for kt in range(8):
                B.mm(ps[:], MEMT[:, kt, mb * 128:(mb + 1) * 128], Wv[:, kt, :], kt == 0, kt == 7)
            B.cp(T2[:], ps[:], eng="act")
            B.dma(mv_p[mb * 128:(mb + 1) * 128, :], T2[:], "o_mv", final=True)
            B.cp(MVb[l][:, mb, :], ps[:])

    def load_layer_params(l):
        B.dma(GNG[:], bcast_row(gn_g[l], 1024), "prm")
        B.dma(LNB[:, 0, :], bcast_row(ln1_g[l], D), "prm")
        B.dma(LNB[:, 1, :], bcast_row(ln1_b[l], D), "prm")

    def xT_phase(nch):
        for c in range(nch):
            B.cp(Xb[:], X[:, c, :], eng="act")
            transposes_to(XT[:, :, c * 128:(c + 1) * 128], lambda k: Xb[:, k * 128:(k + 1) * 128], 8, PT[c % 2])

    def in_proj(l, nch, sample, last_chunk_out=None, mid_hook=None):
        NTk = nch * 128
        for j in range(8):
            W = w_next()
            ncols = 512 if j < 7 else 256
            for c in range(nch):
                ps = PB[(j * nch + c) % 3]
                for kt in range(8):
                    B.mm(ps[:, 0:ncols], XT[:, kt, c * 128:(c + 1) * 128], W[:, kt, 0:ncols], kt == 0, kt == 7)
                tab = 0 if sample else c
                if j == 0:
                    rope_block(ps[:], 8, tab, out_ap_bf=RQb[:])
                    transposes_to(M["RQT"][:, :, c * 128:(c + 1) * 128], lambda k: RQb[:, k * 128:(k + 1) * 128], 4, PT[1])
                elif j == 1:
                    rope_block(ps[:], 8, tab, out_ap_f32=RF[:], out_ap_bf=RQb[:])
                    for k in range(4):
                        B.tr(PT[1][:, k * 128:(k + 1) * 128], RQb[:, k * 128:(k + 1) * 128])
                    for hh in range(2):
                        B.cp(AV(RKm[:], (c * 8 + hh) * 128, [[256, 4], [1, 128]], p0=hh * 64, pn=64),
                             PT[1][hh * 64:(hh + 1) * 64, 0:512].rearrange("p (a b) -> p a b", a=4))
                    B.tt(RKw[:, c, :].rearrange("p (h d) -> p h d", h=8), RF[:].rearrange("p (h d) -> p h d", h=8),
                         AV(WEND[:], 0, [[1, 8], [0, 64]]), ALU.mult)
                elif j in (2, 3):
                    B.cp(Vb[:, c, (j - 2) * 512:(j - 1) * 512], ps[:], eng="act")
                elif j in (4, 5):
                    B.act(SRG[:, c, (j - 4) * 512:(j - 3) * 512], ps[:], AF.Silu)
                elif j == 6:
                    rope_block(ps[:], 8, tab, out_ap_bf=AV(RQb[:], 0, [[64, 2], [128, 4], [1, 64]]), perm=True)
                    transposes_to(SQT[:, c, :].rearrange("p (a b) -> p a b", a=4),
                                  lambda k: RQb[:, k * 128:(k + 1) * 128], 4, PT[1])
                else:
                    rope_block(ps[:, 0:128], 2, tab, out_ap_f32=SKf[:, 0:128], out_ap_bf=RQb[:, 0:128])
                    B.tr(PT[1][:, 0:128], RQb[:, 0:128])
                    for g in range(2):
                        B.cp(SKm[g * 64:(g + 1) * 64, 1 + c, g, :], PT[1][g * 64:(g + 1) * 64, 0:128])
                    B.cp(SKf[:, 128:256], ps[:, 128:256], eng="act")
                    for g in range(2):
                        for hh in range(2):
                            B.cp(SVp[:, 1 + c, g, hh, hh * 64:(hh + 1) * 64], SKf[:, 128 + g * 64:128 + (g + 1) * 64])
                    if last_chunk_out is not None and c == nch - 1:
                        last_chunk_out()
        if mid_hook is not None:
            mid_hook()
        for jj in range(7):
            W = w_next()
            for t in range(4):
                ti = jj * 4 + t
                ps = PB[ti % 3]
                for kt in range(8):
                    B.mm(ps[:, 0:NTk], W[:, kt, t * 128:(t + 1) * 128], XT[:, kt, 0:NTk], kt == 0, kt == 7)
                if ti < 4:
                    B.cp(M["MQT"][:, ti, 0:NTk], ps[:, 0:NTk])
                else:
                    B.act(M["GT"][:, ti - 4, 0:NTk], ps[:, 0:NTk], AF.Sigmoid)

    def swa_pv_norm(l, c, has_prev, slot_prev, slot_own):
        for pr in range(4):
            g = pr // 2
            for which in range(2):
                dst = (PB[5], PB[0])[which][:, pr * 128:(pr + 1) * 128]
                first = True
                for hh in range(2):
                    t = 2 * (pr % 2) + hh
                    for blk in range(2):
                        if blk == 0 and not has_prev:
                            continue
                        slot = slot_prev if blk == 0 else slot_own
                        lhs = SVp[:, slot, g, hh, :] if which == 0 else ONp[:, hh, :]
                        B.mm(dst, lhs, E[:, g * 2 + blk, t * 128:(t + 1) * 128], first, hh == 1 and blk == 1)
                        first = False
        for pr in range(4):
            B.ts(DEN[:, pr * 128:(pr + 1) * 128], PB[0][:, pr * 128:(pr + 1) * 128], SINKE[:, l, pr:pr + 1], ALU.add)
        B.recip(DEN[:], DEN[:])
        B.tt(SWOT[:, :, c * 128:(c + 1) * 128], PB[5][:].rearrange("p (a i) -> p a i", a=4),
             DEN[:].rearrange("p (a i) -> p a i", a=4), ALU.mult)

    def branches_out_mlp(l, nch, mid_hook=None, tail_hook=None):
        NTk = nch * 128
        for cb in range(2):
            Wret = w_next()
            Wsm = w_next()
            for o4 in range(4):
                ot = cb * 4 + o4
                bk = (ot % 2) * 3
                pr_, psw, pme = PB[bk], PB[bk + 1], PB[bk + 2]
                for kt in range(8):
                    B.mm(pr_[:, 0:NTk], Wret[:, kt, o4 * 128:(o4 + 1) * 128], GRT[:, kt, 0:NTk], kt == 0, kt == 7)
                for kt in range(4):
                    B.mm(psw[:, 0:NTk], Wsm[:, kt, o4 * 128:(o4 + 1) * 128], SWOT[:, kt, 0:NTk], kt == 0, kt == 3)
                for kt in range(4):
                    B.mm(pme[:, 0:NTk], Wsm[:, 4 + kt, o4 * 128:(o4 + 1) * 128], MOT[:, kt, 0:NTk], kt == 0, kt == 3)
                B.tt(T1[:, 0:NTk], pr_[:, 0:NTk], M["GT"][:, ot, 0:NTk], ALU.mult)
                B.tt(T2[:, 0:NTk], psw[:, 0:NTk], M["GT"][:, 8 + ot, 0:NTk], ALU.mult)
                B.tt(RF[:, 0:NTk], pme[:, 0:NTk], M["GT"][:, 16 + ot, 0:NTk], ALU.mult)
                B.tt(T1[:, 0:NTk], T1[:, 0:NTk], T2[:, 0:NTk], ALU.add)
                B.tt(MT[:, ot, 0:NTk], T1[:, 0:NTk], RF[:, 0:NTk], ALU.add)
        for cb in range(2):
            Wo = w_next()
            for c in range(nch):
                ps = PB[(cb * nch + c) % 6]
                for kt in range(8):
                    B.mm(ps[:], MT[:, kt, c * 128:(c + 1) * 128], Wo[:, kt, :], kt == 0, kt == 7)
                xs = X[:, c, cb * 512:(cb + 1) * 512]
                B.stt(xs, xs, ALPHA, ps[:], ALU.mult, ALU.add)
        for c in range(nch):
            layer_norm_chunk(X[:, c, :], 0)
        B.dma(LNB[:, 0, :], bcast_row(ln2_g[l], D), "prm")
        B.dma(LNB[:, 1, :], bcast_row(ln2_b[l], D), "prm")
        xT_phase(nch)
        for jb in range(8):
            W = w_next()
            for t in range(4):
                ft = jb * 4 + t
                ps = PB[ft % 6]
                R = (T1, T2)[ft % 2]
                for kt in range(8):
                    B.mm(ps[:, 0:NTk], W[:, kt, t * 128:(t + 1) * 128], XT[:, kt, 0:NTk], kt == 0, kt == 7)
                B.act(R[:, 0:NTk], ps[:, 0:NTk], AF.Relu)
                B.tt(M["HT"][:, ft, 0:NTk], R[:, 0:NTk], R[:, 0:NTk], ALU.mult)
        if mid_hook is not None:
            mid_hook()
        for cb in range(2):
            for rb in range(4):
                W = w_next()
                for c in range(nch):
                    for k8 in range(8):
                        B.mm(PB[c][:], M["HT"][:, rb * 8 + k8, c * 128:(c + 1) * 128], W[:, k8, :],
                             rb == 0 and k8 == 0, rb == 3 and k8 == 7)
            if cb == 1 and tail_hook is not None:
                tail_hook()
            for c in range(nch):
                xs = X[:, c, cb * 512:(cb + 1) * 512]
                B.stt(xs, xs, ALPHA, PB[c][:], ALU.mult, ALU.add)
        for c in range(nch):
            layer_norm_chunk(X[:, c, :], 0)

    def prefetch_input_bf16(t):
        for c in range(4):
            for half in range(2):
                dst = XbN[:, c, half * 512:(half + 1) * 512]
                if t < nsc:
                    B.dma(RF[:], x_p[t * 512 + c * 128:t * 512 + (c + 1) * 128, half * 512:(half + 1) * 512], None)
                if t >= 2:
                    B.dma(XbS, cc_dst[t % 2][c * 128:(c + 1) * 128, half * 512:(half + 1) * 512], None)
                if t < 2:
                    B.ts(dst, RF[:], SEL[:, 0:1], ALU.mult)
                elif t < nsc:
                    B.ts(RF[:], RF[:], SEL[:, 0:1], ALU.mult)
                    B.stt(dst, XbS, SEL[:, 1:2], RF[:], ALU.mult, ALU.add)
                else:
                    B.ts(dst, XbS, SEL[:, 1:2], ALU.mult)

    def xt_from_prefetch():
        for c in range(4):
            transposes_to(XT[:, :, c * 128:(c + 1) * 128], lambda k: XbN[:, c, k * 128:(k + 1) * 128], 8, PT[c % 2],
                          evac_eng="act")

    def load_x_fp32(sc):
        if sc < nsc:
            B.dma(X[:], x_p[sc * 512:(sc + 1) * 512, :].rearrange("(c p) d -> p c d", p=128), "x")
            Xf = X[:].rearrange("p c d -> p (c d)")
            B.ts(Xf, Xf, SEL[:, 0:1], ALU.mult)
        else:
            B.memset(X[:], 0.0)
        if sc >= 2:
            gsrc = cc_dst[sc % 2]
            for c in range(4):
                for half in range(2):
                    stg = (T1, T2)[(c * 2 + half) % 2]
                    B.dma(stg[:], gsrc[c * 128:(c + 1) * 128, half * 512:(half + 1) * 512], None)
                    xs = X[:, c, half * 512:(half + 1) * 512]
                    B.stt(xs, stg[:], SEL[:, 1:2], xs, ALU.mult, ALU.add)

    def prompt_pass(sc, l):
        chunk0 = sc * 4
        B.dma(ROPE[:, 0, :, :], r_cos[:, chunk0:chunk0 + 4, :], "rope")
        B.dma(ROPE[:, 1, :, :], r_sin[:, chunk0:chunk0 + 4, :], "rope")
        B.dma(ROPE[:, 2, :, :], r_nsin[:, chunk0:chunk0 + 4, :], "rope")
        load_layer_params(l)
        if sc == 0:
            mem_kv(l)
        if sc == 0:
            prefetch_input_bf16(0)
            xt_from_prefetch()
        B.cp(SKm[:, 0], SKTprev[l][:])
        B.cp(SVp[:, 0], SVprev[l][:])
        ver = {nsc - 1: 0, nsc + 1: 1}.get(sc, None)
        last = ver is not None

        def last_out():
            B.dma(sk_p[ver], SKf[:, 0:128], "o_sk", final=True)
            B.dma(sv_p[ver], SKf[:, 128:256], "o_sk", final=True)
        in_proj(l, 4, False, last_out if last else None, mid_hook=lambda: load_x_fp32(sc))
        S, Sb_ = Sst[l], Sbf[l]
        for c in range(4):
            cs = slice(c * 128, (c + 1) * 128)
            for h in range(8):
                pr, hh = h // 2, h % 2
                ps = PB[3 + h // 4]
                B.mm(ps[:, (h % 4) * 128:(h % 4 + 1) * 128], RKm[:, c, h, :], M["RQT"][:, pr, cs], True, True)
            for half in range(2):
                B.tt(ST[:, half * 4:(half + 1) * 4, :], PB[3 + half][:].rearrange("p (h i) -> p h i", h=4),
                     DEC[:, half * 4:(half + 1) * 4, :], ALU.mult)
            for hh in range(2):
                B.tt(AV(RQs2[:], hh * 128, [[256, 4], [1, 128]], p0=hh * 64, pn=64),
                     M["RQT"][hh * 64:(hh + 1) * 64, :, cs], GQ[hh * 64:(hh + 1) * 64, :, :], ALU.mult)
            obanks = (PB[5], PB[0])
            for h in range(8):
                pr, hh = h // 2, h % 2
                po = obanks[h // 4][:, (h % 4) * 128:(h % 4 + 1) * 128]
                B.mm(po, ST[:, h, :], Vb[:, c, h * 128:(h + 1) * 128], True, False)
                B.mm(po, RQs2[:, h, :], Sb_[:, pr, :], False, True)
            group_norm_gate(obanks, c, 4)
            for pr in range(4):
                ps = PB[1 + pr % 2]
                B.mm(ps[:, 0:256], RKw[:, c, pr * 128:(pr + 1) * 128], Vb[:, c, pr * 256:(pr + 1) * 256], True, True)
                for hh in range(2):
                    sv = S[hh * 64:(hh + 1) * 64, pr, :]
                    B.stt(sv, sv, GL[hh * 64:(hh + 1) * 64, pr:pr + 1],
                          ps[hh * 64:(hh + 1) * 64, hh * 128:(hh + 1) * 128], ALU.mult, ALU.add)
            B.cp(Sb_[:], S[:], eng="act")
            if last and c == 3:
                B.dma(bass.AP(rs_p.tensor, ver * RH * RDK * RDV, [[128, 128], [2 * RDK * RDV, 4], [1, 128]]), S[:],
                      "o_rs", final=True)
            if stage < 6:
                continue
            has_prev = not (sc == 0 and c == 0)
            banks = {(0, 0): PB[1], (0, 1): PB[2], (1, 0): PB[3], (1, 1): PB[4]}
            for g in range(2):
                qv = SQT[:, c, :]
                for blk in range(2):
                    if blk == 0 and not has_prev:
                        continue
                    ps = banks[(g, blk)]
                    B.mm(ps[:], SKm[:, c + blk, g, :], qv, True, False)
                    if blk == 0 and sc == 2 and c == 0:
                        B.mm(ps[:], B.identb[:], FLAGN[:], False, False)
                    B.mm(ps[:], B.identb[:], (MPREV if blk == 0 else MOWN)[:], False, True)
                    B.act(E[:, g * 2 + blk, :], ps[:], AF.Exp, scale=SHD ** -0.5)
            swa_pv_norm(l, c, has_prev, c, c + 1)
            if stage < 7:
                continue
            for blk in range(2):
                ps = PB[1 + blk]
                for h in range(4):
                    B.mm(ps[:, h * 128:(h + 1) * 128], MKT[l][:, h, blk * 128:(blk + 1) * 128], M["MQT"][:, h, cs], True, True)
                B.act(EM[:, blk, :], ps[:], AF.Exp, scale=MHD ** -0.5)
            for h in range(4):
                for blk in range(2):
                    B.mm(PB[3][:, h * 128:(h + 1) * 128], MVb[l][:, blk, h * 128:(h + 1) * 128],
                         EM[:, blk, h * 128:(h + 1) * 128], blk == 0, blk == 1)
                for blk in range(2):
                    B.mm(PB[4][:, h * 128:(h + 1) * 128], onesb[:], EM[:, blk, h * 128:(h + 1) * 128], blk == 0, blk == 1)
            B.recip(DEN[:], PB[4][:])
            B.tt(MOT[:, :, cs], PB[3][:].rearrange("p (a i) -> p a i", a=4),
                 DEN[:].rearrange("p (a i) -> p a i", a=4), ALU.mult)
        B.cp(SKTprev[l][:], SKm[:, 4])
        B.cp(SVprev[l][:], SVp[:, 4])
        if stage < 8:
            return
        nxt = sc + 1 < NSLOT
        branches_out_mlp(l, 4, mid_hook=(lambda: prefetch_input_bf16(sc + 1)) if nxt else None,
                         tail_hook=xt_from_prefetch if nxt else None)
        B.dma(y_p[sc * 512:(sc + 1) * 512, :].rearrange("(c p) d -> p c d", p=128), X[:], "o_y", final=True)
        if sc < nsc:
            k = sc % 2
            B.dma(cc_src[k].rearrange("(c p) d -> p c d", p=128), X[:], "!ccs%d" % k)
            src_ap, dst_ap = cc_src[k], cc_dst[k]
            if not cfg.get("nocc", False):
              B.T.add("pool", lambda: nc.gpsimd.collective_compute(
                "AllGather", ALU.bypass, replica_groups=[[i, i + 4] for i in range(4)], ins=[src_ap], outs=[dst_ap]),
                w=[dst_ap], r=[src_ap], dma_key="!cc%d" % sc, inc=1)

    def load_sample_consts():
        B.dma(DEC[:], cd["decs"], None)
        B.dma(GQ[:], cd["gqs"], None)
        B.dma(WEND[:], cd["wends"], None)
        B.dma(GL[:], cd["gls"], None)
        B.dma(ROPE[:, 0, 0:1, :], cd["coss"], None)
        B.dma(ROPE[:, 1, 0:1, :], cd["sins"], None)
        B.dma(ROPE[:, 2, 0:1, :], cd["nsins"], None)
        B.dma(BMR[:], cd["bmrow"], None)
        B.dma(identF[:], cd["ident"], None)
        B.dma(T1[:], cd["mnews"], None)
        B.cp(MOWN[:], T1[:])
        B.dma(T1[:], cd["mcache"], None)
        B.cp(MPREV[:], T1[:])

    def reload_prompt_masks():
        B.dma(T1[:], cd["mown"], None)
        B.cp(MOWN[:], T1[:])
        B.dma(T1[:], cd["mprev"], None)
        B.cp(MPREV[:], T1[:])

    def sample_pass(l):
        NB = SB_PER_CORE
        load_layer_params(l)
        if l == 0:
            B.dma(X[:, 0, :], x_s, None)
        B.dma(sk_s[l][:, 0:120, :], c_sk[l][:, 8:128, :], "!cck", final=True)
        B.dma(sv_s[l][:, 0:120, :], c_sv[l][:, 8:128, :], "!ccv", final=True)
        xT_phase(1)

        def new_rows_out():
            for b in range(NB):
                B.dma(sk_s[l, b, 120:128, :], SKf[8 * b:8 * b + 8, 0:128], None, final=True)
                B.dma(sv_s[l, b, 120:128, :], SKf[8 * b:8 * b + 8, 128:256], None, final=True)
        in_proj(l, 1, True, new_rows_out)
        RQT_, MQT_ = M["RQT"], M["MQT"]
        if sstage < 2:
            return
        cs = slice(0, 128)
        for h in range(8):
            pr = h // 2
            ps = PB[3 + h // 4]
            B.mm(ps[:, (h % 4) * 128:(h % 4 + 1) * 128], RKm[:, 0, h, :], RQT_[:, pr, cs], True, True)
        for half in range(2):
            B.tt(ST[:, half * 4:(half + 1) * 4, :], PB[3 + half][:].rearrange("p (h i) -> p h i", h=4),
                 DEC[:, half * 4:(half + 1) * 4, :], ALU.mult)
        for hh in range(2):
            B.tt(AV(RQs2[:], hh * 128, [[256, 4], [1, 128]], p0=hh * 64, pn=64),
                 RQT_[hh * 64:(hh + 1) * 64, :, cs], GQ[hh * 64:(hh + 1) * 64, :, :], ALU.mult)
        ot = (PB[5], PB[0])
        B.memset(ot[0][:], 0.0)
        B.memset(ot[1][:], 0.0)
        for h in range(8):
            B.mm(ot[h // 4][:, (h % 4) * 128:(h % 4 + 1) * 128], Vb[:, 0, h * 128:(h + 1) * 128], ST[:, h, :],
                 False, False, skip=True)
        for grp in range(4):
            b0 = grp * 4
            st_src = bass.AP(st_ret.tensor, (l * NB + b0) * RH * RDK * RDV,
                             [[128, 128], [RH * RDK * RDV, 4], [2 * RDK * RDV, 4], [1, 128]])
            B.dma(S0g, st_src, None)
            B.dma(S0bg, st_src, None, q="pool")
            if grp % 2 == 0:
                rnd = grp // 2
                B.tt(RKwX, AV(RKw[:], 0, [[0, 8], [1, 512]]),
                     AV(BMR[:], rnd * 8, [[1, 8], [0, 512]]), ALU.mult)
            for bl in range(4):
                b = b0 + bl
                for h in range(8):
                    pr = h // 2
                    B.mm(ot[h // 4][:, (h % 4) * 128 + 8 * b:(h % 4) * 128 + 8 * b + 8], S0bg[:, bl, pr, :],
                         RQs2[:, h, 8 * b:8 * b + 8], False, False, skip=True)
                for pr in range(4):
                    ps = PB[1 + pr // 2]
                    B.mm(ps[:, (pr % 2) * 256:(pr % 2 + 1) * 256], RKwX[:, b % 8, pr * 128:(pr + 1) * 128],
                         Vb[:, 0, pr * 256:(pr + 1) * 256], True, True)
                for pr in range(4):
                    ps = PB[1 + pr // 2]
                    for hh in range(2):
                        sv = S0g[hh * 64:(hh + 1) * 64, bl, pr, :]
                        c0 = (pr % 2) * 256 + hh * 128
                        B.stt(sv, sv, GL[hh * 64:(hh + 1) * 64, pr:pr + 1],
                              ps[hh * 64:(hh + 1) * 64, c0:c0 + 128], ALU.mult, ALU.add)
            B.dma(bass.AP(rs_s.tensor, (l * NB + b0) * RH * RDK * RDV,
                          [[128, 128], [RH * RDK * RDV, 4], [2 * RDK * RDV, 4], [1, 128]]), S0g, None, final=True)
        for half in range(2):
            B.cp(Y[:, half * 4:(half + 1) * 4, :], ot[half][:].rearrange("p (h i) -> p h i", h=4), eng="act")
        for h in range(8):
            B.trf(PB[3 + h // 4][:, (h % 4) * 128:(h % 4 + 1) * 128], Y[:, h, :], identF[:])
        group_norm_gate((PB[3], PB[4]), 0, 1)
        if sstage < 3:
            return
        for g in range(2):
            ps = PB[1 + g]
            B.mm(ps[:], SKm[:, 1, g, :], SQT[:, 0, :], True, False)
            B.mm(ps[:], B.identb[:], MOWN[:], False, True)
            B.act(E[:, g * 2 + 1, :], ps[:], AF.Exp, scale=SHD ** -0.5)
        B.memset(SQTs, 0.0)
        for g in range(2):
            B.cp(AV(SQTs, g * 512, [[32, 16], [8, 4], [1, 8]], p0=g * 64, pn=64),
                 AV(SQT[:], 0, [[8, 16], [128, 4], [1, 8]], p0=g * 64, pn=64))
        B.dma(Kraw, c_sk[l].rearrange("b k c -> k b c"), None, q="pool")
        for half in range(2):
            for k in range(8):
                B.tr(PT[half][:, k * 128:(k + 1) * 128], Kraw[:, half * 8 + k, :])
            B.cp(KcT[:, half * 8:(half + 1) * 8, :], PT[half][:].rearrange("p (a b) -> p a b", a=8))
        for half in range(2):
            ps = PB[3 + half]
            B.memset(ps[:], 0.0)
            for b8 in range(8):
                b = half * 8 + b8
                for g in range(2):
                    B.mm(ps[:, (b8 * 2 + g) * 32:(b8 * 2 + g + 1) * 32], KcT[:, b, :],
                         SQTs[:, g, b * 32:(b + 1) * 32], False, False, skip=True)
            B.mm(ps[:], B.identb[:], MPREV[:], False, False, skip=True)
            B.act(Ec[:, half * 8:(half + 1) * 8, :, :].rearrange("p b g c -> p (b g c)"), ps[:], AF.Exp,
                  scale=SHD ** -0.5)
        B.dma(Kraw, c_sv[l].rearrange("b k c -> k b c"), None, q="pool")
        num, den = PB[5], PB[0]
        B.memset(num[:], 0.0)
        B.memset(den[:], 0.0)
        for pr in range(4):
            g = pr // 2
            for hh in range(2):
                t = 2 * (pr % 2) + hh
                B.mm(num[:, pr * 128:(pr + 1) * 128], SVp[:, 1, g, hh, :], E[:, g * 2 + 1, t * 128:(t + 1) * 128],
                     False, False, skip=True)
                B.mm(den[:, pr * 128:(pr + 1) * 128], ONp[:, hh, :], E[:, g * 2 + 1, t * 128:(t + 1) * 128],
                     False, False, skip=True)
        for grp in range(4):
            b0 = grp * 4
            B.memset(Vcp, 0.0)
            for hh in range(2):
                B.cp(AV(Vcp, hh * 128 + hh * 64, [[512, 4], [256, 2], [1, 64]]),
                     AV(Kraw, b0 * 128, [[128, 4], [64, 2], [1, 64]]))
            for bl in range(4):
                b = b0 + bl
                for pr in range(4):
                    g = pr // 2
                    for hh in range(2):
                        t = 2 * (pr % 2) + hh
                        rhs = Ec[:, b, g, t * 8:(t + 1) * 8]
                        B.mm(num[:, pr * 128 + 8 * b:pr * 128 + 8 * b + 8], Vcp[:, bl, g, hh, :], rhs,
                             False, False, skip=True)
                        B.mm(den[:, pr * 128 + 8 * b:pr * 128 + 8 * b + 8], ONp[:, hh, :], rhs,
                             False, False, skip=True)
        for pr in range(4):
            B.ts(DEN[:, pr * 128:(pr + 1) * 128], den[:, pr * 128:(pr + 1) * 128], SINKE[:, l, pr:pr + 1], ALU.add)
        B.recip(DEN[:], DEN[:])
        B.tt(SWOT[:, :, cs], num[:].rearrange("p (a i) -> p a i", a=4),
             DEN[:].rearrange("p (a i) -> p a i", a=4), ALU.mult)
        if sstage < 4:
            return
        Kmraw = Vb[:, 1:3, :].rearrange("p a (k c) -> p (a k) c", k=2)
        KmT = SRG[:, 1:3, :].rearrange("p a (h m) -> p a h m", h=4)
        Vm = RKm[:, 1:3, :, :].rearrange("p a h c -> p (a h c)").rearrange("p (k c) -> p k c", k=4)
        Em = RKw[:, 1, 0:128].rearrange("p (b k h i) -> p b k h i", b=2, k=2, h=4)
        mnum, mden = PB[3], PB[4]
        B.memset(mnum[:], 0.0)
        B.memset(mden[:], 0.0)
        for grp in range(NB // 2):
            b0 = grp * 2
            B.dma(Kmraw, c_mk[l][b0:b0 + 2].rearrange("b (k m) c -> m (b k) c", k=2), None, q="pool")
            B.dma(Vm, c_mv[l][b0:b0 + 2].rearrange("b (k m) c -> m (b k) c", k=2), None, q="pool")
            sc_ps = PB[1 + grp % 2]
            B.memset(sc_ps[:, 0:128], 0.0)
            for bl in range(2):
                b = b0 + bl
                for blk in range(2):
                    for h in range(4):
                        B.tr(PT[bl][:, (h * 2 + blk) * 128:(h * 2 + blk + 1) * 128],
                             Kmraw[:, bl * 2 + blk, h * 128:(h + 1) * 128])
                B.cp(KmT[:, bl, :, :].rearrange("p h m -> p (h m)"), PT[bl][:])
                for blk in range(2):
                    for h in range(4):
                        c0 = ((bl * 2 + blk) * 4 + h) * 8
                        B.mm(sc_ps[:, c0:c0 + 8], KmT[:, bl, h, blk * 128:(blk + 1) * 128],
                             MQT_[:, h, 8 * b:8 * b + 8], False, False, skip=True)
            B.act(Em.rearrange("p b k h i -> p (b k h i)"), sc_ps[:, 0:128], AF.Exp, scale=MHD ** -0.5)
            for bl in range(2):
                b = b0 + bl
                for blk in range(2):
                    for h in range(4):
                        rhs = Em[:, bl, blk, h, :]
                        B.mm(mnum[:, h * 128 + 8 * b:h * 128 + 8 * b + 8], Vm[:, bl * 2 + blk, h * 128:(h + 1) * 128],
                             rhs, False, False, skip=True)
                        B.mm(mden[:, h * 128 + 8 * b:h * 128 + 8 * b + 8], onesb[:], rhs, False, False, skip=True)
        B.recip(DEN[:], mden[:])
        B.tt(MOT[:, :, cs], mnum[:].rearrange("p (a i) -> p a i", a=4),
             DEN[:].rearrange("p (a i) -> p a i", a=4), ALU.mult)
        if sstage < 5:
            return
        branches_out_mlp(l, 1)
        if l == nlay - 1:
            B.dma(y_s, X[:, 0, :], None, final=True)

    passes = []
    if do_sample:
        for l in range(nlay):
            passes.append(("s", 0, l))
    for sc in range(NSLOT):
        passes.append(("p", sc, 2))
    for kind, sc, l in passes:
        wstate["plan"] += wplan_for_pass(l, kind == "p" and sc == 0)
    if do_sample:
        load_sample_consts()
    else:
        load_prompt_consts()
    for pi, (kind, sc, l) in enumerate(passes):
        T.phase = pi
        if kind == "s":
            M.update(SV_)
            sample_pass(l)
            if l == nlay - 1:
                M.update(PV_)
                B.memset(RKm[:], 0.0)
                load_prompt_consts()
                reload_prompt_masks()
        elif not cfg.get("noprompt", False):
            prompt_pass(sc, l)
    if stage >= 99 and sstage >= 99 and not cfg.get("noprompt", False):
        assert wstate["idx"] == len(wstate["plan"]), (wstate["idx"], len(wstate["plan"]))
    if dbgname is not None:
        dbg = B.dout("dbg", [128, 4096])
        DBG = X[:].rearrange("p a b -> p (a b)")
        srcs = dict(XT=XT[:], RQT=RQT, RKm=RKm[:, 0], SQT=SQT[:], SKm=SKm[:], Vb=Vb[:, :, :], SRG=SRG[:], MQT=MQT, GT=GT[:, 0:8, :],
                    GRT=GRT[:], SWOT=SWOT[:], MOT=MOT[:], MT=MT, X=X[:], RKw=RKw[:], MKT=MKT[0][:], MVb=MVb[0][:],
                    MEMT=MEMT, HT=HT[:, 0:8, :], S=Sst[0][:], E=E[:], EM=EM[:])[dbgname]
        n = 1
        for s_ in srcs.shape[1:]:
            n *= s_
        dims = "abcd"[:len(srcs.shape) - 1]
        flat = srcs.rearrange("p %s -> p (%s)" % (" ".join(dims), " ".join(dims))) if len(dims) > 1 else srcs
        if dbgname != "X":
            B.cp(DBG[:, 0:n], flat)
        B.dma(dbg[:, 0:n], DBG[:, 0:n], "o_dbg", final=True)
    T.emit(B.final_ops)
    return B


_CACHE = {}


def _core_inputs(inp, core, nsc, hc):
    f = lambda a: np.ascontiguousarray(np.asarray(a, dtype=np.float32))
    b = core % BATCH
    L = core // BATCH
    sb0 = core * SB_PER_CORE
    sl = slice(sb0, sb0 + SB_PER_CORE)
    nslot = nsc + 2
    m = {
        "x_p": f(inp["x_prompt"][b, :nsc * 512]),
        "mem_p": f(inp["mem_prompt"][b]),
        "x_s": f(np.asarray(inp["x_sample"])[sl].reshape(SB_PER_CORE * DEC_SEQ, D)),
        "st_ret": f(np.asarray(inp["state_ret"])[:, sl]),
        "c_sk": f(np.asarray(inp["cache_swa_k"])[:, sl].reshape(DEPTH, SB_PER_CORE, 128, 128)),
        "c_sv": f(np.asarray(inp["cache_swa_v"])[:, sl].reshape(DEPTH, SB_PER_CORE, 128, 128)),
        "c_mk": f(np.asarray(inp["cache_mem_k"])[:, sl].reshape(DEPTH, SB_PER_CORE, NMEM, 512)),
        "c_mv": f(np.asarray(inp["cache_mem_v"])[:, sl].reshape(DEPTH, SB_PER_CORE, NMEM, 512)),
        "sinks": f(inp["attn_sinks"]),
        "gn_g": f(np.asarray(inp["ret_gn_g"]).reshape(DEPTH, 1024)),
    }
    chunk_of = []
    for s in range(nslot):
        for c in range(4):
            ch = (4 * s + c) if L == 0 else (4 * (s - 2) + c)
            if ch < 0 or ch >= nsc * 4:
                ch = 0
            chunk_of.append(ch)
    for nm, key in (("r_cos", "cosp"), ("r_sin", "sinp"), ("r_nsin", "nsinp")):
        m[nm] = np.ascontiguousarray(hc[key][:, chunk_of, :])
    sel = np.zeros((128, 2), np.float32)
    sel[:, L] = 1.0
    m["sel"] = sel
    m["flagneg"] = np.full((128, 512), NEG if L == 1 else 0.0, np.float32)
    return m


_OWN = (("wo_in", "w_in"), ("wo_br_ret", "w_br_ret"), ("wo_br_swa", "w_br_swa"), ("wo_br_mem", "w_br_mem"),
        ("wo_out", "w_out"), ("wo_mem_kv", "w_mem_kv"), ("wo_up", "w_up"), ("wo_down", "w_down"),
        ("o_ln1_g", "ln1_g"), ("o_ln1_b", "ln1_b"), ("o_ln2_g", "ln2_g"), ("o_ln2_b", "ln2_b"))
_FULL = ("w_in", "w_br_ret", "w_br_swa", "w_br_mem", "w_out", "w_mem_kv", "ln1_g", "ln1_b", "w_up", "w_down",
         "ln2_g", "ln2_b")


def run_cores(inp, cfg, cores):
    key = tuple(sorted(cfg.items()))
    if key not in _CACHE:
        _CACHE[key] = build(cfg)
    B = _CACHE[key]
    hc = host_consts()
    f = lambda a: np.ascontiguousarray(np.asarray(a, dtype=np.float32))
    full = {k: f(inp[k]) for k in _FULL}
    ownl = []
    for L in range(DEPTH):
        d = {dst: np.ascontiguousarray(full[srck][L]) for dst, srck in _OWN}
        d["o_sinks"] = f(inp["attn_sinks"])[L:L + 1]
        d["o_gn_g"] = f(np.asarray(inp["ret_gn_g"]).reshape(DEPTH, 1024))[L]
        ownl.append(d)
    in_maps = []
    for core in cores:
        m = _core_inputs(inp, core, cfg["nsc"], hc)
        m.update(full)
        m.update(ownl[core // BATCH])
        for k, v in hc.items():
            m["c_" + k] = v
        in_maps.append(m)
    res = run_bass_kernel_spmd(B.nc, in_maps, core_ids=list(range(len(cores))))
    return res.results


def assemble(r, nsc):
    S = nsc * 512
    y_p = np.stack([r[BATCH + b]["y_p"][2 * 512:2 * 512 + S] for b in range(BATCH)]).astype(np.float32)
    y_s = np.concatenate([r[c]["y_s"].reshape(SB_PER_CORE, DEC_SEQ, D) for c in range(NCORES)]).astype(np.float32)

    def per_layer(name, shape, versioned):
        out = []
        for L in range(DEPTH):
            row = []
            for b in range(BATCH):
                a = r[L * BATCH + b][name]
                a = a[L] if versioned else a
                row.append(np.asarray(a).reshape(shape))
            out.append(np.stack(row))
        return np.stack(out).astype(np.float32)
    rs_p = per_layer("rs_p", (RH, RDK, RDV), True)
    sk_p = per_layer("sk_p", (128, SKV, SHD), True)
    sv_p = per_layer("sv_p", (128, SKV, SHD), True)
    mk_p = per_layer("mk_p", (NMEM, MH, MHD), False)
    mv_p = per_layer("mv_p", (NMEM, MH, MHD), False)
    rs_s = np.concatenate([r[c]["rs_s"] for c in range(NCORES)], axis=1).astype(np.float32)
    sk_s = np.concatenate([r[c]["sk_s"].reshape(DEPTH, SB_PER_CORE, 128, SKV, SHD) for c in range(NCORES)],
                          axis=1).astype(np.float32)
    sv_s = np.concatenate([r[c]["sv_s"].reshape(DEPTH, SB_PER_CORE, 128, SKV, SHD) for c in range(NCORES)],
                          axis=1).astype(np.float32)
    return (y_p, y_s, rs_p, sk_p, sv_p, mk_p, mv_p, rs_s, sk_s, sv_s)


def kernel(**inp):
    cfg = dict(nsc=SEQ // 512, nlay=DEPTH, sample=True)
    r = run_cores(inp, cfg, list(range(NCORES)))
    return assemble(r, cfg["nsc"])
```

```python
import numpy as np
import concourse.bass as bass
import concourse.mybir as mybir
from concourse.bass_utils import run_bass_kernel_spmd

F32 = mybir.dt.float32
BF16 = mybir.dt.bfloat16
AF = mybir.ActivationFunctionType
ALU = mybir.AluOpType

D = 1024
DEPTH = 2
SEQ = 4096
BATCH = 4
DEC_BATCH = 128
DEC_SEQ = 8
PAST_LEN = 16384
RH, RDK, RDV = 8, 64, 128
SH, SKV, SHD = 8, 2, 64
MH, MHD, NMEM = 4, 128, 256
DFF = 4096
IN_W = 7424
ALPHA = (2 * DEPTH) ** 0.25
LN_EPS = 1e-5
GN_EPS = 1e-5
NEG = -2000.0
NCORES = 8
SB_PER_CORE = DEC_BATCH // NCORES


def _dsize(dt):
    return mybir.dt.size(dt)


class Tracker:
    def __init__(self, nc):
        self.nc = nc
        self.ops = []
        self.psum_last = {}
        self.hist = {}
        self.phase = 0
        self.eng_objs = {"pe": nc.tensor, "dve": nc.vector, "act": nc.scalar,
                         "pool": nc.gpsimd, "sp": nc.sync}

    @staticmethod
    def box(ap):
        t = ap.tensor
        name = t.name
        dims = ap.ap
        esz = _dsize(ap.dtype)
        space = str(ap.space)
        if "DRAM" in space.upper() or "HBM" in space.upper() or "dram" in space:
            lo = ap.offset
            hi = lo + sum((c - 1) * s for s, c in dims if c > 0)
            return name, (0, 0, lo * esz, (hi + 1) * esz), True
        fstride = dims[0][0]
        p0 = ap.offset // fstride
        f0 = ap.offset % fstride
        p1 = p0 + dims[0][1] - 1
        f1 = f0 + sum((c - 1) * s for s, c in dims[1:])
        return name, (p0, p1, f0 * esz, (f1 + 1) * esz), False

    @staticmethod
    def overlap(a, b):
        return not (a[1] < b[0] or b[1] < a[0] or a[3] <= b[2] or b[3] <= a[2])

    @staticmethod
    def contains(a, b):
        return a[0] <= b[0] and a[1] >= b[1] and a[2] <= b[2] and a[3] >= b[3]

    def add(self, eng, fn, w=(), r=(), dma_key=None, untracked=(), inc=None):
        opid = len(self.ops)
        deps = set()
        is_dma = dma_key is not None
        rboxes = [self.box(a) for a in r]
        wboxes = [self.box(a) for a in w]
        psum_names = set()
        for name, bx, isdram in rboxes + wboxes:
            if name.startswith("PB") or name.startswith("PT"):
                psum_names.add(name)
        for name in psum_names:
            last = self.psum_last.setdefault(name, {})
            for e2, o2 in last.items():
                if e2 != eng:
                    deps.add(o2)
            last[eng] = opid
        rboxes = [b for b in rboxes if b[0] not in psum_names]
        wboxes = [b for b in wboxes if b[0] not in psum_names]
        for name, bx, isdram in rboxes:
            if name in untracked:
                continue
            h = self.hist.setdefault(name, [])
            for (b2, o2, isw, e2, d2) in h:
                if isw and self.overlap(bx, b2):
                    deps.add(o2)
        for name, bx, isdram in wboxes:
            if name in untracked:
                continue
            h = self.hist.setdefault(name, [])
            for (b2, o2, isw, e2, d2) in h:
                if self.overlap(bx, b2):
                    deps.add(o2)
        for name, bx, isdram in rboxes:
            if name in untracked:
                continue
            h = self.hist[name]
            if not is_dma:
                h[:] = [e for e in h if not ((not e[2]) and e[3] == eng and (not e[4])
                                             and self.contains(bx, e[0]))]
            h.append((bx, opid, False, eng, is_dma))
        for name, bx, isdram in wboxes:
            if name in untracked:
                continue
            h = self.hist[name]
            h[:] = [e for e in h if not self.contains(bx, e[0])]
            h.append((bx, opid, True, eng, is_dma))
        deps.discard(opid)
        self.ops.append(dict(eng=eng, fn=fn, deps=deps, dma_key=dma_key, phase=self.phase,
                             signal=False, inc_override=inc))
        return opid

    def emit(self, final_wait_ops):
        nc = self.nc
        ops = self.ops
        for o in ops:
            if o["eng"] == "pe" and o["dma_key"] is None:
                o["deps"] = {d for d in o["deps"]
                             if not (ops[d]["eng"] == "pe" and ops[d]["dma_key"] is None)}
        for o in ops:
            for d in o["deps"]:
                ops[d]["signal"] = True
        for d in final_wait_ops:
            ops[d]["signal"] = True
        for o in ops:
            if o["dma_key"] is not None:
                o["signal"] = True
        sem_keys = []
        counts = {}
        for o in ops:
            if not o["signal"]:
                continue
            if o["dma_key"] is not None:
                k = ("dma", o["dma_key"])
                inc = 16 if o["inc_override"] is None else o["inc_override"]
            else:
                k = (o["eng"], o["phase"])
                inc = 1
            if k not in counts:
                counts[k] = 0
                sem_keys.append(k)
            counts[k] += inc
            o["sem"] = k
            o["val"] = counts[k]
            o["inc"] = inc
        rng = bass.get_kernel_semaphore_range()
        assert len(sem_keys) <= len(rng) - 2, f"too many semaphores: {len(sem_keys)}"
        sems = {}
        self._sem_cms = []
        for k in sem_keys:
            cm = nc.semaphore(("s_%s_%s" % k).replace("-", "_"))
            s = cm.__enter__()
            self._sem_cms.append(cm)
            sems[k] = s
        waited = {}
        nwait = 0
        issued = {}
        for o in ops:
            eobj = self.eng_objs[o["eng"]]
            need = {}
            for d in o["deps"]:
                od = ops[d]
                k = od["sem"]
                if od["dma_key"] is not None:
                    need[k] = max(need.get(k, 0), issued[k])
                else:
                    need[k] = max(need.get(k, 0), od["val"])
            for k, v in need.items():
                wk = (o["eng"], k)
                if waited.get(wk, 0) >= v:
                    continue
                eobj.wait_ge(sems[k], v)
                waited[wk] = v
                nwait += 1
            ins = o["fn"]()
            if o["signal"]:
                ins.then_inc(sems[o["sem"]], o["inc"])
                if o["dma_key"] is not None:
                    issued[o["sem"]] = o["val"]
        need = {}
        for d in final_wait_ops:
            od = ops[d]
            need[od["sem"]] = max(need.get(od["sem"], 0), od["val"])
        for k, v in need.items():
            nc.sync.wait_ge(sems[k], v)
        self.stats = dict(nops=len(ops), nwait=nwait, nsem=len(sem_keys),
                          maxcount=max(counts.values()) if counts else 0)

    def close(self):
        for cm in reversed(self._sem_cms):
            cm.__exit__(None, None, None)


def ret_gammas():
    return (1.0 - np.exp2(-5.0 - np.arange(RH, dtype=np.float64)))


def host_consts():
    c = {}
    c["ident"] = np.eye(128, dtype=np.float32)
    half = 32
    inv = np.power(np.float32(10000.0), -np.arange(half, dtype=np.float32) / np.float32(half)).astype(np.float32)
    pos = np.arange(SEQ, dtype=np.float32)
    ang = (pos[:, None] * inv[None, :]).astype(np.float32)
    cosp = np.cos(ang).astype(np.float32).reshape(SEQ // 128, 128, half).transpose(1, 0, 2)
    sinp = np.sin(ang).astype(np.float32).reshape(SEQ // 128, 128, half).transpose(1, 0, 2)
    c["cosp"] = np.ascontiguousarray(cosp)
    c["sinp"] = np.ascontiguousarray(sinp)
    c["nsinp"] = np.ascontiguousarray(-sinp)
    poss = (PAST_LEN + (np.arange(128) % DEC_SEQ)).astype(np.float32)
    angs = (poss[:, None] * inv[None, :]).astype(np.float32)
    c["coss"] = np.cos(angs).astype(np.float32).reshape(128, 1, half)
    c["sins"] = np.sin(angs).astype(np.float32).reshape(128, 1, half)
    c["nsins"] = (-c["sins"]).astype(np.float32)
    g = ret_gammas()
    lg = np.log(g)
    j = np.arange(128)[:, None]
    i = np.arange(128)[None, :]
    dec = np.zeros((128, RH, 128), np.float64)
    decs = np.zeros((128, RH, 128), np.float64)
    for h in range(RH):
        dec[:, h, :] = np.where(i >= j, np.exp(lg[h] * np.maximum(i - j, 0)), 0.0) * RDK ** -0.5
        same = (i // DEC_SEQ) == (j // DEC_SEQ)
        decs[:, h, :] = np.where((i >= j) & same, np.exp(lg[h] * np.maximum(i - j, 0)), 0.0) * RDK ** -0.5
    c["decp"] = dec.astype(np.float32)
    c["decs"] = decs.astype(np.float32)
    gq = np.zeros((128, 4, 128), np.float64)
    gqs = np.zeros((128, 4, 128), np.float64)
    gl = np.zeros((128, 4), np.float64)
    gls = np.zeros((128, 4), np.float64)
    ii = np.arange(128)
    for pr in range(4):
        for hh in range(2):
            h = 2 * pr + hh
            gq[hh * 64:(hh + 1) * 64, pr, :] = np.exp(lg[h] * (ii + 1.0))[None, :]
            gqs[hh * 64:(hh + 1) * 64, pr, :] = np.exp(lg[h] * ((ii % DEC_SEQ) + 1.0))[None, :]
            gl[hh * 64:(hh + 1) * 64, pr] = np.exp(lg[h] * 128.0)
            gls[hh * 64:(hh + 1) * 64, pr] = np.exp(lg[h] * float(DEC_SEQ))
    c["gqp"] = gq.astype(np.float32)
    c["gqs"] = gqs.astype(np.float32)
    c["glp"] = gl.astype(np.float32)
    c["gls"] = gls.astype(np.float32)
    wend = np.exp(lg[None, :] * (127.0 - ii[:, None])) * RDK ** -0.5
    wends = np.exp(lg[None, :] * (DEC_SEQ - 1.0 - (ii[:, None] % DEC_SEQ))) * RDK ** -0.5
    c["wendp"] = wend.astype(np.float32)
    c["wends"] = wends.astype(np.float32)
    own = np.where(j <= i, 0.0, NEG)
    prev = np.where(j > i, 0.0, NEG)
    c["mown"] = np.tile(own, (1, 4)).astype(np.float32)
    c["mprev"] = np.tile(prev, (1, 4)).astype(np.float32)
    same = (i // DEC_SEQ) == (j // DEC_SEQ)
    news = np.where(same & (j <= i), 0.0, NEG)
    c["mnews"] = np.tile(news, (1, 4)).astype(np.float32)
    kk = np.arange(128)[:, None]
    t8 = np.arange(DEC_SEQ)[None, :]
    mc = np.where(kk >= t8 + 1, 0.0, NEG)
    c["mcache"] = np.tile(mc, (1, 64)).astype(np.float32)
    bm = (np.arange(128)[:, None] // DEC_SEQ == np.arange(SB_PER_CORE)[None, :]).astype(np.float32)
    c["bmrow"] = bm
    return c


CONST_SHAPES = None


class Builder:
    def __init__(self, cfg):
        self.cfg = cfg
        self.nc = bass.Bass("TRN2", target_bir_lowering=False)
        self.T = Tracker(self.nc)
        self.cms = []
        self.final_ops = []

    def sb(self, name, shape, dt):
        cm = self.nc.sbuf_tensor(name, list(shape), dt)
        t = cm.__enter__()
        self.cms.append(cm)
        return t

    def ps(self, name, shape, dt):
        cm = self.nc.psum_tensor(name, list(shape), dt)
        t = cm.__enter__()
        self.cms.append(cm)
        return t

    def din(self, name, shape, dt=F32):
        return self.nc.dram_tensor(name, list(shape), dt, kind="ExternalInput").ap()

    def dout(self, name, shape, dt=F32):
        return self.nc.dram_tensor(name, list(shape), dt, kind="ExternalOutput").ap()

    def mm(self, out, lhsT, rhs, start, stop, skip=False):
        nc = self.nc
        if skip:
            return self.T.add("pe", lambda: nc.tensor.matmul(out, lhsT=lhsT, rhs=rhs, start=start, stop=stop,
                                                             skip_group_check=True),
                              w=[out], r=[lhsT, rhs, out])
        return self.T.add("pe", lambda: nc.tensor.matmul(out, lhsT=lhsT, rhs=rhs, start=start, stop=stop),
                          w=[out], r=[lhsT, rhs] + ([] if start else [out]))

    def trf(self, out, in_, identf):
        nc = self.nc
        return self.T.add("pe", lambda: nc.tensor.transpose(out=out, in_=in_, identity=identf),
                          w=[out], r=[in_, identf])

    def tr(self, out, in_):
        nc = self.nc
        ident = self.identb[:]
        return self.T.add("pe", lambda: nc.tensor.transpose(out=out, in_=in_, identity=ident),
                          w=[out], r=[in_, ident])

    def act(self, out, in_, func, scale=1.0, bias=None):
        nc = self.nc
        r = [in_]
        kw = {}
        if not isinstance(scale, float):
            r.append(scale)
        if bias is not None:
            kw["bias"] = bias
            if not isinstance(bias, float):
                r.append(bias)
        return self.T.add("act", lambda: nc.scalar.activation(out=out, in_=in_, func=func, scale=scale, **kw),
                          w=[out], r=r)

    def tt(self, out, in0, in1, op, eng="dve"):
        nc = self.nc
        e = nc.vector if eng == "dve" else nc.gpsimd
        return self.T.add(eng, lambda: e.tensor_tensor(out=out, in0=in0, in1=in1, op=op), w=[out], r=[in0, in1])

    def ts(self, out, in0, s1, op0, s2=None, op1=None, eng="dve"):
        nc = self.nc
        e = nc.vector if eng == "dve" else nc.gpsimd
        r = [in0] + [s for s in (s1, s2) if s is not None and not isinstance(s, float)]
        if op1 is None:
            return self.T.add(eng, lambda: e.tensor_scalar(out=out, in0=in0, scalar1=s1, scalar2=None, op0=op0),
                              w=[out], r=r)
        return self.T.add(eng, lambda: e.tensor_scalar(out=out, in0=in0, scalar1=s1, scalar2=s2, op0=op0, op1=op1),
                          w=[out], r=r)

    def stt(self, out, in0, scalar, in1, op0, op1):
        nc = self.nc
        r = [in0, in1] + ([] if isinstance(scalar, float) else [scalar])
        return self.T.add("dve", lambda: nc.vector.scalar_tensor_tensor(out=out, in0=in0, scalar=scalar, in1=in1,
                                                                         op0=op0, op1=op1), w=[out], r=r)

    def cp(self, out, in_, eng="dve"):
        nc = self.nc
        if eng == "act":
            return self.T.add("act", lambda: nc.scalar.copy(out=out, in_=in_), w=[out], r=[in_])
        e = nc.vector if eng == "dve" else nc.gpsimd
        return self.T.add(eng, lambda: e.tensor_copy(out=out, in_=in_), w=[out], r=[in_])

    def memset(self, ap, val, eng="dve"):
        nc = self.nc
        e = nc.vector if eng == "dve" else nc.gpsimd
        return self.T.add(eng, lambda: e.memset(ap, val), w=[ap])

    def dma(self, out, in_, key, q="sp", final=False):
        nc = self.nc
        e = {"sp": nc.sync, "pool": nc.gpsimd, "act": nc.scalar}[q]
        if key is None or not key.startswith("!"):
            on, inn = out.tensor.name, in_.tensor.name
            key = ("st_" + inn) if on in self.dram_names else ("ld_" + on)
        op = self.T.add(q, lambda: e.dma_start(out=out, in_=in_), w=[out], r=[in_], dma_key=key,
                        untracked=self.untracked)
        if final:
            self.final_ops.append(op)
        return op

    def bnstats(self, out, in_):
        nc = self.nc
        return self.T.add("dve", lambda: nc.vector.bn_stats(out=out, in_=in_), w=[out], r=[in_])

    def bnaggr(self, out, in_):
        nc = self.nc
        return self.T.add("dve", lambda: nc.vector.bn_aggr(out=out, in_=in_), w=[out], r=[in_])

    def recip(self, out, in_):
        nc = self.nc
        return self.T.add("dve", lambda: nc.vector.reciprocal(out=out, in_=in_), w=[out], r=[in_])


def V(t, off, dims):
    f = 1
    for s in t.shape[1:]:
        f *= s
    return bass.AP(t, off, [[f, 128]] + [list(d) for d in dims])


def VP(t, p0, pn, off, dims):
    f = 1
    for s in t.shape[1:]:
        f *= s
    return bass.AP(t, p0 * f + off, [[f, pn]] + [list(d) for d in dims])


def AV(view, off, dims, p0=0, pn=128):
    f = view.ap[0][0]
    base = view.offset
    return bass.AP(view.tensor, base + p0 * f + off, [[f, pn]] + [list(d) for d in dims])


def build(cfg):
    nsc = cfg["nsc"]
    nlay = cfg["nlay"]
    stage = cfg.get("stage", 99)
    sstage = cfg.get("sstage", 99)
    dbgname = cfg.get("dbg", None)
    do_sample = cfg["sample"]
    B = Builder(cfg)
    nc = B.nc
    T = B.T
    NTOK = nsc * 512
    x_p = B.din("x_p", [NTOK, D])
    mem_p = B.din("mem_p", [NMEM, D])
    x_s = B.din("x_s", [128, D])
    st_ret = B.din("st_ret", [DEPTH, SB_PER_CORE, RH, RDK, RDV])
    c_sk = B.din("c_sk", [DEPTH, SB_PER_CORE, 128, 128])
    c_sv = B.din("c_sv", [DEPTH, SB_PER_CORE, 128, 128])
    c_mk = B.din("c_mk", [DEPTH, SB_PER_CORE, NMEM, 512])
    c_mv = B.din("c_mv", [DEPTH, SB_PER_CORE, NMEM, 512])
    w_in = B.din("w_in", [DEPTH, D, IN_W])
    w_br_ret = B.din("w_br_ret", [DEPTH, 1024, D])
    w_br_swa = B.din("w_br_swa", [DEPTH, 512, D])
    w_br_mem = B.din("w_br_mem", [DEPTH, 512, D])
    w_out = B.din("w_out", [DEPTH, D, D])
    w_mem_kv = B.din("w_mem_kv", [DEPTH, D, 1024])
    sinks = B.din("sinks", [DEPTH, SH])
    gn_g = B.din("gn_g", [DEPTH, 1024])
    ln1_g = B.din("ln1_g", [DEPTH, D])
    ln1_b = B.din("ln1_b", [DEPTH, D])
    w_up = B.din("w_up", [DEPTH, D, DFF])
    w_down = B.din("w_down", [DEPTH, DFF, D])
    ln2_g = B.din("ln2_g", [DEPTH, D])
    ln2_b = B.din("ln2_b", [DEPTH, D])
    own = dict(w_in=B.din("wo_in", [D, IN_W]), w_br_ret=B.din("wo_br_ret", [1024, D]),
               w_br_swa=B.din("wo_br_swa", [512, D]), w_br_mem=B.din("wo_br_mem", [512, D]),
               w_out=B.din("wo_out", [D, D]), w_mem_kv=B.din("wo_mem_kv", [D, 1024]),
               sinks=B.din("o_sinks", [1, SH]), gn_g=B.din("o_gn_g", [1024]), ln1_g=B.din("o_ln1_g", [D]),
               ln1_b=B.din("o_ln1_b", [D]), w_up=B.din("wo_up", [D, DFF]), w_down=B.din("wo_down", [DFF, D]),
               ln2_g=B.din("o_ln2_g", [D]), ln2_b=B.din("o_ln2_b", [D]))
    own_names = [v.tensor.name for v in own.values()]
    sinks_full = sinks
    w_in = [w_in[0], w_in[1], own["w_in"]]
    w_br_ret = [w_br_ret[0], w_br_ret[1], own["w_br_ret"]]
    w_br_swa = [w_br_swa[0], w_br_swa[1], own["w_br_swa"]]
    w_br_mem = [w_br_mem[0], w_br_mem[1], own["w_br_mem"]]
    w_out = [w_out[0], w_out[1], own["w_out"]]
    w_mem_kv = [w_mem_kv[0], w_mem_kv[1], own["w_mem_kv"]]
    w_up = [w_up[0], w_up[1], own["w_up"]]
    w_down = [w_down[0], w_down[1], own["w_down"]]
    gn_g = [gn_g[0], gn_g[1], own["gn_g"]]
    ln1_g = [ln1_g[0], ln1_g[1], own["ln1_g"]]
    ln1_b = [ln1_b[0], ln1_b[1], own["ln1_b"]]
    ln2_g = [ln2_g[0], ln2_g[1], own["ln2_g"]]
    ln2_b = [ln2_b[0], ln2_b[1], own["ln2_b"]]
    NSLOT = nsc + 2
    sel_d = B.din("sel", [128, 2])
    flag_d = B.din("flagneg", [128, 512])
    r_cos = B.din("r_cos", [128, NSLOT * 4, 32])
    r_sin = B.din("r_sin", [128, NSLOT * 4, 32])
    r_nsin = B.din("r_nsin", [128, NSLOT * 4, 32])
    cc_src = [nc.dram_tensor("cc_src%d" % i, [512, D], F32, kind="Internal").ap() for i in range(2)]
    cc_dst = [nc.dram_tensor("cc_dst%d" % i, [2 * 512, D], F32, kind="Internal").ap() for i in range(2)]
    hc = host_consts()
    cd = {k: B.din("c_" + k, list(v.shape)) for k, v in hc.items()}
    y_p = B.dout("y_p", [NSLOT * 512, D])
    y_s = B.dout("y_s", [128, D])
    rs_p = B.dout("rs_p", [2, RH, RDK, RDV])
    sk_p = B.dout("sk_p", [2, 128, 128])
    sv_p = B.dout("sv_p", [2, 128, 128])
    mk_p = B.dout("mk_p", [NMEM, 512])
    mv_p = B.dout("mv_p", [NMEM, 512])
    rs_s = B.dout("rs_s", [DEPTH, SB_PER_CORE, RH, RDK, RDV])
    sk_s = B.dout("sk_s", [DEPTH, SB_PER_CORE, 128, 128])
    sv_s = B.dout("sv_s", [DEPTH, SB_PER_CORE, 128, 128])
    B.dram_names = set(t.tensor.name for t in [y_p, y_s, rs_p, sk_p, sv_p, mk_p, mv_p, rs_s, sk_s, sv_s])
    B.untracked = set(n.tensor.name for n in
                      [x_p, mem_p, x_s, st_ret, c_sk, c_sv, c_mk, c_mv, w_in[0], w_br_ret[0], w_br_swa[0], w_br_mem[0],
                       w_out[0],
                       w_mem_kv[0], sinks, gn_g[0], ln1_g[0], ln1_b[0], w_up[0], w_down[0], ln2_g[0], ln2_b[0],
                       sel_d, flag_d, r_cos, r_sin, r_nsin]
                      + list(cd.values())) | set(own_names)

    sb = B.sb
    B.identb = sb("identb", [128, 128], BF16)
    onesb = sb("onesb", [128, 128], BF16)
    X = sb("X", [128, 4, D], F32)
    XT = sb("XT", [128, 8, 512], BF16)
    Xb = sb("Xb", [128, D], BF16)
    ARENA = sb("ARENA", [128, 32 * 512], BF16)
    HT = ARENA[:].rearrange("p (a b) -> p a b", a=32)
    GT = ARENA[:, 0:24 * 512].rearrange("p (a b) -> p a b", a=24)
    MQT = ARENA[:, 24 * 512:28 * 512].rearrange("p (a b) -> p a b", a=4)
    RQT = ARENA[:, 28 * 512:32 * 512].rearrange("p (a b) -> p a b", a=4)
    PV_ = dict(HT=HT, GT=GT, MQT=MQT, RQT=RQT)
    SV_ = dict(HT=ARENA[:, 0:32 * 128].rearrange("p (a b) -> p a b", a=32),
               GT=ARENA[:, 0:24 * 128].rearrange("p (a b) -> p a b", a=24),
               MQT=ARENA[:, 24 * 128:28 * 128].rearrange("p (a b) -> p a b", a=4),
               RQT=ARENA[:, 28 * 128:32 * 128].rearrange("p (a b) -> p a b", a=4))
    a0 = 32 * 128
    RKwX = ARENA[:, a0:a0 + 4096].rearrange("p (r c) -> p r c", r=8)
    S0bg = ARENA[:, a0 + 4096:a0 + 6144].rearrange("p (b r e) -> p b r e", b=4, r=4)
    Ec = ARENA[:, a0 + 6144:a0 + 7168].rearrange("p (b g c) -> p b g c", b=16, g=2)
    KcT = ARENA[:, a0 + 7168:a0 + 9216].rearrange("p (b k) -> p b k", b=16)
    Kraw = ARENA[:, a0 + 9216:a0 + 11264].rearrange("p (b k) -> p b k", b=16)
    SQTs = ARENA[:, a0 + 11264:a0 + 12288].rearrange("p (g c) -> p g c", g=2)
    S0g = X[:, 1:3, :].rearrange("p a (b e) -> p (a b) e", e=128).rearrange("p (b r) e -> p b r e", b=4)
    Vcp = X[:, 3, :].bitcast(BF16).rearrange("p (b g h c) -> p b g h c", b=4, g=2, h=2)
    identF = sb("identF", [128, 128], F32)
    BMR = sb("BMR", [128, 16], F32)
    RKm = sb("RKm", [128, 4, 8, 128], BF16)
    SQT = sb("SQT", [128, 4, 512], BF16)
    SKm = sb("SKm", [128, 5, 2, 128], BF16)
    RKw = sb("RKw", [128, 4, 512], BF16)
    Vb = sb("Vb", [128, 4, 1024], BF16)
    MT = Vb[:].rearrange("p a (k t) -> p (a k) t", k=2)
    SRG = sb("SRG", [128, 4, 1024], BF16)
    SVp = sb("SVp", [128, 5, 2, 2, 128], BF16)
    ONp = sb("ONp", [128, 2, 128], BF16)
    GRT = sb("GRT", [128, 8, 512], BF16)
    SWOT = sb("SWOT", [128, 4, 512], BF16)
    MOT = sb("MOT", [128, 4, 512], BF16)
    T1 = sb("T1", [128, 512], F32)
    T2 = sb("T2", [128, 512], F32)
    RF = sb("RF", [128, 512], F32)
    RQb = sb("RQb", [128, 512], BF16)
    ST = sb("ST", [128, 8, 128], BF16)
    RQs2 = sb("RQs2", [128, 8, 128], BF16)
    Y = sb("Y", [128, 8, 128], F32)
    GR = sb("GR", [128, 1024], BF16)
    STAT = sb("STAT", [128, 8, 6], F32)
    MV = sb("MV", [128, 8, 2], F32)
    E = sb("E", [128, 4, 512], BF16)
    MEMT = E[:].rearrange("p a (k m) -> p (a k) m", k=2)
    identf = T2[:, 0:128]
    MSKf = T1
    DEN = sb("DEN", [128, 512], F32)
    LNB = sb("LNB", [128, 2, D], F32)
    GNG = sb("GNG", [128, 1024], F32)
    NW = 3
    WR = [sb("WR%d" % i, [128, 8, 512], BF16) for i in range(NW)]
    ROPE = sb("ROPE", [128, 3, 4, 32], F32)
    DEC = sb("DEC", [128, 8, 128], F32)
    GQ = sb("GQ", [128, 4, 128], F32)
    WEND = sb("WEND", [128, 8], F32)
    GL = sb("GL", [128, 4], F32)
    MOWN = sb("MOWN", [128, 512], BF16)
    MPREV = sb("MPREV", [128, 512], BF16)
    SINKE = sb("SINKE", [128, 3, 4], F32)
    Sst = [sb("Sst", [128, 4, 128], F32)] * 3
    Sbf = [sb("Sbf", [128, 4, 128], BF16)] * 3
    SKTprev = [sb("SKTp", [128, 2, 128], BF16)] * 3
    SVprev = [sb("SVpv", [128, 2, 2, 128], BF16)] * 3
    MKT = [sb("MKT", [128, 4, 256], BF16)] * 3
    MVb = [sb("MVb", [128, 2, 512], BF16)] * 3
    SEL = sb("SEL", [128, 2], F32)
    XbN = sb("XbN", [128, 4, D], BF16)
    XbS = Xb[:].bitcast(F32)
    FLAGN = sb("FLAGN", [128, 512], BF16)
    SKf = sb("SKf", [128, 256], F32)
    PB = [B.ps("PB%d" % i, [128, 512], F32) for i in range(6)]
    PT = [B.ps("PT%d" % i, [128, 1024], BF16) for i in range(2)]

    B.dma(identf, cd["ident"], "c0")
    B.cp(B.identb[:], identf)
    B.memset(onesb[:], 1.0)
    B.memset(ONp[:], 0.0)
    B.memset(ONp[:, 0, 0:64], 1.0)
    B.memset(ONp[:, 1, 64:128], 1.0)
    B.memset(SVp[:], 0.0)
    B.memset(RKm[:], 0.0)
    B.memset(SKm[:], 0.0)
    B.memset(RQs2[:], 0.0)
    for l in range(1):
        B.memset(SVprev[l][:], 0.0)
        B.memset(SKTprev[l][:], 0.0)
        B.memset(Sst[l][:], 0.0)
        B.memset(Sbf[l][:], 0.0)
    B.dma(SEL[:], sel_d, None)
    B.dma(MSKf[:], flag_d, None)
    B.cp(FLAGN[:], MSKf[:])
    B.dma(MSKf[:], cd["mown"], "c0")
    B.cp(MOWN[:], MSKf[:])
    B.dma(MSKf[:], cd["mprev"], "c0")
    B.cp(MPREV[:], MSKf[:])
    SKRAW = sb("SKRAW", [128, 3, SH], F32)
    B.dma(SKRAW[:, 0:2, :], bass.AP(sinks_full.tensor, 0, [[0, 128], [SH, DEPTH], [1, SH]]), "c0")
    B.dma(SKRAW[:, 2, :], bass.AP(own["sinks"].tensor, 0, [[0, 128], [1, SH]]), "c0")
    for hh in range(2):
        B.act(SINKE[hh * 64:(hh + 1) * 64, :, :],
              AV(SKRAW[:], hh, [[SH, 3], [2, 4]], p0=hh * 64, pn=64), AF.Exp)

    def load_prompt_consts():
        B.dma(DEC[:], cd["decp"], "c0")
        B.dma(GQ[:], cd["gqp"], "c0")
        B.dma(WEND[:], cd["wendp"], "c0")
        B.dma(GL[:], cd["glp"], "c0")

    wstate = dict(idx=0, plan=[])

    def wplan_for_pass(l, first):
        pl = []
        if first:
            pl.append((w_mem_kv[l][:, 0:512], 8, 512))
            pl.append((w_mem_kv[l][:, 512:1024], 8, 512))
        for j in range(7):
            pl.append((w_in[l][:, j * 512:(j + 1) * 512], 8, 512))
        pl.append((w_in[l][:, 3584:3840], 8, 256))
        for jj in range(7):
            pl.append((w_in[l][:, 3840 + jj * 512:3840 + (jj + 1) * 512], 8, 512))
        for cb in range(2):
            pl.append((w_br_ret[l][:, cb * 512:(cb + 1) * 512], 8, 512))
            pl.append(((w_br_swa[l][:, cb * 512:(cb + 1) * 512], w_br_mem[l][:, cb * 512:(cb + 1) * 512]), 8, 512))
        for cb in range(2):
            pl.append((w_out[l][:, cb * 512:(cb + 1) * 512], 8, 512))
        for jb in range(8):
            pl.append((w_up[l][:, jb * 512:(jb + 1) * 512], 8, 512))
        for cb in range(2):
            for rb in range(4):
                pl.append((w_down[l][rb * 1024:(rb + 1) * 1024, cb * 512:(cb + 1) * 512], 8, 512))
        return pl

    def w_issue(i):
        src, nkt, ncols = wstate["plan"][i]
        slot = WR[i % NW]
        if isinstance(src, tuple):
            for half, s in enumerate(src):
                B.dma(slot[:, half * 4:(half + 1) * 4, 0:ncols], s.rearrange("(kt p) c -> p kt c", p=128),
                      "w%d" % (i % NW), q="pool")
        else:
            B.dma(slot[:, 0:nkt, 0:ncols], src.rearrange("(kt p) c -> p kt c", p=128), "w%d" % (i % NW), q="pool")

    PF = 1

    def w_next():
        i = wstate["idx"]
        if i == 0:
            for k in range(min(PF, len(wstate["plan"]))):
                w_issue(k)
        if i + PF < len(wstate["plan"]):
            w_issue(i + PF)
        wstate["idx"] = i + 1
        return WR[i % NW]

    def transposes_to(dst_view_fn, src_fn, n, ptile, evac_eng="dve"):
        for k in range(n):
            B.tr(ptile[:, k * 128:(k + 1) * 128], src_fn(k))
        B.cp(dst_view_fn, ptile[:, 0:n * 128].rearrange("p (a b) -> p a b", a=n), eng=evac_eng)

    def rope_block(ps, nheads, tabidx, out_ap_f32=None, out_ap_bf=None, sample=False, perm=False):
        w = nheads * 64
        cosb = AV(ROPE[:], (0 * 4 + tabidx) * 32, [[0, nheads], [0, 2], [1, 32]])
        sinb = AV(ROPE[:], (1 * 4 + tabidx) * 32, [[0, nheads], [1, 32]])
        nsinb = AV(ROPE[:], (2 * 4 + tabidx) * 32, [[0, nheads], [1, 32]])
        psv = ps.rearrange("p (h t f) -> p h t f", h=nheads, t=2)
        t1v = T1[:, 0:w].rearrange("p (h t f) -> p h t f", h=nheads, t=2)
        t2v = T2[:, 0:w].rearrange("p (h t f) -> p h t f", h=nheads, t=2)
        B.tt(t1v, psv, cosb, ALU.mult)
        B.tt(t2v[:, :, 0, :], psv[:, :, 1, :], nsinb, ALU.mult)
        B.tt(t2v[:, :, 1, :], psv[:, :, 0, :], sinb, ALU.mult)
        if out_ap_f32 is not None:
            B.tt(out_ap_f32, T1[:, 0:w], T2[:, 0:w], ALU.add)
            if out_ap_bf is not None:
                B.cp(out_ap_bf, out_ap_f32, eng="act")
        elif perm:
            B.tt(out_ap_bf, T1[:, 0:w].rearrange("p (s t d) -> p s t d", s=2, t=4),
                 T2[:, 0:w].rearrange("p (s t d) -> p s t d", s=2, t=4), ALU.add)
        else:
            B.tt(out_ap_bf, T1[:, 0:w], T2[:, 0:w], ALU.add)

    def layer_norm_chunk(xc, gslot=0):
        for a in range(2):
            B.bnstats(STAT[:, a, :], xc[:, a * 512:(a + 1) * 512])
        B.bnaggr(MV[:, 0, :], STAT[:, 0:2, :].rearrange("p a b -> p (a b)"))
        B.ts(MV[:, 1, 0:1], MV[:, 0, 1:2], LN_EPS, ALU.add)
        B.act(MV[:, 1, 0:1], MV[:, 1, 0:1], AF.Ln)
        B.act(MV[:, 1, 0:1], MV[:, 1, 0:1], AF.Exp, scale=-0.5)
        B.ts(xc, xc, MV[:, 0, 0:1], ALU.subtract, MV[:, 1, 0:1], ALU.mult)
        B.tt(xc, xc, LNB[:, 0, :], ALU.mult)
        B.tt(xc, xc, LNB[:, 1, :], ALU.add)

    def group_norm_gate(ops, c, nch_tok):
        for h in range(8):
            B.bnstats(STAT[:, h, :], ops[h // 4][:, (h % 4) * 128:(h % 4 + 1) * 128])
        for h in range(8):
            B.bnaggr(MV[:, h, :], STAT[:, h, :])
        B.ts(MV[:, :, 1], MV[:, :, 1], GN_EPS, ALU.add)
        B.act(MV[:, :, 1], MV[:, :, 1], AF.Ln)
        B.act(MV[:, :, 1], MV[:, :, 1], AF.Exp, scale=-0.5)
        for half in range(2):
            pv = ops[half][:].rearrange("p (h e) -> p h e", h=4)
            B.tt(Y[:, half * 4:(half + 1) * 4, :], pv,
                 AV(MV[:], half * 8, [[2, 4], [0, 128]]), ALU.subtract)
        B.tt(Y[:], Y[:], AV(MV[:], 1, [[2, 8], [0, 128]]), ALU.mult)
        B.tt(Y[:], Y[:], GNG[:].rearrange("p (h e) -> p h e", h=8), ALU.mult)
        B.tt(GR[:], Y[:].rearrange("p h e -> p (h e)"), SRG[:, c, :], ALU.mult)
        transposes_to(GRT[:, :, c * 128:(c + 1) * 128], lambda k: GR[:, k * 128:(k + 1) * 128], 8, PT[0],
                      evac_eng="act")


    EM = sb("EM", [128, 2, 512], BF16)

    M = dict(PV_)

    def bcast_row(dram_ap_row, n):
        return bass.AP(dram_ap_row.tensor, dram_ap_row.offset, [[0, 128], [1, n]])

    def compute_memT():
        for mb in range(2):
            for half in range(2):
                B.dma(T1[:], mem_p[mb * 128:(mb + 1) * 128, half * 512:(half + 1) * 512], "x")
                B.cp(Xb[:, 0:512], T1[:], eng="act")
                transposes_to(MEMT[:, half * 4:(half + 1) * 4, mb * 128:(mb + 1) * 128],
                              lambda k: Xb[:, k * 128:(k + 1) * 128], 4, PT[half])

    def mem_kv(l):
        import os
        SUB = int(os.environ.get("SUB", "99"))
        compute_memT()
        Wk = w_next()
        for mb in range(2):
            ps = PB[mb]
            for kt in range(8):
                B.mm(ps[:], MEMT[:, kt, mb * 128:(mb + 1) * 128], Wk[:, kt, :], kt == 0, kt == 7)
            if SUB < 1:
                continue
            B.cp(T1[:], ps[:], eng="act")
            if SUB < 2:
                continue
            B.dma(mk_p[mb * 128:(mb + 1) * 128, :], T1[:], "o_mk", final=True)
            if SUB < 3:
                continue
            B.cp(RQb[:], ps[:])
            transposes_to(MKT[l][:, :, mb * 128:(mb + 1) * 128], lambda k: RQb[:, k * 128:(k + 1) * 128], 4, PT[mb])
        if SUB < 4:
            return
        Wv = w_next()
        for mb in range(2):
            ps = PB[2 + mb]
            for kt in range(8):
                B.mm(ps[:], MEMT[:, kt, mb * 128:(mb + 1) * 128], Wv[:, kt, :], kt == 0, kt == 7)
            B.cp(T2[:], ps[:], eng="act")
            B.dma(mv_p[mb * 128:(mb + 1) * 128, :], T2[:], "o_mv", final=True)
            B.cp(MVb[l][:, mb, :], ps[:])

    def load_layer_params(l):
        B.dma(GNG[:], bcast_row(gn_g[l], 1024), "prm")
        B.dma(LNB[:, 0, :], bcast_row(ln1_g[l], D), "prm")
        B.dma(LNB[:, 1, :], bcast_row(ln1_b[l], D), "prm")

    def xT_phase(nch):
        for c in range(nch):
            B.cp(Xb[:], X[:, c, :], eng="act")
            transposes_to(XT[:, :, c * 128:(c + 1) * 128], lambda k: Xb[:, k * 128:(k + 1) * 128], 8, PT[c % 2])

    def in_proj(l, nch, sample, last_chunk_out=None, mid_hook=None):
        NTk = nch * 128
        for j in range(8):
            W = w_next()
            ncols = 512 if j < 7 else 256
            for c in range(nch):
                ps = PB[(j * nch + c) % 3]
                for kt in range(8):
                    B.mm(ps[:, 0:ncols], XT[:, kt, c * 128:(c + 1) * 128], W[:, kt, 0:ncols], kt == 0, kt == 7)
                tab = 0 if sample else c
                if j == 0:
                    rope_block(ps[:], 8, tab, out_ap_bf=RQb[:])
                    transposes_to(M["RQT"][:, :, c * 128:(c + 1) * 128], lambda k: RQb[:, k * 128:(k + 1) * 128], 4, PT[1])
                elif j == 1:
                    rope_block(ps[:], 8, tab, out_ap_f32=RF[:], out_ap_bf=RQb[:])
                    for k in range(4):
                        B.tr(PT[1][:, k * 128:(k + 1) * 128], RQb[:, k * 128:(k + 1) * 128])
                    for hh in range(2):
                        B.cp(AV(RKm[:], (c * 8 + hh) * 128, [[256, 4], [1, 128]], p0=hh * 64, pn=64),
                             PT[1][hh * 64:(hh + 1) * 64, 0:512].rearrange("p (a b) -> p a b", a=4))
                    B.tt(RKw[:, c, :].rearrange("p (h d) -> p h d", h=8), RF[:].rearrange("p (h d) -> p h d", h=8),
                         AV(WEND[:], 0, [[1, 8], [0, 64]]), ALU.mult)
                elif j in (2, 3):
                    B.cp(Vb[:, c, (j - 2) * 512:(j - 1) * 512], ps[:], eng="act")
                elif j in (4, 5):
                    B.act(SRG[:, c, (j - 4) * 512:(j - 3) * 512], ps[:], AF.Silu)
                elif j == 6:
                    rope_block(ps[:], 8, tab, out_ap_bf=AV(RQb[:], 0, [[64, 2], [128, 4], [1, 64]]), perm=True)
                    transposes_to(SQT[:, c, :].rearrange("p (a b) -> p a b", a=4),
                                  lambda k: RQb[:, k * 128:(k + 1) * 128], 4, PT[1])
                else:
                    rope_block(ps[:, 0:128], 2, tab, out_ap_f32=SKf[:, 0:128], out_ap_bf=RQb[:, 0:128])
                    B.tr(PT[1][:, 0:128], RQb[:, 0:128])
                    for g in range(2):
                        B.cp(SKm[g * 64:(g + 1) * 64, 1 + c, g, :], PT[1][g * 64:(g + 1) * 64, 0:128])
                    B.cp(SKf[:, 128:256], ps[:, 128:256], eng="act")
                    for g in range(2):
                        for hh in range(2):
                            B.cp(SVp[:, 1 + c, g, hh, hh * 64:(hh + 1) * 64], SKf[:, 128 + g * 64:128 + (g + 1) * 64])
                    if last_chunk_out is not None and c == nch - 1:
                        last_chunk_out()
        for jj in range(7):
            if jj == 1 and mid_hook is not None:
                mid_hook()
            W = w_next()
            for t in range(4):
                ti = jj * 4 + t
                ps = PB[ti % 3]
                for kt in range(8):
                    B.mm(ps[:, 0:NTk], W[:, kt, t * 128:(t + 1) * 128], XT[:, kt, 0:NTk], kt == 0, kt == 7)
                if ti < 4:
                    B.cp(M["MQT"][:, ti, 0:NTk], ps[:, 0:NTk], eng="act")
                else:
                    B.act(M["GT"][:, ti - 4, 0:NTk], ps[:, 0:NTk], AF.Sigmoid)

    def swa_pv_norm(l, c, has_prev, slot_prev, slot_own):
        for pr in range(4):
            g = pr // 2
            for which in range(2):
                dst = (PB[5], PB[0])[which][:, pr * 128:(pr + 1) * 128]
                first = True
                for hh in range(2):
                    t = 2 * (pr % 2) + hh
                    for blk in range(2):
                        if blk == 0 and not has_prev:
                            continue
                        slot = slot_prev if blk == 0 else slot_own
                        lhs = SVp[:, slot, g, hh, :] if which == 0 else ONp[:, hh, :]
                        B.mm(dst, lhs, E[:, g * 2 + blk, t * 128:(t + 1) * 128], first, hh == 1 and blk == 1)
                        first = False
        for pr in range(4):
            B.ts(DEN[:, pr * 128:(pr + 1) * 128], PB[0][:, pr * 128:(pr + 1) * 128], SINKE[:, l, pr:pr + 1], ALU.add)
        B.recip(DEN[:], DEN[:])
        B.tt(SWOT[:, :, c * 128:(c + 1) * 128], PB[5][:].rearrange("p (a i) -> p a i", a=4),
             DEN[:].rearrange("p (a i) -> p a i", a=4), ALU.mult)

    def branches_out_mlp(l, nch, mid_hook=None, tail_hook=None):
        NTk = nch * 128
        for cb in range(2):
            Wret = w_next()
            Wsm = w_next()
            for o4 in range(4):
                ot = cb * 4 + o4
                bk = (ot % 2) * 3
                pr_, psw, pme = PB[bk], PB[bk + 1], PB[bk + 2]
                for kt in range(8):
                    B.mm(pr_[:, 0:NTk], Wret[:, kt, o4 * 128:(o4 + 1) * 128], GRT[:, kt, 0:NTk], kt == 0, kt == 7)
                for kt in range(4):
                    B.mm(psw[:, 0:NTk], Wsm[:, kt, o4 * 128:(o4 + 1) * 128], SWOT[:, kt, 0:NTk], kt == 0, kt == 3)
                for kt in range(4):
                    B.mm(pme[:, 0:NTk], Wsm[:, 4 + kt, o4 * 128:(o4 + 1) * 128], MOT[:, kt, 0:NTk], kt == 0, kt == 3)
                B.tt(T1[:, 0:NTk], pr_[:, 0:NTk], M["GT"][:, ot, 0:NTk], ALU.mult)
                B.tt(T2[:, 0:NTk], psw[:, 0:NTk], M["GT"][:, 8 + ot, 0:NTk], ALU.mult)
                B.tt(RF[:, 0:NTk], pme[:, 0:NTk], M["GT"][:, 16 + ot, 0:NTk], ALU.mult)
                B.tt(T1[:, 0:NTk], T1[:, 0:NTk], T2[:, 0:NTk], ALU.add)
                B.tt(MT[:, ot, 0:NTk], T1[:, 0:NTk], RF[:, 0:NTk], ALU.add)
        for cb in range(2):
            Wo = w_next()
            for c in range(nch):
                ps = PB[(cb * nch + c) % 6]
                for kt in range(8):
                    B.mm(ps[:], MT[:, kt, c * 128:(c + 1) * 128], Wo[:, kt, :], kt == 0, kt == 7)
                xs = X[:, c, cb * 512:(cb + 1) * 512]
                B.stt(xs, xs, ALPHA, ps[:], ALU.mult, ALU.add)
        for c in range(nch):
            layer_norm_chunk(X[:, c, :], 0)
        B.dma(LNB[:, 0, :], bcast_row(ln2_g[l], D), "prm")
        B.dma(LNB[:, 1, :], bcast_row(ln2_b[l], D), "prm")
        xT_phase(nch)
        for jb in range(8):
            W = w_next()
            for t in range(4):
                ft = jb * 4 + t
                ps = PB[ft % 6]
                R = (T1, T2)[ft % 2]
                for kt in range(8):
                    B.mm(ps[:, 0:NTk], W[:, kt, t * 128:(t + 1) * 128], XT[:, kt, 0:NTk], kt == 0, kt == 7)
                B.act(R[:, 0:NTk], ps[:, 0:NTk], AF.Relu)
                B.tt(M["HT"][:, ft, 0:NTk], R[:, 0:NTk], R[:, 0:NTk], ALU.mult)
        for cb in range(2):
            for rb in range(4):
                W = w_next()
                for c in range(nch):
                    for k8 in range(8):
                        B.mm(PB[c][:], M["HT"][:, rb * 8 + k8, c * 128:(c + 1) * 128], W[:, k8, :],
                             rb == 0 and k8 == 0, rb == 3 and k8 == 7)
            if cb == 1 and tail_hook is not None:
                tail_hook()
            for c in range(nch):
                xs = X[:, c, cb * 512:(cb + 1) * 512]
                B.stt(xs, xs, ALPHA, PB[c][:], ALU.mult, ALU.add)
            if cb == 0 and mid_hook is not None:
                mid_hook()
        for c in range(nch):
            layer_norm_chunk(X[:, c, :], 0)

    def prefetch_input_bf16(t):
        for c in range(4):
            for half in range(2):
                dst = XbN[:, c, half * 512:(half + 1) * 512]
                if t < nsc:
                    B.dma(RF[:], x_p[t * 512 + c * 128:t * 512 + (c + 1) * 128, half * 512:(half + 1) * 512], None)
                if t >= 2:
                    B.dma(XbS, cc_dst[t % 2][c * 128:(c + 1) * 128, half * 512:(half + 1) * 512], None)
                if t < 2:
                    B.ts(dst, RF[:], SEL[:, 0:1], ALU.mult)
                elif t < nsc:
                    B.ts(RF[:], RF[:], SEL[:, 0:1], ALU.mult)
                    B.stt(dst, XbS, SEL[:, 1:2], RF[:], ALU.mult, ALU.add)
                else:
                    B.ts(dst, XbS, SEL[:, 1:2], ALU.mult)

    def xt_from_prefetch():
        for c in range(4):
            transposes_to(XT[:, :, c * 128:(c + 1) * 128], lambda k: XbN[:, c, k * 128:(k + 1) * 128], 8, PT[c % 2],
                          evac_eng="act")

    def load_x_fp32(sc):
        if sc < nsc:
            B.dma(X[:], x_p[sc * 512:(sc + 1) * 512, :].rearrange("(c p) d -> p c d", p=128), "x")
            Xf = X[:].rearrange("p c d -> p (c d)")
            B.ts(Xf, Xf, SEL[:, 0:1], ALU.mult)
        else:
            B.memset(X[:], 0.0)
        if sc >= 2:
            gsrc = cc_dst[sc % 2]
            for c in range(4):
                for half in range(2):
                    stg = (T1, T2)[(c * 2 + half) % 2]
                    B.dma(stg[:], gsrc[c * 128:(c + 1) * 128, half * 512:(half + 1) * 512], None)
                    xs = X[:, c, half * 512:(half + 1) * 512]
                    B.stt(xs, stg[:], SEL[:, 1:2], xs, ALU.mult, ALU.add)

    def prompt_pass(sc, l):
        chunk0 = sc * 4
        B.dma(ROPE[:, 0, :, :], r_cos[:, chunk0:chunk0 + 4, :], "rope")
        B.dma(ROPE[:, 1, :, :], r_sin[:, chunk0:chunk0 + 4, :], "rope")
        B.dma(ROPE[:, 2, :, :], r_nsin[:, chunk0:chunk0 + 4, :], "rope")
        load_layer_params(l)
        if sc == 0:
            mem_kv(l)
        if sc == 0:
            prefetch_input_bf16(0)
            xt_from_prefetch()
        B.cp(SKm[:, 0], SKTprev[l][:])
        B.cp(SVp[:, 0], SVprev[l][:])
        ver = {nsc - 1: 0, nsc + 1: 1}.get(sc, None)
        last = ver is not None

        def last_out():
            B.dma(sk_p[ver], SKf[:, 0:128], "o_sk", final=True)
            B.dma(sv_p[ver], SKf[:, 128:256], "o_sk", final=True)
        in_proj(l, 4, False, last_out if last else None, mid_hook=lambda: load_x_fp32(sc))
        S, Sb_ = Sst[l], Sbf[l]
        for c in range(4):
            cs = slice(c * 128, (c + 1) * 128)
            for h in range(8):
                pr, hh = h // 2, h % 2
                ps = PB[3 + h // 4]
                B.mm(ps[:, (h % 4) * 128:(h % 4 + 1) * 128], RKm[:, c, h, :], M["RQT"][:, pr, cs], True, True)
            for half in range(2):
                B.tt(ST[:, half * 4:(half + 1) * 4, :], PB[3 + half][:].rearrange("p (h i) -> p h i", h=4),
                     DEC[:, half * 4:(half + 1) * 4, :], ALU.mult)
            for hh in range(2):
                B.tt(AV(RQs2[:], hh * 128, [[256, 4], [1, 128]], p0=hh * 64, pn=64),
                     M["RQT"][hh * 64:(hh + 1) * 64, :, cs], GQ[hh * 64:(hh + 1) * 64, :, :], ALU.mult)
            obanks = (PB[5], PB[0])
            for h in range(8):
                pr, hh = h // 2, h % 2
                po = obanks[h // 4][:, (h % 4) * 128:(h % 4 + 1) * 128]
                B.mm(po, ST[:, h, :], Vb[:, c, h * 128:(h + 1) * 128], True, False)
                B.mm(po, RQs2[:, h, :], Sb_[:, pr, :], False, True)
            group_norm_gate(obanks, c, 4)
            for pr in range(4):
                ps = PB[1 + pr % 2]
                B.mm(ps[:, 0:256], RKw[:, c, pr * 128:(pr + 1) * 128], Vb[:, c, pr * 256:(pr + 1) * 256], True, True)
                for hh in range(2):
                    sv = S[hh * 64:(hh + 1) * 64, pr, :]
                    B.stt(sv, sv, GL[hh * 64:(hh + 1) * 64, pr:pr + 1],
                          ps[hh * 64:(hh + 1) * 64, hh * 128:(hh + 1) * 128], ALU.mult, ALU.add)
            B.cp(Sb_[:], S[:], eng="act")
            if last and c == 3:
                B.dma(bass.AP(rs_p.tensor, ver * RH * RDK * RDV, [[128, 128], [2 * RDK * RDV, 4], [1, 128]]), S[:],
                      "o_rs", final=True)
            if stage < 6:
                continue
            has_prev = not (sc == 0 and c == 0)
            banks = {(0, 0): PB[1], (0, 1): PB[2], (1, 0): PB[3], (1, 1): PB[4]}
            for g in range(2):
                qv = SQT[:, c, :]
                for blk in range(2):
                    if blk == 0 and not has_prev:
                        continue
                    ps = banks[(g, blk)]
                    B.mm(ps[:], SKm[:, c + blk, g, :], qv, True, False)
                    if blk == 0 and sc == 2 and c == 0:
                        B.mm(ps[:], B.identb[:], FLAGN[:], False, False)
                    B.mm(ps[:], B.identb[:], (MPREV if blk == 0 else MOWN)[:], False, True)
                    B.act(E[:, g * 2 + blk, :], ps[:], AF.Exp, scale=SHD ** -0.5)
            swa_pv_norm(l, c, has_prev, c, c + 1)
            if stage < 7:
                continue
            for blk in range(2):
                ps = PB[1 + blk]
                for h in range(4):
                    B.mm(ps[:, h * 128:(h + 1) * 128], MKT[l][:, h, blk * 128:(blk + 1) * 128], M["MQT"][:, h, cs], True, True)
                B.act(EM[:, blk, :], ps[:], AF.Exp, scale=MHD ** -0.5)
            for h in range(4):
                for blk in range(2):
                    B.mm(PB[3][:, h * 128:(h + 1) * 128], MVb[l][:, blk, h * 128:(h + 1) * 128],
                         EM[:, blk, h * 128:(h + 1) * 128], blk == 0, blk == 1)
                for blk in range(2):
                    B.mm(PB[4][:, h * 128:(h + 1) * 128], onesb[:], EM[:, blk, h * 128:(h + 1) * 128], blk == 0, blk == 1)
            B.recip(DEN[:], PB[4][:])
            B.tt(MOT[:, :, cs], PB[3][:].rearrange("p (a i) -> p a i", a=4),
                 DEN[:].rearrange("p (a i) -> p a i", a=4), ALU.mult)
        B.cp(SKTprev[l][:], SKm[:, 4])
        B.cp(SVprev[l][:], SVp[:, 4])
        if stage < 8:
            return
        nxt = sc + 1 < NSLOT
        branches_out_mlp(l, 4, mid_hook=(lambda: prefetch_input_bf16(sc + 1)) if nxt else None,
                         tail_hook=xt_from_prefetch if nxt else None)
        B.dma(y_p[sc * 512:(sc + 1) * 512, :].rearrange("(c p) d -> p c d", p=128), X[:], "o_y", final=True)
        if sc < nsc:
            k = sc % 2
            B.dma(cc_src[k].rearrange("(c p) d -> p c d", p=128), X[:], "!ccs%d" % k)
            src_ap, dst_ap = cc_src[k], cc_dst[k]
            if not cfg.get("nocc", False):
              B.T.add("pool", lambda: nc.gpsimd.collective_compute(
                "AllGather", ALU.bypass, replica_groups=[[i, i + 4] for i in range(4)], ins=[src_ap], outs=[dst_ap]),
                w=[dst_ap], r=[src_ap], dma_key="!cc%d" % sc, inc=1)

    def load_sample_consts():
        B.dma(DEC[:], cd["decs"], None)
        B.dma(GQ[:], cd["gqs"], None)
        B.dma(WEND[:], cd["wends"], None)
        B.dma(GL[:], cd["gls"], None)
        B.dma(ROPE[:, 0, 0:1, :], cd["coss"], None)
        B.dma(ROPE[:, 1, 0:1, :], cd["sins"], None)
        B.dma(ROPE[:, 2, 0:1, :], cd["nsins"], None)
        B.dma(BMR[:], cd["bmrow"], None)
        B.dma(identF[:], cd["ident"], None)
        B.dma(T1[:], cd["mnews"], None)
        B.cp(MOWN[:], T1[:])
        B.dma(T1[:], cd["mcache"], None)
        B.cp(MPREV[:], T1[:])

    def reload_prompt_masks():
        B.dma(T1[:], cd["mown"], None)
        B.cp(MOWN[:], T1[:])
        B.dma(T1[:], cd["mprev"], None)
        B.cp(MPREV[:], T1[:])

    def sample_pass(l):
        NB = SB_PER_CORE
        load_layer_params(l)
        if l == 0:
            B.dma(X[:, 0, :], x_s, None)
        B.dma(sk_s[l][:, 0:120, :], c_sk[l][:, 8:128, :], "!cck", final=True)
        B.dma(sv_s[l][:, 0:120, :], c_sv[l][:, 8:128, :], "!ccv", final=True)
        xT_phase(1)

        def new_rows_out():
            for b in range(NB):
                B.dma(sk_s[l, b, 120:128, :], SKf[8 * b:8 * b + 8, 0:128], None, final=True)
                B.dma(sv_s[l, b, 120:128, :], SKf[8 * b:8 * b + 8, 128:256], None, final=True)
        in_proj(l, 1, True, new_rows_out)
        RQT_, MQT_ = M["RQT"], M["MQT"]
        if sstage < 2:
            return
        cs = slice(0, 128)
        for h in range(8):
            pr = h // 2
            ps = PB[3 + h // 4]
            B.mm(ps[:, (h % 4) * 128:(h % 4 + 1) * 128], RKm[:, 0, h, :], RQT_[:, pr, cs], True, True)
        for half in range(2):
            B.tt(ST[:, half * 4:(half + 1) * 4, :], PB[3 + half][:].rearrange("p (h i) -> p h i", h=4),
                 DEC[:, half * 4:(half + 1) * 4, :], ALU.mult)
        for hh in range(2):
            B.tt(AV(RQs2[:], hh * 128, [[256, 4], [1, 128]], p0=hh * 64, pn=64),
                 RQT_[hh * 64:(hh + 1) * 64, :, cs], GQ[hh * 64:(hh + 1) * 64, :, :], ALU.mult)
        ot = (PB[5], PB[0])
        B.memset(ot[0][:], 0.0)
        B.memset(ot[1][:], 0.0)
        for h in range(8):
            B.mm(ot[h // 4][:, (h % 4) * 128:(h % 4 + 1) * 128], Vb[:, 0, h * 128:(h + 1) * 128], ST[:, h, :],
                 False, False, skip=True)
        for grp in range(4):
            b0 = grp * 4
            st_src = bass.AP(st_ret.tensor, (l * NB + b0) * RH * RDK * RDV,
                             [[128, 128], [RH * RDK * RDV, 4], [2 * RDK * RDV, 4], [1, 128]])
            B.dma(S0g, st_src, None)
            B.dma(S0bg, st_src, None, q="pool")
            if grp % 2 == 0:
                rnd = grp // 2
                B.tt(RKwX, AV(RKw[:], 0, [[0, 8], [1, 512]]),
                     AV(BMR[:], rnd * 8, [[1, 8], [0, 512]]), ALU.mult)
            for bl in range(4):
                b = b0 + bl
                for h in range(8):
                    pr = h // 2
                    B.mm(ot[h // 4][:, (h % 4) * 128 + 8 * b:(h % 4) * 128 + 8 * b + 8], S0bg[:, bl, pr, :],
                         RQs2[:, h, 8 * b:8 * b + 8], False, False, skip=True)
                for pr in range(4):
                    ps = PB[1 + pr // 2]
                    B.mm(ps[:, (pr % 2) * 256:(pr % 2 + 1) * 256], RKwX[:, b % 8, pr * 128:(pr + 1) * 128],
                         Vb[:, 0, pr * 256:(pr + 1) * 256], True, True)
                for pr in range(4):
                    ps = PB[1 + pr // 2]
                    for hh in range(2):
                        sv = S0g[hh * 64:(hh + 1) * 64, bl, pr, :]
                        c0 = (pr % 2) * 256 + hh * 128
                        B.stt(sv, sv, GL[hh * 64:(hh + 1) * 64, pr:pr + 1],
                              ps[hh * 64:(hh + 1) * 64, c0:c0 + 128], ALU.mult, ALU.add)
            B.dma(bass.AP(rs_s.tensor, (l * NB + b0) * RH * RDK * RDV,
                          [[128, 128], [RH * RDK * RDV, 4], [2 * RDK * RDV, 4], [1, 128]]), S0g, None, final=True)
        for half in range(2):
            B.cp(Y[:, half * 4:(half + 1) * 4, :], ot[half][:].rearrange("p (h i) -> p h i", h=4), eng="act")
        for h in range(8):
            B.trf(PB[3 + h // 4][:, (h % 4) * 128:(h % 4 + 1) * 128], Y[:, h, :], identF[:])
        group_norm_gate((PB[3], PB[4]), 0, 1)
        if sstage < 3:
            return
        for g in range(2):
            ps = PB[1 + g]
            B.mm(ps[:], SKm[:, 1, g, :], SQT[:, 0, :], True, False)
            B.mm(ps[:], B.identb[:], MOWN[:], False, True)
            B.act(E[:, g * 2 + 1, :], ps[:], AF.Exp, scale=SHD ** -0.5)
        B.memset(SQTs, 0.0)
        for g in range(2):
            B.cp(AV(SQTs, g * 512, [[32, 16], [8, 4], [1, 8]], p0=g * 64, pn=64),
                 AV(SQT[:], 0, [[8, 16], [128, 4], [1, 8]], p0=g * 64, pn=64))
        B.dma(Kraw, c_sk[l].rearrange("b k c -> k b c"), None, q="pool")
        for half in range(2):
            for k in range(8):
                B.tr(PT[half][:, k * 128:(k + 1) * 128], Kraw[:, half * 8 + k, :])
            B.cp(KcT[:, half * 8:(half + 1) * 8, :], PT[half][:].rearrange("p (a b) -> p a b", a=8))
        for half in range(2):
            ps = PB[3 + half]
            B.memset(ps[:], 0.0)
            for b8 in range(8):
                b = half * 8 + b8
                for g in range(2):
                    B.mm(ps[:, (b8 * 2 + g) * 32:(b8 * 2 + g + 1) * 32], KcT[:, b, :],
                         SQTs[:, g, b * 32:(b + 1) * 32], False, False, skip=True)
            B.mm(ps[:], B.identb[:], MPREV[:], False, False, skip=True)
            B.act(Ec[:, half * 8:(half + 1) * 8, :, :].rearrange("p b g c -> p (b g c)"), ps[:], AF.Exp,
                  scale=SHD ** -0.5)
        B.dma(Kraw, c_sv[l].rearrange("b k c -> k b c"), None, q="pool")
        num, den = PB[5], PB[0]
        B.memset(num[:], 0.0)
        B.memset(den[:], 0.0)
        for pr in range(4):
            g = pr // 2
            for hh in range(2):
                t = 2 * (pr % 2) + hh
                B.mm(num[:, pr * 128:(pr + 1) * 128], SVp[:, 1, g, hh, :], E[:, g * 2 + 1, t * 128:(t + 1) * 128],
                     False, False, skip=True)
                B.mm(den[:, pr * 128:(pr + 1) * 128], ONp[:, hh, :], E[:, g * 2 + 1, t * 128:(t + 1) * 128],
                     False, False, skip=True)
        for grp in range(4):
            b0 = grp * 4
            B.memset(Vcp, 0.0)
            for hh in range(2):
                B.cp(AV(Vcp, hh * 128 + hh * 64, [[512, 4], [256, 2], [1, 64]]),
                     AV(Kraw, b0 * 128, [[128, 4], [64, 2], [1, 64]]))
            for bl in range(4):
                b = b0 + bl
                for pr in range(4):
                    g = pr // 2
                    for hh in range(2):
                        t = 2 * (pr % 2) + hh
                        rhs = Ec[:, b, g, t * 8:(t + 1) * 8]
                        B.mm(num[:, pr * 128 + 8 * b:pr * 128 + 8 * b + 8], Vcp[:, bl, g, hh, :], rhs,
                             False, False, skip=True)
                        B.mm(den[:, pr * 128 + 8 * b:pr * 128 + 8 * b + 8], ONp[:, hh, :], rhs,
                             False, False, skip=True)
        for pr in range(4):
            B.ts(DEN[:, pr * 128:(pr + 1) * 128], den[:, pr * 128:(pr + 1) * 128], SINKE[:, l, pr:pr + 1], ALU.add)
        B.recip(DEN[:], DEN[:])
        B.tt(SWOT[:, :, cs], num[:].rearrange("p (a i) -> p a i", a=4),
             DEN[:].rearrange("p (a i) -> p a i", a=4), ALU.mult)
        if sstage < 4:
            return
        Kmraw = Vb[:, 1:3, :].rearrange("p a (k c) -> p (a k) c", k=2)
        KmT = SRG[:, 1:3, :].rearrange("p a (h m) -> p a h m", h=4)
        Vm = RKm[:, 1:3, :, :].rearrange("p a h c -> p (a h c)").rearrange("p (k c) -> p k c", k=4)
        Em = RKw[:, 1, 0:128].rearrange("p (b k h i) -> p b k h i", b=2, k=2, h=4)
        mnum, mden = PB[3], PB[4]
        B.memset(mnum[:], 0.0)
        B.memset(mden[:], 0.0)
        for grp in range(NB // 2):
            b0 = grp * 2
            B.dma(Kmraw, c_mk[l][b0:b0 + 2].rearrange("b (k m) c -> m (b k) c", k=2), None, q="pool")
            B.dma(Vm, c_mv[l][b0:b0 + 2].rearrange("b (k m) c -> m (b k) c", k=2), None, q="pool")
            sc_ps = PB[1 + grp % 2]
            B.memset(sc_ps[:, 0:128], 0.0)
            for bl in range(2):
                b = b0 + bl
                for blk in range(2):
                    for h in range(4):
                        B.tr(PT[bl][:, (h * 2 + blk) * 128:(h * 2 + blk + 1) * 128],
                             Kmraw[:, bl * 2 + blk, h * 128:(h + 1) * 128])
                B.cp(KmT[:, bl, :, :].rearrange("p h m -> p (h m)"), PT[bl][:])
                for blk in range(2):
                    for h in range(4):
                        c0 = ((bl * 2 + blk) * 4 + h) * 8
                        B.mm(sc_ps[:, c0:c0 + 8], KmT[:, bl, h, blk * 128:(blk + 1) * 128],
                             MQT_[:, h, 8 * b:8 * b + 8], False, False, skip=True)
            B.act(Em.rearrange("p b k h i -> p (b k h i)"), sc_ps[:, 0:128], AF.Exp, scale=MHD ** -0.5)
            for bl in range(2):
                b = b0 + bl
                for blk in range(2):
                    for h in range(4):
                        rhs = Em[:, bl, blk, h, :]
                        B.mm(mnum[:, h * 128 + 8 * b:h * 128 + 8 * b + 8], Vm[:, bl * 2 + blk, h * 128:(h + 1) * 128],
                             rhs, False, False, skip=True)
                        B.mm(mden[:, h * 128 + 8 * b:h * 128 + 8 * b + 8], onesb[:], rhs, False, False, skip=True)
        B.recip(DEN[:], mden[:])
        B.tt(MOT[:, :, cs], mnum[:].rearrange("p (a i) -> p a i", a=4),
             DEN[:].rearrange("p (a i) -> p a i", a=4), ALU.mult)
        if sstage < 5:
            return
        branches_out_mlp(l, 1)
        if l == nlay - 1:
            B.dma(y_s, X[:, 0, :], None, final=True)

    passes = []
    if do_sample:
        for l in range(nlay):
            passes.append(("s", 0, l))
    for sc in range(NSLOT):
        passes.append(("p", sc, 2))
    for kind, sc, l in passes:
        wstate["plan"] += wplan_for_pass(l, kind == "p" and sc == 0)
    if do_sample:
        load_sample_consts()
    else:
        load_prompt_consts()
    for pi, (kind, sc, l) in enumerate(passes):
        T.phase = pi
        if kind == "s":
            M.update(SV_)
            sample_pass(l)
            if l == nlay - 1:
                M.update(PV_)
                B.memset(RKm[:], 0.0)
                load_prompt_consts()
                reload_prompt_masks()
        elif not cfg.get("noprompt", False):
            prompt_pass(sc, l)
    if stage >= 99 and sstage >= 99 and not cfg.get("noprompt", False):
        assert wstate["idx"] == len(wstate["plan"]), (wstate["idx"], len(wstate["plan"]))
    if dbgname is not None:
        dbg = B.dout("dbg", [128, 4096])
        DBG = X[:].rearrange("p a b -> p (a b)")
        srcs = dict(XT=XT[:], RQT=RQT, RKm=RKm[:, 0], SQT=SQT[:], SKm=SKm[:], Vb=Vb[:, :, :], SRG=SRG[:], MQT=MQT, GT=GT[:, 0:8, :],
                    GRT=GRT[:], SWOT=SWOT[:], MOT=MOT[:], MT=MT, X=X[:], RKw=RKw[:], MKT=MKT[0][:], MVb=MVb[0][:],
                    MEMT=MEMT, HT=HT[:, 0:8, :], S=Sst[0][:], E=E[:], EM=EM[:])[dbgname]
        n = 1
        for s_ in srcs.shape[1:]:
            n *= s_
        dims = "abcd"[:len(srcs.shape) - 1]
        flat = srcs.rearrange("p %s -> p (%s)" % (" ".join(dims), " ".join(dims))) if len(dims) > 1 else srcs
        if dbgname != "X":
            B.cp(DBG[:, 0:n], flat)
        B.dma(dbg[:, 0:n], DBG[:, 0:n], "o_dbg", final=True)
    T.emit(B.final_ops)
    return B


_CACHE = {}


def _core_inputs(inp, core, nsc, hc):
    f = lambda a: np.ascontiguousarray(np.asarray(a, dtype=np.float32))
    b = core % BATCH
    L = core // BATCH
    sb0 = core * SB_PER_CORE
    sl = slice(sb0, sb0 + SB_PER_CORE)
    nslot = nsc + 2
    m = {
        "x_p": f(inp["x_prompt"][b, :nsc * 512]),
        "mem_p": f(inp["mem_prompt"][b]),
        "x_s": f(np.asarray(inp["x_sample"])[sl].reshape(SB_PER_CORE * DEC_SEQ, D)),
        "st_ret": f(np.asarray(inp["state_ret"])[:, sl]),
        "c_sk": f(np.asarray(inp["cache_swa_k"])[:, sl].reshape(DEPTH, SB_PER_CORE, 128, 128)),
        "c_sv": f(np.asarray(inp["cache_swa_v"])[:, sl].reshape(DEPTH, SB_PER_CORE, 128, 128)),
        "c_mk": f(np.asarray(inp["cache_mem_k"])[:, sl].reshape(DEPTH, SB_PER_CORE, NMEM, 512)),
        "c_mv": f(np.asarray(inp["cache_mem_v"])[:, sl].reshape(DEPTH, SB_PER_CORE, NMEM, 512)),
        "sinks": f(inp["attn_sinks"]),
        "gn_g": f(np.asarray(inp["ret_gn_g"]).reshape(DEPTH, 1024)),
    }
    chunk_of = []
    for s in range(nslot):
        for c in range(4):
            ch = (4 * s + c) if L == 0 else (4 * (s - 2) + c)
            if ch < 0 or ch >= nsc * 4:
                ch = 0
            chunk_of.append(ch)
    for nm, key in (("r_cos", "cosp"), ("r_sin", "sinp"), ("r_nsin", "nsinp")):
        m[nm] = np.ascontiguousarray(hc[key][:, chunk_of, :])
    sel = np.zeros((128, 2), np.float32)
    sel[:, L] = 1.0
    m["sel"] = sel
    m["flagneg"] = np.full((128, 512), NEG if L == 1 else 0.0, np.float32)
    return m


_OWN = (("wo_in", "w_in"), ("wo_br_ret", "w_br_ret"), ("wo_br_swa", "w_br_swa"), ("wo_br_mem", "w_br_mem"),
        ("wo_out", "w_out"), ("wo_mem_kv", "w_mem_kv"), ("wo_up", "w_up"), ("wo_down", "w_down"),
        ("o_ln1_g", "ln1_g"), ("o_ln1_b", "ln1_b"), ("o_ln2_g", "ln2_g"), ("o_ln2_b", "ln2_b"))
_FULL = ("w_in", "w_br_ret", "w_br_swa", "w_br_mem", "w_out", "w_mem_kv", "ln1_g", "ln1_b", "w_up", "w_down",
         "ln2_g", "ln2_b")


def run_cores(inp, cfg, cores):
    key = tuple(sorted(cfg.items()))
    if key not in _CACHE:
        _CACHE[key] = build(cfg)
    B = _CACHE[key]
    hc = host_consts()
    f = lambda a: np.ascontiguousarray(np.asarray(a, dtype=np.float32))
    full = {k: f(inp[k]) for k in _FULL}
    ownl = []
    for L in range(DEPTH):
        d = {dst: np.ascontiguousarray(full[srck][L]) for dst, srck in _OWN}
        d["o_sinks"] = f(inp["attn_sinks"])[L:L + 1]
        d["o_gn_g"] = f(np.asarray(inp["ret_gn_g"]).reshape(DEPTH, 1024))[L]
        ownl.append(d)
    in_maps = []
    for core in cores:
        m = _core_inputs(inp, core, cfg["nsc"], hc)
        m.update(full)
        m.update(ownl[core // BATCH])
        for k, v in hc.items():
            m["c_" + k] = v
        in_maps.append(m)
    res = run_bass_kernel_spmd(B.nc, in_maps, core_ids=list(range(len(cores))))
    return res.results


def assemble(r, nsc):
    S = nsc * 512
    y_p = np.stack([r[BATCH + b]["y_p"][2 * 512:2 * 512 + S] for b in range(BATCH)]).astype(np.float32)
    y_s = np.concatenate([r[c]["y_s"].reshape(SB_PER_CORE, DEC_SEQ, D) for c in range(NCORES)]).astype(np.float32)

    def per_layer(name, shape, versioned):
        out = []
        for L in range(DEPTH):
            row = []
            for b in range(BATCH):
                a = r[L * BATCH + b][name]
                a = a[L] if versioned else a
                row.append(np.asarray(a).reshape(shape))
            out.append(np.stack(row))
        return np.stack(out).astype(np.float32)
    rs_p = per_layer("rs_p", (RH, RDK, RDV), True)
    sk_p = per_layer("sk_p", (128, SKV, SHD), True)
    sv_p = per_layer("sv_p", (128, SKV, SHD), True)
    mk_p = per_layer("mk_p", (NMEM, MH, MHD), False)
    mv_p = per_layer("mv_p", (NMEM, MH, MHD), False)
    rs_s = np.concatenate([r[c]["rs_s"] for c in range(NCORES)], axis=1).astype(np.float32)
    sk_s = np.concatenate([r[c]["sk_s"].reshape(DEPTH, SB_PER_CORE, 128, SKV, SHD) for c in range(NCORES)],
                          axis=1).astype(np.float32)
    sv_s = np.concatenate([r[c]["sv_s"].reshape(DEPTH, SB_PER_CORE, 128, SKV, SHD) for c in range(NCORES)],
                          axis=1).astype(np.float32)
    return (y_p, y_s, rs_p, sk_p, sv_p, mk_p, mv_p, rs_s, sk_s, sv_s)


def kernel(**inp):
    cfg = dict(nsc=SEQ // 512, nlay=DEPTH, sample=True)
    r = run_cores(inp, cfg, list(range(NCORES)))
    return assemble(r, cfg["nsc"])
```

```python
import numpy as np
import concourse.bass as bass
import concourse.mybir as mybir
from concourse.bass_utils import run_bass_kernel_spmd

F32 = mybir.dt.float32
BF16 = mybir.dt.bfloat16
AF = mybir.ActivationFunctionType
ALU = mybir.AluOpType

D = 1024
DEPTH = 2
SEQ = 4096
BATCH = 4
DEC_BATCH = 128
DEC_SEQ = 8
PAST_LEN = 16384
RH, RDK, RDV = 8, 64, 128
SH, SKV, SHD = 8, 2, 64
MH, MHD, NMEM = 4, 128, 256
DFF = 4096
IN_W = 7424
ALPHA = (2 * DEPTH) ** 0.25
LN_EPS = 1e-5
GN_EPS = 1e-5
NEG = -2000.0
NCORES = 8
SB_PER_CORE = DEC_BATCH // NCORES


def _dsize(dt):
    return mybir.dt.size(dt)


class Tracker:
    def __init__(self, nc):
        self.nc = nc
        self.ops = []
        self.psum_last = {}
        self.hist = {}
        self.phase = 0
        self.eng_objs = {"pe": nc.tensor, "dve": nc.vector, "act": nc.scalar,
                         "pool": nc.gpsimd, "sp": nc.sync}

    @staticmethod
    def box(ap):
        t = ap.tensor
        name = t.name
        dims = ap.ap
        esz = _dsize(ap.dtype)
        space = str(ap.space)
        if "DRAM" in space.upper() or "HBM" in space.upper() or "dram" in space:
            lo = ap.offset
            hi = lo + sum((c - 1) * s for s, c in dims if c > 0)
            return name, (0, 0, lo * esz, (hi + 1) * esz), True
        fstride = dims[0][0]
        p0 = ap.offset // fstride
        f0 = ap.offset % fstride
        p1 = p0 + dims[0][1] - 1
        f1 = f0 + sum((c - 1) * s for s, c in dims[1:])
        return name, (p0, p1, f0 * esz, (f1 + 1) * esz), False

    @staticmethod
    def overlap(a, b):
        return not (a[1] < b[0] or b[1] < a[0] or a[3] <= b[2] or b[3] <= a[2])

    @staticmethod
    def contains(a, b):
        return a[0] <= b[0] and a[1] >= b[1] and a[2] <= b[2] and a[3] >= b[3]

    def add(self, eng, fn, w=(), r=(), dma_key=None, untracked=(), inc=None):
        opid = len(self.ops)
        deps = set()
        is_dma = dma_key is not None
        rboxes = [self.box(a) for a in r]
        wboxes = [self.box(a) for a in w]
        psum_names = set()
        for name, bx, isdram in rboxes + wboxes:
            if name.startswith("PB") or name.startswith("PT"):
                psum_names.add(name)
        for name in psum_names:
            last = self.psum_last.setdefault(name, {})
            for e2, o2 in last.items():
                if e2 != eng:
                    deps.add(o2)
            last[eng] = opid
        rboxes = [b for b in rboxes if b[0] not in psum_names]
        wboxes = [b for b in wboxes if b[0] not in psum_names]
        for name, bx, isdram in rboxes:
            if name in untracked:
                continue
            h = self.hist.setdefault(name, [])
            for (b2, o2, isw, e2, d2) in h:
                if isw and self.overlap(bx, b2):
                    deps.add(o2)
        for name, bx, isdram in wboxes:
            if name in untracked:
                continue
            h = self.hist.setdefault(name, [])
            for (b2, o2, isw, e2, d2) in h:
                if self.overlap(bx, b2):
                    deps.add(o2)
        for name, bx, isdram in rboxes:
            if name in untracked:
                continue
            h = self.hist[name]
            if not is_dma:
                h[:] = [e for e in h if not ((not e[2]) and e[3] == eng and (not e[4])
                                             and self.contains(bx, e[0]))]
            h.append((bx, opid, False, eng, is_dma))
        for name, bx, isdram in wboxes:
            if name in untracked:
                continue
            h = self.hist[name]
            h[:] = [e for e in h if not self.contains(bx, e[0])]
            h.append((bx, opid, True, eng, is_dma))
        deps.discard(opid)
        self.ops.append(dict(eng=eng, fn=fn, deps=deps, dma_key=dma_key, phase=self.phase,
                             signal=False, inc_override=inc))
        return opid

    def emit(self, final_wait_ops):
        nc = self.nc
        ops = self.ops
        for o in ops:
            if o["eng"] == "pe" and o["dma_key"] is None:
                o["deps"] = {d for d in o["deps"]
                             if not (ops[d]["eng"] == "pe" and ops[d]["dma_key"] is None)}
        for o in ops:
            for d in o["deps"]:
                ops[d]["signal"] = True
        for d in final_wait_ops:
            ops[d]["signal"] = True
        for o in ops:
            if o["dma_key"] is not None:
                o["signal"] = True
        sem_keys = []
        counts = {}
        for o in ops:
            if not o["signal"]:
                continue
            if o["dma_key"] is not None:
                k = ("dma", o["dma_key"])
                inc = 16 if o["inc_override"] is None else o["inc_override"]
            else:
                k = (o["eng"], o["phase"])
                inc = 1
            if k not in counts:
                counts[k] = 0
                sem_keys.append(k)
            counts[k] += inc
            o["sem"] = k
            o["val"] = counts[k]
            o["inc"] = inc
        rng = bass.get_kernel_semaphore_range()
        assert len(sem_keys) <= len(rng) - 2, f"too many semaphores: {len(sem_keys)}"
        sems = {}
        self._sem_cms = []
        for k in sem_keys:
            cm = nc.semaphore(("s_%s_%s" % k).replace("-", "_"))
            s = cm.__enter__()
            self._sem_cms.append(cm)
            sems[k] = s
        waited = {}
        nwait = 0
        issued = {}
        for o in ops:
            eobj = self.eng_objs[o["eng"]]
            need = {}
            for d in o["deps"]:
                od = ops[d]
                k = od["sem"]
                if od["dma_key"] is not None:
                    need[k] = max(need.get(k, 0), issued[k])
                else:
                    need[k] = max(need.get(k, 0), od["val"])
            for k, v in need.items():
                wk = (o["eng"], k)
                if waited.get(wk, 0) >= v:
                    continue
                eobj.wait_ge(sems[k], v)
                waited[wk] = v
                nwait += 1
            ins = o["fn"]()
            if o["signal"]:
                ins.then_inc(sems[o["sem"]], o["inc"])
                if o["dma_key"] is not None:
                    issued[o["sem"]] = o["val"]
        need = {}
        for d in final_wait_ops:
            od = ops[d]
            need[od["sem"]] = max(need.get(od["sem"], 0), od["val"])
        for k, v in need.items():
            nc.sync.wait_ge(sems[k], v)
        self.stats = dict(nops=len(ops), nwait=nwait, nsem=len(sem_keys),
                          maxcount=max(counts.values()) if counts else 0)

    def close(self):
        for cm in reversed(self._sem_cms):
            cm.__exit__(None, None, None)


def ret_gammas():
    return (1.0 - np.exp2(-5.0 - np.arange(RH, dtype=np.float64)))


def host_consts():
    c = {}
    c["ident"] = np.eye(128, dtype=np.float32)
    half = 32
    inv = np.power(np.float32(10000.0), -np.arange(half, dtype=np.float32) / np.float32(half)).astype(np.float32)
    pos = np.arange(SEQ, dtype=np.float32)
    ang = (pos[:, None] * inv[None, :]).astype(np.float32)
    cosp = np.cos(ang).astype(np.float32).reshape(SEQ // 128, 128, half).transpose(1, 0, 2)
    sinp = np.sin(ang).astype(np.float32).reshape(SEQ // 128, 128, half).transpose(1, 0, 2)
    c["cosp"] = np.ascontiguousarray(cosp)
    c["sinp"] = np.ascontiguousarray(sinp)
    c["nsinp"] = np.ascontiguousarray(-sinp)
    poss = (PAST_LEN + (np.arange(128) % DEC_SEQ)).astype(np.float32)
    angs = (poss[:, None] * inv[None, :]).astype(np.float32)
    c["coss"] = np.cos(angs).astype(np.float32).reshape(128, 1, half)
    c["sins"] = np.sin(angs).astype(np.float32).reshape(128, 1, half)
    c["nsins"] = (-c["sins"]).astype(np.float32)
    g = ret_gammas()
    lg = np.log(g)
    j = np.arange(128)[:, None]
    i = np.arange(128)[None, :]
    dec = np.zeros((128, RH, 128), np.float64)
    decs = np.zeros((128, RH, 128), np.float64)
    for h in range(RH):
        dec[:, h, :] = np.where(i >= j, np.exp(lg[h] * np.maximum(i - j, 0)), 0.0) * RDK ** -0.5
        same = (i // DEC_SEQ) == (j // DEC_SEQ)
        decs[:, h, :] = np.where((i >= j) & same, np.exp(lg[h] * np.maximum(i - j, 0)), 0.0) * RDK ** -0.5
    c["decp"] = dec.astype(np.float32)
    c["decs"] = decs.astype(np.float32)
    gq = np.zeros((128, 4, 128), np.float64)
    gqs = np.zeros((128, 4, 128), np.float64)
    gl = np.zeros((128, 4), np.float64)
    gls = np.zeros((128, 4), np.float64)
    ii = np.arange(128)
    for pr in range(4):
        for hh in range(2):
            h = 2 * pr + hh
            gq[hh * 64:(hh + 1) * 64, pr, :] = np.exp(lg[h] * (ii + 1.0))[None, :]
            gqs[hh * 64:(hh + 1) * 64, pr, :] = np.exp(lg[h] * ((ii % DEC_SEQ) + 1.0))[None, :]
            gl[hh * 64:(hh + 1) * 64, pr] = np.exp(lg[h] * 128.0)
            gls[hh * 64:(hh + 1) * 64, pr] = np.exp(lg[h] * float(DEC_SEQ))
    c["gqp"] = gq.astype(np.float32)
    c["gqs"] = gqs.astype(np.float32)
    c["glp"] = gl.astype(np.float32)
    c["gls"] = gls.astype(np.float32)
    wend = np.exp(lg[None, :] * (127.0 - ii[:, None])) * RDK ** -0.5
    wends = np.exp(lg[None, :] * (DEC_SEQ - 1.0 - (ii[:, None] % DEC_SEQ))) * RDK ** -0.5
    c["wendp"] = wend.astype(np.float32)
    c["wends"] = wends.astype(np.float32)
    own = np.where(j <= i, 0.0, NEG)
    prev = np.where(j > i, 0.0, NEG)
    c["mown"] = np.tile(own, (1, 4)).astype(np.float32)
    c["mprev"] = np.tile(prev, (1, 4)).astype(np.float32)
    same = (i // DEC_SEQ) == (j // DEC_SEQ)
    news = np.where(same & (j <= i), 0.0, NEG)
    c["mnews"] = np.tile(news, (1, 4)).astype(np.float32)
    kk = np.arange(128)[:, None]
    t8 = np.arange(DEC_SEQ)[None, :]
    mc = np.where(kk >= t8 + 1, 0.0, NEG)
    c["mcache"] = np.tile(mc, (1, 64)).astype(np.float32)
    bm = (np.arange(128)[:, None] // DEC_SEQ == np.arange(SB_PER_CORE)[None, :]).astype(np.float32)
    c["bmrow"] = bm
    return c


CONST_SHAPES = None


class Builder:
    def __init__(self, cfg):
        self.cfg = cfg
        self.nc = bass.Bass("TRN2", target_bir_lowering=False)
        self.T = Tracker(self.nc)
        self.cms = []
        self.final_ops = []

    def sb(self, name, shape, dt):
        cm = self.nc.sbuf_tensor(name, list(shape), dt)
        t = cm.__enter__()
        self.cms.append(cm)
        return t

    def ps(self, name, shape, dt):
        cm = self.nc.psum_tensor(name, list(shape), dt)
        t = cm.__enter__()
        self.cms.append(cm)
        return t

    def din(self, name, shape, dt=F32):
        return self.nc.dram_tensor(name, list(shape), dt, kind="ExternalInput").ap()

    def dout(self, name, shape, dt=F32):
        return self.nc.dram_tensor(name, list(shape), dt, kind="ExternalOutput").ap()

    def mm(self, out, lhsT, rhs, start, stop, skip=False):
        nc = self.nc
        if skip:
            return self.T.add("pe", lambda: nc.tensor.matmul(out, lhsT=lhsT, rhs=rhs, start=start, stop=stop,
                                                             skip_group_check=True),
                              w=[out], r=[lhsT, rhs, out])
        return self.T.add("pe", lambda: nc.tensor.matmul(out, lhsT=lhsT, rhs=rhs, start=start, stop=stop),
                          w=[out], r=[lhsT, rhs] + ([] if start else [out]))

    def trf(self, out, in_, identf):
        nc = self.nc
        return self.T.add("pe", lambda: nc.tensor.transpose(out=out, in_=in_, identity=identf),
                          w=[out], r=[in_, identf])

    def tr(self, out, in_):
        nc = self.nc
        ident = self.identb[:]
        return self.T.add("pe", lambda: nc.tensor.transpose(out=out, in_=in_, identity=ident),
                          w=[out], r=[in_, ident])

    def act(self, out, in_, func, scale=1.0, bias=None):
        nc = self.nc
        r = [in_]
        kw = {}
        if not isinstance(scale, float):
            r.append(scale)
        if bias is not None:
            kw["bias"] = bias
            if not isinstance(bias, float):
                r.append(bias)
        return self.T.add("act", lambda: nc.scalar.activation(out=out, in_=in_, func=func, scale=scale, **kw),
                          w=[out], r=r)

    def tt(self, out, in0, in1, op, eng="dve"):
        nc = self.nc
        e = nc.vector if eng == "dve" else nc.gpsimd
        return self.T.add(eng, lambda: e.tensor_tensor(out=out, in0=in0, in1=in1, op=op), w=[out], r=[in0, in1])

    def ts(self, out, in0, s1, op0, s2=None, op1=None, eng="dve"):
        nc = self.nc
        e = nc.vector if eng == "dve" else nc.gpsimd
        r = [in0] + [s for s in (s1, s2) if s is not None and not isinstance(s, float)]
        if op1 is None:
            return self.T.add(eng, lambda: e.tensor_scalar(out=out, in0=in0, scalar1=s1, scalar2=None, op0=op0),
                              w=[out], r=r)
        return self.T.add(eng, lambda: e.tensor_scalar(out=out, in0=in0, scalar1=s1, scalar2=s2, op0=op0, op1=op1),
                          w=[out], r=r)

    def stt(self, out, in0, scalar, in1, op0, op1):
        nc = self.nc
        r = [in0, in1] + ([] if isinstance(scalar, float) else [scalar])
        return self.T.add("dve", lambda: nc.vector.scalar_tensor_tensor(out=out, in0=in0, scalar=scalar, in1=in1,
                                                                         op0=op0, op1=op1), w=[out], r=r)

    def cp(self, out, in_, eng="dve"):
        nc = self.nc
        if eng == "act":
            return self.T.add("act", lambda: nc.scalar.copy(out=out, in_=in_), w=[out], r=[in_])
        e = nc.vector if eng == "dve" else nc.gpsimd
        return self.T.add(eng, lambda: e.tensor_copy(out=out, in_=in_), w=[out], r=[in_])

    def memset(self, ap, val, eng="dve"):
        nc = self.nc
        e = nc.vector if eng == "dve" else nc.gpsimd
        return self.T.add(eng, lambda: e.memset(ap, val), w=[ap])

    def dma(self, out, in_, key, q="sp", final=False):
        nc = self.nc
        e = {"sp": nc.sync, "pool": nc.gpsimd, "act": nc.scalar}[q]
        if key is None or not key.startswith("!"):
            on, inn = out.tensor.name, in_.tensor.name
            key = ("st_" + inn) if on in self.dram_names else ("ld_" + on)
        op = self.T.add(q, lambda: e.dma_start(out=out, in_=in_), w=[out], r=[in_], dma_key=key,
                        untracked=self.untracked)
        if final:
            self.final_ops.append(op)
        return op

    def bnstats(self, out, in_):
        nc = self.nc
        return self.T.add("dve", lambda: nc.vector.bn_stats(out=out, in_=in_), w=[out], r=[in_])

    def bnaggr(self, out, in_):
        nc = self.nc
        return self.T.add("dve", lambda: nc.vector.bn_aggr(out=out, in_=in_), w=[out], r=[in_])

    def recip(self, out, in_):
        nc = self.nc
        return self.T.add("dve", lambda: nc.vector.reciprocal(out=out, in_=in_), w=[out], r=[in_])


def V(t, off, dims):
    f = 1
    for s in t.shape[1:]:
        f *= s
    return bass.AP(t, off, [[f, 128]] + [list(d) for d in dims])


def VP(t, p0, pn, off, dims):
    f = 1
    for s in t.shape[1:]:
        f *= s
    return bass.AP(t, p0 * f + off, [[f, pn]] + [list(d) for d in dims])


def AV(view, off, dims, p0=0, pn=128):
    f = view.ap[0][0]
    base = view.offset
    return bass.AP(view.tensor, base + p0 * f + off, [[f, pn]] + [list(d) for d in dims])


def build(cfg):
    nsc = cfg["nsc"]
    nlay = cfg["nlay"]
    stage = cfg.get("stage", 99)
    sstage = cfg.get("sstage", 99)
    dbgname = cfg.get("dbg", None)
    do_sample = cfg["sample"]
    B = Builder(cfg)
    nc = B.nc
    T = B.T
    NTOK = nsc * 512
    x_p = B.din("x_p", [NTOK, D])
    mem_p = B.din("mem_p", [NMEM, D])
    x_s = B.din("x_s", [128, D])
    st_ret = B.din("st_ret", [DEPTH, SB_PER_CORE, RH, RDK, RDV])
    c_sk = B.din("c_sk", [DEPTH, SB_PER_CORE, 128, 128])
    c_sv = B.din("c_sv", [DEPTH, SB_PER_CORE, 128, 128])
    c_mk = B.din("c_mk", [DEPTH, SB_PER_CORE, NMEM, 512])
    c_mv = B.din("c_mv", [DEPTH, SB_PER_CORE, NMEM, 512])
    w_in = B.din("w_in", [DEPTH, D, IN_W])
    w_br_ret = B.din("w_br_ret", [DEPTH, 1024, D])
    w_br_swa = B.din("w_br_swa", [DEPTH, 512, D])
    w_br_mem = B.din("w_br_mem", [DEPTH, 512, D])
    w_out = B.din("w_out", [DEPTH, D, D])
    w_mem_kv = B.din("w_mem_kv", [DEPTH, D, 1024])
    sinks = B.din("sinks", [DEPTH, SH])
    gn_g = B.din("gn_g", [DEPTH, 1024])
    ln1_g = B.din("ln1_g", [DEPTH, D])
    ln1_b = B.din("ln1_b", [DEPTH, D])
    w_up = B.din("w_up", [DEPTH, D, DFF])
    w_down = B.din("w_down", [DEPTH, DFF, D])
    ln2_g = B.din("ln2_g", [DEPTH, D])
    ln2_b = B.din("ln2_b", [DEPTH, D])
    own = dict(w_in=B.din("wo_in", [D, IN_W]), w_br_ret=B.din("wo_br_ret", [1024, D]),
               w_br_swa=B.din("wo_br_swa", [512, D]), w_br_mem=B.din("wo_br_mem", [512, D]),
               w_out=B.din("wo_out", [D, D]), w_mem_kv=B.din("wo_mem_kv", [D, 1024]),
               sinks=B.din("o_sinks", [1, SH]), gn_g=B.din("o_gn_g", [1024]), ln1_g=B.din("o_ln1_g", [D]),
               ln1_b=B.din("o_ln1_b", [D]), w_up=B.din("wo_up", [D, DFF]), w_down=B.din("wo_down", [DFF, D]),
               ln2_g=B.din("o_ln2_g", [D]), ln2_b=B.din("o_ln2_b", [D]))
    own_names = [v.tensor.name for v in own.values()]
    sinks_full = sinks
    w_in = [w_in[0], w_in[1], own["w_in"]]
    w_br_ret = [w_br_ret[0], w_br_ret[1], own["w_br_ret"]]
    w_br_swa = [w_br_swa[0], w_br_swa[1], own["w_br_swa"]]
    w_br_mem = [w_br_mem[0], w_br_mem[1], own["w_br_mem"]]
    w_out = [w_out[0], w_out[1], own["w_out"]]
    w_mem_kv = [w_mem_kv[0], w_mem_kv[1], own["w_mem_kv"]]
    w_up = [w_up[0], w_up[1], own["w_up"]]
    w_down = [w_down[0], w_down[1], own["w_down"]]
    gn_g = [gn_g[0], gn_g[1], own["gn_g"]]
    ln1_g = [ln1_g[0], ln1_g[1], own["ln1_g"]]
    ln1_b = [ln1_b[0], ln1_b[1], own["ln1_b"]]
    ln2_g = [ln2_g[0], ln2_g[1], own["ln2_g"]]
    ln2_b = [ln2_b[0], ln2_b[1], own["ln2_b"]]
    NSLOT = nsc + 2
    sel_d = B.din("sel", [128, 2])
    flag_d = B.din("flagneg", [128, 512])
    r_cos = B.din("r_cos", [128, NSLOT * 4, 32])
    r_sin = B.din("r_sin", [128, NSLOT * 4, 32])
    r_nsin = B.din("r_nsin", [128, NSLOT * 4, 32])
    cc_src = [nc.dram_tensor("cc_src%d" % i, [512, D], F32, kind="Internal").ap() for i in range(2)]
    cc_dst = [nc.dram_tensor("cc_dst%d" % i, [2 * 512, D], F32, kind="Internal").ap() for i in range(2)]
    hc = host_consts()
    cd = {k: B.din("c_" + k, list(v.shape)) for k, v in hc.items()}
    y_p = B.dout("y_p", [NSLOT * 512, D])
    y_s = B.dout("y_s", [128, D])
    rs_p = B.dout("rs_p", [2, RH, RDK, RDV])
    sk_p = B.dout("sk_p", [2, 128, 128])
    sv_p = B.dout("sv_p", [2, 128, 128])
    mk_p = B.dout("mk_p", [NMEM, 512])
    mv_p = B.dout("mv_p", [NMEM, 512])
    rs_s = B.dout("rs_s", [DEPTH, SB_PER_CORE, RH, RDK, RDV])
    sk_s = B.dout("sk_s", [DEPTH, SB_PER_CORE, 128, 128])
    sv_s = B.dout("sv_s", [DEPTH, SB_PER_CORE, 128, 128])
    B.dram_names = set(t.tensor.name for t in [y_p, y_s, rs_p, sk_p, sv_p, mk_p, mv_p, rs_s, sk_s, sv_s])
    B.untracked = set(n.tensor.name for n in
                      [x_p, mem_p, x_s, st_ret, c_sk, c_sv, c_mk, c_mv, w_in[0], w_br_ret[0], w_br_swa[0], w_br_mem[0],
                       w_out[0],
                       w_mem_kv[0], sinks, gn_g[0], ln1_g[0], ln1_b[0], w_up[0], w_down[0], ln2_g[0], ln2_b[0],
                       sel_d, flag_d, r_cos, r_sin, r_nsin]
                      + list(cd.values())) | set(own_names)

    sb = B.sb
    B.identb = sb("identb", [128, 128], BF16)
    onesb = sb("onesb", [128, 128], BF16)
    X = sb("X", [128, 4, D], F32)
    XT = sb("XT", [128, 8, 512], BF16)
    Xb = sb("Xb", [128, D], BF16)
    ARENA = sb("ARENA", [128, 32 * 512], BF16)
    HT = ARENA[:].rearrange("p (a b) -> p a b", a=32)
    GT = ARENA[:, 0:24 * 512].rearrange("p (a b) -> p a b", a=24)
    MQT = ARENA[:, 24 * 512:28 * 512].rearrange("p (a b) -> p a b", a=4)
    RQT = ARENA[:, 28 * 512:32 * 512].rearrange("p (a b) -> p a b", a=4)
    PV_ = dict(HT=HT, GT=GT, MQT=MQT, RQT=RQT)
    SV_ = dict(HT=ARENA[:, 0:32 * 128].rearrange("p (a b) -> p a b", a=32),
               GT=ARENA[:, 0:24 * 128].rearrange("p (a b) -> p a b", a=24),
               MQT=ARENA[:, 24 * 128:28 * 128].rearrange("p (a b) -> p a b", a=4),
               RQT=ARENA[:, 28 * 128:32 * 128].rearrange("p (a b) -> p a b", a=4))
    a0 = 32 * 128
    RKwX = ARENA[:, a0:a0 + 4096].rearrange("p (r c) -> p r c", r=8)
    S0bg = ARENA[:, a0 + 4096:a0 + 6144].rearrange("p (b r e) -> p b r e", b=4, r=4)
    Ec = ARENA[:, a0 + 6144:a0 + 7168].rearrange("p (b g c) -> p b g c", b=16, g=2)
    KcT = ARENA[:, a0 + 7168:a0 + 9216].rearrange("p (b k) -> p b k", b=16)
    Kraw = ARENA[:, a0 + 9216:a0 + 11264].rearrange("p (b k) -> p b k", b=16)
    SQTs = ARENA[:, a0 + 11264:a0 + 12288].rearrange("p (g c) -> p g c", g=2)
    S0g = X[:, 1:3, :].rearrange("p a (b e) -> p (a b) e", e=128).rearrange("p (b r) e -> p b r e", b=4)
    Vcp = X[:, 3, :].bitcast(BF16).rearrange("p (b g h c) -> p b g h c", b=4, g=2, h=2)
    identF = sb("identF", [128, 128], F32)
    BMR = sb("BMR", [128, 16], F32)
    RKm = sb("RKm", [128, 4, 8, 128], BF16)
    SQT = sb("SQT", [128, 4, 512], BF16)
    SKm = sb("SKm", [128, 5, 2, 128], BF16)
    RKw = sb("RKw", [128, 4, 512], BF16)
    Vb = sb("Vb", [128, 4, 1024], BF16)
    MT = Vb[:].rearrange("p a (k t) -> p (a k) t", k=2)
    SRG = sb("SRG", [128, 4, 1024], BF16)
    SVp = sb("SVp", [128, 5, 2, 2, 128], BF16)
    ONp = sb("ONp", [128, 2, 128], BF16)
    GRT = sb("GRT", [128, 8, 512], BF16)
    SWOT = sb("SWOT", [128, 4, 512], BF16)
    MOT = sb("MOT", [128, 4, 512], BF16)
    T1 = sb("T1", [128, 512], F32)
    T2 = sb("T2", [128, 512], F32)
    RF = sb("RF", [128, 512], F32)
    RQb = sb("RQb", [128, 512], BF16)
    ST = sb("ST", [128, 8, 128], BF16)
    RQs2 = sb("RQs2", [128, 8, 128], BF16)
    Y = sb("Y", [128, 8, 128], F32)
    GR = sb("GR", [128, 1024], BF16)
    STAT = sb("STAT", [128, 8, 6], F32)
    MV = sb("MV", [128, 8, 2], F32)
    E = sb("E", [128, 4, 512], BF16)
    MEMT = E[:].rearrange("p a (k m) -> p (a k) m", k=2)
    identf = T2[:, 0:128]
    MSKf = T1
    DEN = sb("DEN", [128, 512], F32)
    LNB = sb("LNB", [128, 2, D], F32)
    GNG = sb("GNG", [128, 1024], F32)
    NW = 3
    WR = [sb("WR%d" % i, [128, 8, 512], BF16) for i in range(NW)]
    ROPE = sb("ROPE", [128, 3, 4, 32], F32)
    DEC = sb("DEC", [128, 8, 128], F32)
    GQ = sb("GQ", [128, 4, 128], F32)
    WEND = sb("WEND", [128, 8], F32)
    GL = sb("GL", [128, 4], F32)
    MOWN = sb("MOWN", [128, 512], BF16)
    MPREV = sb("MPREV", [128, 512], BF16)
    SINKE = sb("SINKE", [128, 3, 4], F32)
    Sst = [sb("Sst", [128, 4, 128], F32)] * 3
    Sbf = [sb("Sbf", [128, 4, 128], BF16)] * 3
    SKTprev = [sb("SKTp", [128, 2, 128], BF16)] * 3
    SVprev = [sb("SVpv", [128, 2, 2, 128], BF16)] * 3
    MKT = [sb("MKT", [128, 4, 256], BF16)] * 3
    MVb = [sb("MVb", [128, 2, 512], BF16)] * 3
    SEL = sb("SEL", [128, 2], F32)
    XbN = sb("XbN", [128, 4, D], BF16)
    XbS = Xb[:].bitcast(F32)
    FLAGN = sb("FLAGN", [128, 512], BF16)
    SKf = sb("SKf", [128, 256], F32)
    PB = [B.ps("PB%d" % i, [128, 512], F32) for i in range(6)]
    PT = [B.ps("PT%d" % i, [128, 1024], BF16) for i in range(2)]

    B.dma(identf, cd["ident"], "c0")
    B.cp(B.identb[:], identf)
    B.memset(onesb[:], 1.0)
    B.memset(ONp[:], 0.0)
    B.memset(ONp[:, 0, 0:64], 1.0)
    B.memset(ONp[:, 1, 64:128], 1.0)
    B.memset(SVp[:], 0.0)
    B.memset(RKm[:], 0.0)
    B.memset(SKm[:], 0.0)
    B.memset(RQs2[:], 0.0)
    for l in range(1):
        B.memset(SVprev[l][:], 0.0)
        B.memset(SKTprev[l][:], 0.0)
        B.memset(Sst[l][:], 0.0)
        B.memset(Sbf[l][:], 0.0)
    B.dma(SEL[:], sel_d, None)
    B.dma(MSKf[:], flag_d, None)
    B.cp(FLAGN[:], MSKf[:])
    B.dma(MSKf[:], cd["mown"], "c0")
    B.cp(MOWN[:], MSKf[:])
    B.dma(MSKf[:], cd["mprev"], "c0")
    B.cp(MPREV[:], MSKf[:])
    SKRAW = sb("SKRAW", [128, 3, SH], F32)
    B.dma(SKRAW[:, 0:2, :], bass.AP(sinks_full.tensor, 0, [[0, 128], [SH, DEPTH], [1, SH]]), "c0")
    B.dma(SKRAW[:, 2, :], bass.AP(own["sinks"].tensor, 0, [[0, 128], [1, SH]]), "c0")
    for hh in range(2):
        B.act(SINKE[hh * 64:(hh + 1) * 64, :, :],
              AV(SKRAW[:], hh, [[SH, 3], [2, 4]], p0=hh * 64, pn=64), AF.Exp)

    def load_prompt_consts():
        B.dma(DEC[:], cd["decp"], "c0")
        B.dma(GQ[:], cd["gqp"], "c0")
        B.dma(WEND[:], cd["wendp"], "c0")
        B.dma(GL[:], cd["glp"], "c0")

    wstate = dict(idx=0, plan=[])

    def wplan_for_pass(l, first):
        pl = []
        if first:
            pl.append((w_mem_kv[l][:, 0:512], 8, 512))
            pl.append((w_mem_kv[l][:, 512:1024], 8, 512))
        for j in range(7):
            pl.append((w_in[l][:, j * 512:(j + 1) * 512], 8, 512))
        pl.append((w_in[l][:, 3584:3840], 8, 256))
        for jj in range(7):
            pl.append((w_in[l][:, 3840 + jj * 512:3840 + (jj + 1) * 512], 8, 512))
        for cb in range(2):
            pl.append((w_br_ret[l][:, cb * 512:(cb + 1) * 512], 8, 512))
            pl.append(((w_br_swa[l][:, cb * 512:(cb + 1) * 512], w_br_mem[l][:, cb * 512:(cb + 1) * 512]), 8, 512))
        for cb in range(2):
            pl.append((w_out[l][:, cb * 512:(cb + 1) * 512], 8, 512))
        for jb in range(8):
            pl.append((w_up[l][:, jb * 512:(jb + 1) * 512], 8, 512))
        for cb in range(2):
            for rb in range(4):
                pl.append((w_down[l][rb * 1024:(rb + 1) * 1024, cb * 512:(cb + 1) * 512], 8, 512))
        return pl

    def w_issue(i):
        src, nkt, ncols = wstate["plan"][i]
        slot = WR[i % NW]
        if isinstance(src, tuple):
            for half, s in enumerate(src):
                B.dma(slot[:, half * 4:(half + 1) * 4, 0:ncols], s.rearrange("(kt p) c -> p kt c", p=128),
                      "w%d" % (i % NW), q="pool")
        else:
            B.dma(slot[:, 0:nkt, 0:ncols], src.rearrange("(kt p) c -> p kt c", p=128), "w%d" % (i % NW), q="pool")

    PF = 1

    def w_next():
        i = wstate["idx"]
        if i == 0:
            for k in range(min(PF, len(wstate["plan"]))):
                w_issue(k)
        if i + PF < len(wstate["plan"]):
            w_issue(i + PF)
        wstate["idx"] = i + 1
        return WR[i % NW]

    def transposes_to(dst_view_fn, src_fn, n, ptile, evac_eng="dve"):
        for k in range(n):
            B.tr(ptile[:, k * 128:(k + 1) * 128], src_fn(k))
        B.cp(dst_view_fn, ptile[:, 0:n * 128].rearrange("p (a b) -> p a b", a=n), eng=evac_eng)

    def rope_block(ps, nheads, tabidx, out_ap_f32=None, out_ap_bf=None, sample=False, perm=False):
        w = nheads * 64
        cosb = AV(ROPE[:], (0 * 4 + tabidx) * 32, [[0, nheads], [0, 2], [1, 32]])
        sinb = AV(ROPE[:], (1 * 4 + tabidx) * 32, [[0, nheads], [1, 32]])
        nsinb = AV(ROPE[:], (2 * 4 + tabidx) * 32, [[0, nheads], [1, 32]])
        psv = ps.rearrange("p (h t f) -> p h t f", h=nheads, t=2)
        t1v = T1[:, 0:w].rearrange("p (h t f) -> p h t f", h=nheads, t=2)
        t2v = T2[:, 0:w].rearrange("p (h t f) -> p h t f", h=nheads, t=2)
        B.tt(t1v, psv, cosb, ALU.mult)
        B.tt(t2v[:, :, 0, :], psv[:, :, 1, :], nsinb, ALU.mult)
        B.tt(t2v[:, :, 1, :], psv[:, :, 0, :], sinb, ALU.mult)
        if out_ap_f32 is not None:
            B.tt(out_ap_f32, T1[:, 0:w], T2[:, 0:w], ALU.add)
            if out_ap_bf is not None:
                B.cp(out_ap_bf, out_ap_f32, eng="act")
        elif perm:
            B.tt(out_ap_bf, T1[:, 0:w].rearrange("p (s t d) -> p s t d", s=2, t=4),
                 T2[:, 0:w].rearrange("p (s t d) -> p s t d", s=2, t=4), ALU.add)
        else:
            B.tt(out_ap_bf, T1[:, 0:w], T2[:, 0:w], ALU.add)

    RSTD = sb("RSTD", [128, 4], F32)

    def layer_norm_chunks(nch):
        for c in range(nch):
            for a in range(2):
                B.bnstats(STAT[:, 2 * c + a, :], X[:, c, a * 512:(a + 1) * 512])
            B.bnaggr(MV[:, c, :], STAT[:, 2 * c:2 * c + 2, :].rearrange("p a b -> p (a b)"))
        B.ts(RSTD[:, 0:nch], MV[:, 0:nch, 1], LN_EPS, ALU.add)
        B.act(RSTD[:, 0:nch], RSTD[:, 0:nch], AF.Ln)
        B.act(RSTD[:, 0:nch], RSTD[:, 0:nch], AF.Exp, scale=-0.5)
        for c in range(nch):
            xc = X[:, c, :]
            B.ts(xc, xc, MV[:, c, 0:1], ALU.subtract, RSTD[:, c:c + 1], ALU.mult)
            B.tt(xc, xc, LNB[:, 0, :], ALU.mult)
            B.tt(xc, xc, LNB[:, 1, :], ALU.add)

    def gn_part1(ops):
        for h in range(8):
            B.bnstats(STAT[:, h, :], ops[h // 4][:, (h % 4) * 128:(h % 4 + 1) * 128])
        for h in range(8):
            B.bnaggr(MV[:, h, :], STAT[:, h, :])
        B.ts(MV[:, :, 1], MV[:, :, 1], GN_EPS, ALU.add)
        B.act(MV[:, :, 1], MV[:, :, 1], AF.Ln)
        B.act(MV[:, :, 1], MV[:, :, 1], AF.Exp, scale=-0.5)
        for half in range(2):
            pv = ops[half][:].rearrange("p (h e) -> p h e", h=4)
            B.tt(Y[:, half * 4:(half + 1) * 4, :], pv,
                 AV(MV[:], half * 8, [[2, 4], [0, 128]]), ALU.subtract)

    def gn_part2(c):
        B.tt(Y[:], Y[:], AV(MV[:], 1, [[2, 8], [0, 128]]), ALU.mult)
        B.tt(Y[:], Y[:], GNG[:].rearrange("p (h e) -> p h e", h=8), ALU.mult)
        B.tt(GR[:], Y[:].rearrange("p h e -> p (h e)"), SRG[:, c, :], ALU.mult)
        transposes_to(GRT[:, :, c * 128:(c + 1) * 128], lambda k: GR[:, k * 128:(k + 1) * 128], 8, PT[0],
                      evac_eng="act")

    def group_norm_gate(ops, c, nch_tok):
        gn_part1(ops)
        gn_part2(c)

    EM = sb("EM", [128, 2, 512], BF16)

    M = dict(PV_)

    def bcast_row(dram_ap_row, n):
        return bass.AP(dram_ap_row.tensor, dram_ap_row.offset, [[0, 128], [1, n]])

    def compute_memT():
        for mb in range(2):
            for half in range(2):
                B.dma(T1[:], mem_p[mb * 128:(mb + 1) * 128, half * 512:(half + 1) * 512], "x")
                B.cp(Xb[:, 0:512], T1[:], eng="act")
                transposes_to(MEMT[:, half * 4:(half + 1) * 4, mb * 128:(mb + 1) * 128],
                              lambda k: Xb[:, k * 128:(k + 1) * 128], 4, PT[half])

    def mem_kv(l):
        import os
        SUB = int(os.environ.get("SUB", "99"))
        compute_memT()
        Wk = w_next()
        for mb in range(2):
            ps = PB[mb]
            for kt in range(8):
                B.mm(ps[:], MEMT[:, kt, mb * 128:(mb + 1) * 128], Wk[:, kt, :], kt == 0, kt == 7)
            if SUB < 1:
                continue
            B.cp(T1[:], ps[:], eng="act")
            if SUB < 2:
                continue
            B.dma(mk_p[mb * 128:(mb + 1) * 128, :], T1[:], "o_mk", final=True)
            if SUB < 3:
                continue
            B.cp(RQb[:], ps[:])
            transposes_to(MKT[l][:, :, mb * 128:(mb + 1) * 128], lambda k: RQb[:, k * 128:(k + 1) * 128], 4, PT[mb])
        if SUB < 4:
            return
        Wv = w_next()
        for mb in range(2):
            ps = PB[2 + mb]
            for kt in range(8):
                B.mm(ps[:], MEMT[:, kt, mb * 128:(mb + 1) * 128], Wv[:, kt, :], kt == 0, kt == 7)
            B.cp(T2[:], ps[:], eng="act")
            B.dma(mv_p[mb * 128:(mb + 1) * 128, :], T2[:], "o_mv", final=True)
            B.cp(MVb[l][:, mb, :], ps[:])

    def load_layer_params(l):
        B.dma(GNG[:], bcast_row(gn_g[l], 1024), "prm")
        B.dma(LNB[:, 0, :], bcast_row(ln1_g[l], D), "prm")
        B.dma(LNB[:, 1, :], bcast_row(ln1_b[l], D), "prm")

    def xT_phase(nch):
        for c in range(nch):
            B.cp(Xb[:], X[:, c, :], eng="act")
            transposes_to(XT[:, :, c * 128:(c + 1) * 128], lambda k: Xb[:, k * 128:(k + 1) * 128], 8, PT[c % 2])

    def in_proj(l, nch, sample, last_chunk_out=None, mid_hook=None):
        NTk = nch * 128
        for j in range(8):
            W = w_next()
            ncols = 512 if j < 7 else 256
            for c in range(nch):
                ps = PB[(j * nch + c) % 6]
                for kt in range(8):
                    B.mm(ps[:, 0:ncols], XT[:, kt, c * 128:(c + 1) * 128], W[:, kt, 0:ncols], kt == 0, kt == 7)
                tab = 0 if sample else c
                if j == 0:
                    rope_block(ps[:], 8, tab, out_ap_bf=RQb[:])
                    transposes_to(M["RQT"][:, :, c * 128:(c + 1) * 128], lambda k: RQb[:, k * 128:(k + 1) * 128], 4, PT[1])
                elif j == 1:
                    rope_block(ps[:], 8, tab, out_ap_f32=RF[:], out_ap_bf=RQb[:])
                    for k in range(4):
                        B.tr(PT[1][:, k * 128:(k + 1) * 128], RQb[:, k * 128:(k + 1) * 128])
                    for hh in range(2):
                        B.cp(AV(RKm[:], (c * 8 + hh) * 128, [[256, 4], [1, 128]], p0=hh * 64, pn=64),
                             PT[1][hh * 64:(hh + 1) * 64, 0:512].rearrange("p (a b) -> p a b", a=4))
                    B.tt(RKw[:, c, :].rearrange("p (h d) -> p h d", h=8), RF[:].rearrange("p (h d) -> p h d", h=8),
                         AV(WEND[:], 0, [[1, 8], [0, 64]]), ALU.mult)
                elif j in (2, 3):
                    B.cp(Vb[:, c, (j - 2) * 512:(j - 1) * 512], ps[:], eng="act")
                elif j in (4, 5):
                    B.act(SRG[:, c, (j - 4) * 512:(j - 3) * 512], ps[:], AF.Silu)
                elif j == 6:
                    rope_block(ps[:], 8, tab, out_ap_bf=AV(RQb[:], 0, [[64, 2], [128, 4], [1, 64]]), perm=True)
                    transposes_to(SQT[:, c, :].rearrange("p (a b) -> p a b", a=4),
                                  lambda k: RQb[:, k * 128:(k + 1) * 128], 4, PT[1])
                else:
                    rope_block(ps[:, 0:128], 2, tab, out_ap_f32=SKf[:, 0:128], out_ap_bf=RQb[:, 0:128])
                    B.tr(PT[1][:, 0:128], RQb[:, 0:128])
                    for g in range(2):
                        B.cp(SKm[g * 64:(g + 1) * 64, 1 + c, g, :], PT[1][g * 64:(g + 1) * 64, 0:128])
                    B.cp(SKf[:, 128:256], ps[:, 128:256], eng="act")
                    for g in range(2):
                        for hh in range(2):
                            B.cp(SVp[:, 1 + c, g, hh, hh * 64:(hh + 1) * 64], SKf[:, 128 + g * 64:128 + (g + 1) * 64])
                    if last_chunk_out is not None and c == nch - 1:
                        last_chunk_out()
        for jj in range(7):
            if jj == 1 and mid_hook is not None:
                mid_hook()
            W = w_next()
            for t in range(4):
                ti = jj * 4 + t
                ps = PB[ti % 6]
                for kt in range(8):
                    B.mm(ps[:, 0:NTk], W[:, kt, t * 128:(t + 1) * 128], XT[:, kt, 0:NTk], kt == 0, kt == 7)
                if ti < 4:
                    B.cp(M["MQT"][:, ti, 0:NTk], ps[:, 0:NTk], eng="act")
                else:
                    B.act(M["GT"][:, ti - 4, 0:NTk], ps[:, 0:NTk], AF.Sigmoid)

    def swa_pv(l, c, has_prev, slot_prev, slot_own):
        for pr in range(4):
            g = pr // 2
            for which in range(2):
                dst = (PB[5], PB[0])[which][:, pr * 128:(pr + 1) * 128]
                first = True
                for hh in range(2):
                    t = 2 * (pr % 2) + hh
                    for blk in range(2):
                        if blk == 0 and not has_prev:
                            continue
                        slot = slot_prev if blk == 0 else slot_own
                        lhs = SVp[:, slot, g, hh, :] if which == 0 else ONp[:, hh, :]
                        B.mm(dst, lhs, E[:, g * 2 + blk, t * 128:(t + 1) * 128], first, hh == 1 and blk == 1)
                        first = False

    def swa_norm(l, c):
        for pr in range(4):
            B.ts(DEN[:, pr * 128:(pr + 1) * 128], PB[0][:, pr * 128:(pr + 1) * 128], SINKE[:, l, pr:pr + 1], ALU.add)
        B.recip(DEN[:], DEN[:])
        B.tt(SWOT[:, :, c * 128:(c + 1) * 128], PB[5][:].rearrange("p (a i) -> p a i", a=4),
             DEN[:].rearrange("p (a i) -> p a i", a=4), ALU.mult)

    def swa_pv_norm(l, c, has_prev, slot_prev, slot_own):
        swa_pv(l, c, has_prev, slot_prev, slot_own)
        swa_norm(l, c)

    def branches_out_mlp(l, nch, mid_hook=None, tail_hook=None):
        NTk = nch * 128
        for cb in range(2):
            Wret = w_next()
            Wsm = w_next()
            for o4 in range(4):
                ot = cb * 4 + o4
                bk = (ot % 2) * 3
                pr_, psw, pme = PB[bk], PB[bk + 1], PB[bk + 2]
                for kt in range(8):
                    B.mm(pr_[:, 0:NTk], Wret[:, kt, o4 * 128:(o4 + 1) * 128], GRT[:, kt, 0:NTk], kt == 0, kt == 7)
                for kt in range(4):
                    B.mm(psw[:, 0:NTk], Wsm[:, kt, o4 * 128:(o4 + 1) * 128], SWOT[:, kt, 0:NTk], kt == 0, kt == 3)
                for kt in range(4):
                    B.mm(pme[:, 0:NTk], Wsm[:, 4 + kt, o4 * 128:(o4 + 1) * 128], MOT[:, kt, 0:NTk], kt == 0, kt == 3)
                B.tt(T1[:, 0:NTk], pr_[:, 0:NTk], M["GT"][:, ot, 0:NTk], ALU.mult)
                B.tt(T2[:, 0:NTk], psw[:, 0:NTk], M["GT"][:, 8 + ot, 0:NTk], ALU.mult)
                B.tt(RF[:, 0:NTk], pme[:, 0:NTk], M["GT"][:, 16 + ot, 0:NTk], ALU.mult)
                B.tt(T1[:, 0:NTk], T1[:, 0:NTk], T2[:, 0:NTk], ALU.add)
                B.tt(MT[:, ot, 0:NTk], T1[:, 0:NTk], RF[:, 0:NTk], ALU.add)
        for cb in range(2):
            Wo = w_next()
            for c in range(nch):
                ps = PB[(cb * nch + c) % 6]
                for kt in range(8):
                    B.mm(ps[:], MT[:, kt, c * 128:(c + 1) * 128], Wo[:, kt, :], kt == 0, kt == 7)
                xs = X[:, c, cb * 512:(cb + 1) * 512]
                B.stt(xs, xs, ALPHA, ps[:], ALU.mult, ALU.add)
        layer_norm_chunks(nch)
        B.dma(LNB[:, 0, :], bcast_row(ln2_g[l], D), "prm")
        B.dma(LNB[:, 1, :], bcast_row(ln2_b[l], D), "prm")
        xT_phase(nch)
        for jb in range(8):
            W = w_next()
            for t in range(4):
                ft = jb * 4 + t
                ps = PB[ft % 6]
                R = (T1, T2)[ft % 2]
                for kt in range(8):
                    B.mm(ps[:, 0:NTk], W[:, kt, t * 128:(t + 1) * 128], XT[:, kt, 0:NTk], kt == 0, kt == 7)
                B.act(R[:, 0:NTk], ps[:, 0:NTk], AF.Relu)
                B.tt(M["HT"][:, ft, 0:NTk], R[:, 0:NTk], R[:, 0:NTk], ALU.mult)
        for cb in range(2):
            for rb in range(4):
                W = w_next()
                for c in range(nch):
                    for k8 in range(8):
                        B.mm(PB[c][:], M["HT"][:, rb * 8 + k8, c * 128:(c + 1) * 128], W[:, k8, :],
                             rb == 0 and k8 == 0, rb == 3 and k8 == 7)
            if cb == 1 and tail_hook is not None:
                tail_hook()
            for c in range(nch):
                xs = X[:, c, cb * 512:(cb + 1) * 512]
                B.stt(xs, xs, ALPHA, PB[c][:], ALU.mult, ALU.add)
            if cb == 0 and mid_hook is not None:
                mid_hook()
        layer_norm_chunks(nch)

    def prefetch_input_bf16(t):
        for c in range(4):
            for half in range(2):
                dst = XbN[:, c, half * 512:(half + 1) * 512]
                if t < nsc:
                    B.dma(RF[:], x_p[t * 512 + c * 128:t * 512 + (c + 1) * 128, half * 512:(half + 1) * 512], None)
                if t >= 2:
                    B.dma(XbS, cc_dst[t % 2][c * 128:(c + 1) * 128, half * 512:(half + 1) * 512], None)
                if t < 2:
                    B.ts(dst, RF[:], SEL[:, 0:1], ALU.mult)
                elif t < nsc:
                    B.ts(RF[:], RF[:], SEL[:, 0:1], ALU.mult)
                    B.stt(dst, XbS, SEL[:, 1:2], RF[:], ALU.mult, ALU.add)
                else:
                    B.ts(dst, XbS, SEL[:, 1:2], ALU.mult)

    def xt_from_prefetch():
        for c in range(4):
            transposes_to(XT[:, :, c * 128:(c + 1) * 128], lambda k: XbN[:, c, k * 128:(k + 1) * 128], 8, PT[c % 2],
                          evac_eng="act")

    def load_x_fp32(sc):
        if sc < nsc:
            B.dma(X[:], x_p[sc * 512:(sc + 1) * 512, :].rearrange("(c p) d -> p c d", p=128), "x")
            Xf = X[:].rearrange("p c d -> p (c d)")
            B.ts(Xf, Xf, SEL[:, 0:1], ALU.mult)
        else:
            B.memset(X[:], 0.0)
        if sc >= 2:
            gsrc = cc_dst[sc % 2]
            for c in range(4):
                for half in range(2):
                    stg = (T1, T2)[(c * 2 + half) % 2]
                    B.dma(stg[:], gsrc[c * 128:(c + 1) * 128, half * 512:(half + 1) * 512], None)
                    xs = X[:, c, half * 512:(half + 1) * 512]
                    B.stt(xs, stg[:], SEL[:, 1:2], xs, ALU.mult, ALU.add)

    def prompt_pass(sc, l):
        chunk0 = sc * 4
        B.dma(ROPE[:, 0, :, :], r_cos[:, chunk0:chunk0 + 4, :], "rope")
        B.dma(ROPE[:, 1, :, :], r_sin[:, chunk0:chunk0 + 4, :], "rope")
        B.dma(ROPE[:, 2, :, :], r_nsin[:, chunk0:chunk0 + 4, :], "rope")
        load_layer_params(l)
        if sc == 0:
            mem_kv(l)
        if sc == 0:
            prefetch_input_bf16(0)
            xt_from_prefetch()
        B.cp(SKm[:, 0], SKTprev[l][:])
        B.cp(SVp[:, 0], SVprev[l][:])
        ver = {nsc - 1: 0, nsc + 1: 1}.get(sc, None)
        last = ver is not None

        def last_out():
            B.dma(sk_p[ver], SKf[:, 0:128], "o_sk", final=True)
            B.dma(sv_p[ver], SKf[:, 128:256], "o_sk", final=True)
        in_proj(l, 4, False, last_out if last else None, mid_hook=lambda: load_x_fp32(sc))
        S, Sb_ = Sst[l], Sbf[l]
        DENm = sb("DENm%d" % sc, [128, 512], F32) if False else DEN
        for c in range(4):
            cs = slice(c * 128, (c + 1) * 128)
            has_prev = not (sc == 0 and c == 0)
            obanks = (PB[5], PB[0])
            for h in range(8):
                pr = h // 2
                ps = PB[3 + h // 4]
                B.mm(ps[:, (h % 4) * 128:(h % 4 + 1) * 128], RKm[:, c, h, :], M["RQT"][:, pr, cs], True, True)
            for half in range(2):
                B.tt(ST[:, half * 4:(half + 1) * 4, :], PB[3 + half][:].rearrange("p (h i) -> p h i", h=4),
                     DEC[:, half * 4:(half + 1) * 4, :], ALU.mult)
            for hh in range(2):
                B.tt(AV(RQs2[:], hh * 128, [[256, 4], [1, 128]], p0=hh * 64, pn=64),
                     M["RQT"][hh * 64:(hh + 1) * 64, :, cs], GQ[hh * 64:(hh + 1) * 64, :, :], ALU.mult)
            banks = {(0, 0): PB[1], (0, 1): PB[2], (1, 0): PB[3], (1, 1): PB[4]}
            for g in range(2):
                qv = SQT[:, c, :]
                for blk in range(2):
                    if blk == 0 and not has_prev:
                        continue
                    ps = banks[(g, blk)]
                    B.mm(ps[:], SKm[:, c + blk, g, :], qv, True, False)
                    if blk == 0 and sc == 2 and c == 0:
                        B.mm(ps[:], B.identb[:], FLAGN[:], False, False)
                    B.mm(ps[:], B.identb[:], (MPREV if blk == 0 else MOWN)[:], False, True)
                    B.act(E[:, g * 2 + blk, :], ps[:], AF.Exp, scale=SHD ** -0.5)
            for h in range(8):
                pr = h // 2
                po = obanks[h // 4][:, (h % 4) * 128:(h % 4 + 1) * 128]
                B.mm(po, ST[:, h, :], Vb[:, c, h * 128:(h + 1) * 128], True, False)
                B.mm(po, RQs2[:, h, :], Sb_[:, pr, :], False, True)
            for blk in range(2):
                ps = PB[1 + blk]
                for h in range(4):
                    B.mm(ps[:, h * 128:(h + 1) * 128], MKT[l][:, h, blk * 128:(blk + 1) * 128], M["MQT"][:, h, cs], True, True)
                B.act(EM[:, blk, :], ps[:], AF.Exp, scale=MHD ** -0.5)
            gn_part1(obanks)
            swa_pv(l, c, has_prev, c, c + 1)
            for h in range(4):
                for blk in range(2):
                    B.mm(PB[3][:, h * 128:(h + 1) * 128], MVb[l][:, blk, h * 128:(h + 1) * 128],
                         EM[:, blk, h * 128:(h + 1) * 128], blk == 0, blk == 1)
                for blk in range(2):
                    B.mm(PB[4][:, h * 128:(h + 1) * 128], onesb[:], EM[:, blk, h * 128:(h + 1) * 128], blk == 0, blk == 1)
            gn_part2(c)
            swa_norm(l, c)
            B.recip(DEN[:], PB[4][:])
            B.tt(MOT[:, :, cs], PB[3][:].rearrange("p (a i) -> p a i", a=4),
                 DEN[:].rearrange("p (a i) -> p a i", a=4), ALU.mult)
            for pr in range(4):
                ps = PB[1 + pr % 2]
                B.mm(ps[:, 0:256], RKw[:, c, pr * 128:(pr + 1) * 128], Vb[:, c, pr * 256:(pr + 1) * 256], True, True)
                for hh in range(2):
                    sv = S[hh * 64:(hh + 1) * 64, pr, :]
                    B.stt(sv, sv, GL[hh * 64:(hh + 1) * 64, pr:pr + 1],
                          ps[hh * 64:(hh + 1) * 64, hh * 128:(hh + 1) * 128], ALU.mult, ALU.add)
            B.cp(Sb_[:], S[:], eng="act")
            if last and c == 3:
                B.dma(bass.AP(rs_p.tensor, ver * RH * RDK * RDV, [[128, 128], [2 * RDK * RDV, 4], [1, 128]]), S[:],
                      "o_rs", final=True)
        B.cp(SKTprev[l][:], SKm[:, 4])
        B.cp(SVprev[l][:], SVp[:, 4])
        if stage < 8:
            return
        nxt = sc + 1 < NSLOT
        branches_out_mlp(l, 4, mid_hook=(lambda: prefetch_input_bf16(sc + 1)) if nxt else None,
                         tail_hook=xt_from_prefetch if nxt else None)
        B.dma(y_p[sc * 512:(sc + 1) * 512, :].rearrange("(c p) d -> p c d", p=128), X[:], "o_y", final=True)
        if sc < nsc:
            k = sc % 2
            B.dma(cc_src[k].rearrange("(c p) d -> p c d", p=128), X[:], "!ccs%d" % k)
            src_ap, dst_ap = cc_src[k], cc_dst[k]
            if not cfg.get("nocc", False):
              B.T.add("pool", lambda: nc.gpsimd.collective_compute(
                "AllGather", ALU.bypass, replica_groups=[[i, i + 4] for i in range(4)], ins=[src_ap], outs=[dst_ap]),
                w=[dst_ap], r=[src_ap], dma_key="!cc%d" % sc, inc=1)

    def load_sample_consts():
        B.dma(DEC[:], cd["decs"], None)
        B.dma(GQ[:], cd["gqs"], None)
        B.dma(WEND[:], cd["wends"], None)
        B.dma(GL[:], cd["gls"], None)
        B.dma(ROPE[:, 0, 0:1, :], cd["coss"], None)
        B.dma(ROPE[:, 1, 0:1, :], cd["sins"], None)
        B.dma(ROPE[:, 2, 0:1, :], cd["nsins"], None)
        B.dma(BMR[:], cd["bmrow"], None)
        B.dma(identF[:], cd["ident"], None)
        B.dma(T1[:], cd["mnews"], None)
        B.cp(MOWN[:], T1[:])
        B.dma(T1[:], cd["mcache"], None)
        B.cp(MPREV[:], T1[:])

    def reload_prompt_masks():
        B.dma(T1[:], cd["mown"], None)
        B.cp(MOWN[:], T1[:])
        B.dma(T1[:], cd["mprev"], None)
        B.cp(MPREV[:], T1[:])

    def sample_pass(l):
        NB = SB_PER_CORE
        load_layer_params(l)
        if l == 0:
            B.dma(X[:, 0, :], x_s, None)
        B.dma(sk_s[l][:, 0:120, :], c_sk[l][:, 8:128, :], "!cck", final=True)
        B.dma(sv_s[l][:, 0:120, :], c_sv[l][:, 8:128, :], "!ccv", final=True)
        xT_phase(1)

        def new_rows_out():
            for b in range(NB):
                B.dma(sk_s[l, b, 120:128, :], SKf[8 * b:8 * b + 8, 0:128], None, final=True)
                B.dma(sv_s[l, b, 120:128, :], SKf[8 * b:8 * b + 8, 128:256], None, final=True)
        in_proj(l, 1, True, new_rows_out)
        RQT_, MQT_ = M["RQT"], M["MQT"]
        if sstage < 2:
            return
        cs = slice(0, 128)
        for h in range(8):
            pr = h // 2
            ps = PB[3 + h // 4]
            B.mm(ps[:, (h % 4) * 128:(h % 4 + 1) * 128], RKm[:, 0, h, :], RQT_[:, pr, cs], True, True)
        for half in range(2):
            B.tt(ST[:, half * 4:(half + 1) * 4, :], PB[3 + half][:].rearrange("p (h i) -> p h i", h=4),
                 DEC[:, half * 4:(half + 1) * 4, :], ALU.mult)
        for hh in range(2):
            B.tt(AV(RQs2[:], hh * 128, [[256, 4], [1, 128]], p0=hh * 64, pn=64),
                 RQT_[hh * 64:(hh + 1) * 64, :, cs], GQ[hh * 64:(hh + 1) * 64, :, :], ALU.mult)
        ot = (PB[5], PB[0])
        B.memset(ot[0][:], 0.0)
        B.memset(ot[1][:], 0.0)
        for h in range(8):
            B.mm(ot[h // 4][:, (h % 4) * 128:(h % 4 + 1) * 128], Vb[:, 0, h * 128:(h + 1) * 128], ST[:, h, :],
                 False, False, skip=True)
        for grp in range(4):
            b0 = grp * 4
            st_src = bass.AP(st_ret.tensor, (l * NB + b0) * RH * RDK * RDV,
                             [[128, 128], [RH * RDK * RDV, 4], [2 * RDK * RDV, 4], [1, 128]])
            B.dma(S0g, st_src, None)
            B.dma(S0bg, st_src, None, q="pool")
            if grp % 2 == 0:
                rnd = grp // 2
                B.tt(RKwX, AV(RKw[:], 0, [[0, 8], [1, 512]]),
                     AV(BMR[:], rnd * 8, [[1, 8], [0, 512]]), ALU.mult)
            for bl in range(4):
                b = b0 + bl
                for h in range(8):
                    pr = h // 2
                    B.mm(ot[h // 4][:, (h % 4) * 128 + 8 * b:(h % 4) * 128 + 8 * b + 8], S0bg[:, bl, pr, :],
                         RQs2[:, h, 8 * b:8 * b + 8], False, False, skip=True)
                for pr in range(4):
                    ps = PB[1 + pr // 2]
                    B.mm(ps[:, (pr % 2) * 256:(pr % 2 + 1) * 256], RKwX[:, b % 8, pr * 128:(pr + 1) * 128],
                         Vb[:, 0, pr * 256:(pr + 1) * 256], True, True)
                for pr in range(4):
                    ps = PB[1 + pr // 2]
                    for hh in range(2):
                        sv = S0g[hh * 64:(hh + 1) * 64, bl, pr, :]
                        c0 = (pr % 2) * 256 + hh * 128
                        B.stt(sv, sv, GL[hh * 64:(hh + 1) * 64, pr:pr + 1],
                              ps[hh * 64:(hh + 1) * 64, c0:c0 + 128], ALU.mult, ALU.add)
            B.dma(bass.AP(rs_s.tensor, (l * NB + b0) * RH * RDK * RDV,
                          [[128, 128], [RH * RDK * RDV, 4], [2 * RDK * RDV, 4], [1, 128]]), S0g, None, final=True)
        for half in range(2):
            B.cp(Y[:, half * 4:(half + 1) * 4, :], ot[half][:].rearrange("p (h i) -> p h i", h=4), eng="act")
        for h in range(8):
            B.trf(PB[3 + h // 4][:, (h % 4) * 128:(h % 4 + 1) * 128], Y[:, h, :], identF[:])
        group_norm_gate((PB[3], PB[4]), 0, 1)
        if sstage < 3:
            return
        for g in range(2):
            ps = PB[1 + g]
            B.mm(ps[:], SKm[:, 1, g, :], SQT[:, 0, :], True, False)
            B.mm(ps[:], B.identb[:], MOWN[:], False, True)
            B.act(E[:, g * 2 + 1, :], ps[:], AF.Exp, scale=SHD ** -0.5)
        B.memset(SQTs, 0.0)
        for g in range(2):
            B.cp(AV(SQTs, g * 512, [[32, 16], [8, 4], [1, 8]], p0=g * 64, pn=64),
                 AV(SQT[:], 0, [[8, 16], [128, 4], [1, 8]], p0=g * 64, pn=64))
        B.dma(Kraw, c_sk[l].rearrange("b k c -> k b c"), None, q="pool")
        for half in range(2):
            for k in range(8):
                B.tr(PT[half][:, k * 128:(k + 1) * 128], Kraw[:, half * 8 + k, :])
            B.cp(KcT[:, half * 8:(half + 1) * 8, :], PT[half][:].rearrange("p (a b) -> p a b", a=8))
        for half in range(2):
            ps = PB[3 + half]
            B.memset(ps[:], 0.0)
            for b8 in range(8):
                b = half * 8 + b8
                for g in range(2):
                    B.mm(ps[:, (b8 * 2 + g) * 32:(b8 * 2 + g + 1) * 32], KcT[:, b, :],
                         SQTs[:, g, b * 32:(b + 1) * 32], False, False, skip=True)
            B.mm(ps[:], B.identb[:], MPREV[:], False, False, skip=True)
            B.act(Ec[:, half * 8:(half + 1) * 8, :, :].rearrange("p b g c -> p (b g c)"), ps[:], AF.Exp,
                  scale=SHD ** -0.5)
        B.dma(Kraw, c_sv[l].rearrange("b k c -> k b c"), None, q="pool")
        num, den = PB[5], PB[0]
        B.memset(num[:], 0.0)
        B.memset(den[:], 0.0)
        for pr in range(4):
            g = pr // 2
            for hh in range(2):
                t = 2 * (pr % 2) + hh
                B.mm(num[:, pr * 128:(pr + 1) * 128], SVp[:, 1, g, hh, :], E[:, g * 2 + 1, t * 128:(t + 1) * 128],
                     False, False, skip=True)
                B.mm(den[:, pr * 128:(pr + 1) * 128], ONp[:, hh, :], E[:, g * 2 + 1, t * 128:(t + 1) * 128],
                     False, False, skip=True)
        for grp in range(4):
            b0 = grp * 4
            B.memset(Vcp, 0.0)
            for hh in range(2):
                B.cp(AV(Vcp, hh * 128 + hh * 64, [[512, 4], [256, 2], [1, 64]]),
                     AV(Kraw, b0 * 128, [[128, 4], [64, 2], [1, 64]]))
            for bl in range(4):
                b = b0 + bl
                for pr in range(4):
                    g = pr // 2
                    for hh in range(2):
                        t = 2 * (pr % 2) + hh
                        rhs = Ec[:, b, g, t * 8:(t + 1) * 8]
                        B.mm(num[:, pr * 128 + 8 * b:pr * 128 + 8 * b + 8], Vcp[:, bl, g, hh, :], rhs,
                             False, False, skip=True)
                        B.mm(den[:, pr * 128 + 8 * b:pr * 128 + 8 * b + 8], ONp[:, hh, :], rhs,
                             False, False, skip=True)
        for pr in range(4):
            B.ts(DEN[:, pr * 128:(pr + 1) * 128], den[:, pr * 128:(pr + 1) * 128], SINKE[:, l, pr:pr + 1], ALU.add)
        B.recip(DEN[:], DEN[:])
        B.tt(SWOT[:, :, cs], num[:].rearrange("p (a i) -> p a i", a=4),
             DEN[:].rearrange("p (a i) -> p a i", a=4), ALU.mult)
        if sstage < 4:
            return
        Kmraw = Vb[:, 1:3, :].rearrange("p a (k c) -> p (a k) c", k=2)
        KmT = SRG[:, 1:3, :].rearrange("p a (h m) -> p a h m", h=4)
        Vm = RKm[:, 1:3, :, :].rearrange("p a h c -> p (a h c)").rearrange("p (k c) -> p k c", k=4)
        Em = RKw[:, 1, 0:128].rearrange("p (b k h i) -> p b k h i", b=2, k=2, h=4)
        mnum, mden = PB[3], PB[4]
        B.memset(mnum[:], 0.0)
        B.memset(mden[:], 0.0)
        for grp in range(NB // 2):
            b0 = grp * 2
            B.dma(Kmraw, c_mk[l][b0:b0 + 2].rearrange("b (k m) c -> m (b k) c", k=2), None, q="pool")
            B.dma(Vm, c_mv[l][b0:b0 + 2].rearrange("b (k m) c -> m (b k) c", k=2), None, q="pool")
            sc_ps = PB[1 + grp % 2]
            B.memset(sc_ps[:, 0:128], 0.0)
            for bl in range(2):
                b = b0 + bl
                for blk in range(2):
                    for h in range(4):
                        B.tr(PT[bl][:, (h * 2 + blk) * 128:(h * 2 + blk + 1) * 128],
                             Kmraw[:, bl * 2 + blk, h * 128:(h + 1) * 128])
                B.cp(KmT[:, bl, :, :].rearrange("p h m -> p (h m)"), PT[bl][:])
                for blk in range(2):
                    for h in range(4):
                        c0 = ((bl * 2 + blk) * 4 + h) * 8
                        B.mm(sc_ps[:, c0:c0 + 8], KmT[:, bl, h, blk * 128:(blk + 1) * 128],
                             MQT_[:, h, 8 * b:8 * b + 8], False, False, skip=True)
            B.act(Em.rearrange("p b k h i -> p (b k h i)"), sc_ps[:, 0:128], AF.Exp, scale=MHD ** -0.5)
            for bl in range(2):
                b = b0 + bl
                for blk in range(2):
                    for h in range(4):
                        rhs = Em[:, bl, blk, h, :]
                        B.mm(mnum[:, h * 128 + 8 * b:h * 128 + 8 * b + 8], Vm[:, bl * 2 + blk, h * 128:(h + 1) * 128],
                             rhs, False, False, skip=True)
                        B.mm(mden[:, h * 128 + 8 * b:h * 128 + 8 * b + 8], onesb[:], rhs, False, False, skip=True)
        B.recip(DEN[:], mden[:])
        B.tt(MOT[:, :, cs], mnum[:].rearrange("p (a i) -> p a i", a=4),
             DEN[:].rearrange("p (a i) -> p a i", a=4), ALU.mult)
        if sstage < 5:
            return
        branches_out_mlp(l, 1)
        if l == nlay - 1:
            B.dma(y_s, X[:, 0, :], None, final=True)

    passes = []
    if do_sample:
        for l in range(nlay):
            passes.append(("s", 0, l))
    for sc in range(NSLOT):
        passes.append(("p", sc, 2))
    for kind, sc, l in passes:
        wstate["plan"] += wplan_for_pass(l, kind == "p" and sc == 0)
    if do_sample:
        load_sample_consts()
    else:
        load_prompt_consts()
    for pi, (kind, sc, l) in enumerate(passes):
        T.phase = pi
        if kind == "s":
            M.update(SV_)
            sample_pass(l)
            if l == nlay - 1:
                M.update(PV_)
                B.memset(RKm[:], 0.0)
                load_prompt_consts()
                reload_prompt_masks()
        elif not cfg.get("noprompt", False):
            prompt_pass(sc, l)
    if stage >= 99 and sstage >= 99 and not cfg.get("noprompt", False):
        assert wstate["idx"] == len(wstate["plan"]), (wstate["idx"], len(wstate["plan"]))
    if dbgname is not None:
        dbg = B.dout("dbg", [128, 4096])
        DBG = X[:].rearrange("p a b -> p (a b)")
        srcs = dict(XT=XT[:], RQT=RQT, RKm=RKm[:, 0], SQT=SQT[:], SKm=SKm[:], Vb=Vb[:, :, :], SRG=SRG[:], MQT=MQT, GT=GT[:, 0:8, :],
                    GRT=GRT[:], SWOT=SWOT[:], MOT=MOT[:], MT=MT, X=X[:], RKw=RKw[:], MKT=MKT[0][:], MVb=MVb[0][:],
                    MEMT=MEMT, HT=HT[:, 0:8, :], S=Sst[0][:], E=E[:], EM=EM[:])[dbgname]
        n = 1
        for s_ in srcs.shape[1:]:
            n *= s_
        dims = "abcd"[:len(srcs.shape) - 1]
        flat = srcs.rearrange("p %s -> p (%s)" % (" ".join(dims), " ".join(dims))) if len(dims) > 1 else srcs
        if dbgname != "X":
            B.cp(DBG[:, 0:n], flat)
        B.dma(dbg[:, 0:n], DBG[:, 0:n], "o_dbg", final=True)
    T.emit(B.final_ops)
    return B


_CACHE = {}


def _core_inputs(inp, core, nsc, hc):
    f = lambda a: np.ascontiguousarray(np.asarray(a, dtype=np.float32))
    b = core % BATCH
    L = core // BATCH
    sb0 = core * SB_PER_CORE
    sl = slice(sb0, sb0 + SB_PER_CORE)
    nslot = nsc + 2
    m = {
        "x_p": f(inp["x_prompt"][b, :nsc * 512]),
        "mem_p": f(inp["mem_prompt"][b]),
        "x_s": f(np.asarray(inp["x_sample"])[sl].reshape(SB_PER_CORE * DEC_SEQ, D)),
        "st_ret": f(np.asarray(inp["state_ret"])[:, sl]),
        "c_sk": f(np.asarray(inp["cache_swa_k"])[:, sl].reshape(DEPTH, SB_PER_CORE, 128, 128)),
        "c_sv": f(np.asarray(inp["cache_swa_v"])[:, sl].reshape(DEPTH, SB_PER_CORE, 128, 128)),
        "c_mk": f(np.asarray(inp["cache_mem_k"])[:, sl].reshape(DEPTH, SB_PER_CORE, NMEM, 512)),
        "c_mv": f(np.asarray(inp["cache_mem_v"])[:, sl].reshape(DEPTH, SB_PER_CORE, NMEM, 512)),
        "sinks": f(inp["attn_sinks"]),
        "gn_g": f(np.asarray(inp["ret_gn_g"]).reshape(DEPTH, 1024)),
    }
    chunk_of = []
    for s in range(nslot):
        for c in range(4):
            ch = (4 * s + c) if L == 0 else (4 * (s - 2) + c)
            if ch < 0 or ch >= nsc * 4:
                ch = 0
            chunk_of.append(ch)
    for nm, key in (("r_cos", "cosp"), ("r_sin", "sinp"), ("r_nsin", "nsinp")):
        m[nm] = np.ascontiguousarray(hc[key][:, chunk_of, :])
    sel = np.zeros((128, 2), np.float32)
    sel[:, L] = 1.0
    m["sel"] = sel
    m["flagneg"] = np.full((128, 512), NEG if L == 1 else 0.0, np.float32)
    return m


_OWN = (("wo_in", "w_in"), ("wo_br_ret", "w_br_ret"), ("wo_br_swa", "w_br_swa"), ("wo_br_mem", "w_br_mem"),
        ("wo_out", "w_out"), ("wo_mem_kv", "w_mem_kv"), ("wo_up", "w_up"), ("wo_down", "w_down"),
        ("o_ln1_g", "ln1_g"), ("o_ln1_b", "ln1_b"), ("o_ln2_g", "ln2_g"), ("o_ln2_b", "ln2_b"))
_FULL = ("w_in", "w_br_ret", "w_br_swa", "w_br_mem", "w_out", "w_mem_kv", "ln1_g", "ln1_b", "w_up", "w_down",
         "ln2_g", "ln2_b")


def run_cores(inp, cfg, cores):
    key = tuple(sorted(cfg.items()))
    if key not in _CACHE:
        _CACHE[key] = build(cfg)
    B = _CACHE[key]
    hc = host_consts()
    f = lambda a: np.ascontiguousarray(np.asarray(a, dtype=np.float32))
    full = {k: f(inp[k]) for k in _FULL}
    ownl = []
    for L in range(DEPTH):
        d = {dst: np.ascontiguousarray(full[srck][L]) for dst, srck in _OWN}
        d["o_sinks"] = f(inp["attn_sinks"])[L:L + 1]
        d["o_gn_g"] = f(np.asarray(inp["ret_gn_g"]).reshape(DEPTH, 1024))[L]
        ownl.append(d)
    in_maps = []
    for core in cores:
        m = _core_inputs(inp, core, cfg["nsc"], hc)
        m.update(full)
        m.update(ownl[core // BATCH])
        for k, v in hc.items():
            m["c_" + k] = v
        in_maps.append(m)
    res = run_bass_kernel_spmd(B.nc, in_maps, core_ids=list(range(len(cores))))
    return res.results


def assemble(r, nsc):
    S = nsc * 512
    y_p = np.stack([r[BATCH + b]["y_p"][2 * 512:2 * 512 + S] for b in range(BATCH)]).astype(np.float32)
    y_s = np.concatenate([r[c]["y_s"].reshape(SB_PER_CORE, DEC_SEQ, D) for c in range(NCORES)]).astype(np.float32)

    def per_layer(name, shape, versioned):
        out = []
        for L in range(DEPTH):
            row = []
            for b in range(BATCH):
                a = r[L * BATCH + b][name]
                a = a[L] if versioned else a
                row.append(np.asarray(a).reshape(shape))
            out.append(np.stack(row))
        return np.stack(out).astype(np.float32)
    rs_p = per_layer("rs_p", (RH, RDK, RDV), True)
    sk_p = per_layer("sk_p", (128, SKV, SHD), True)
    sv_p = per_layer("sv_p", (128, SKV, SHD), True)
    mk_p = per_layer("mk_p", (NMEM, MH, MHD), False)
    mv_p = per_layer("mv_p", (NMEM, MH, MHD), False)
    rs_s = np.concatenate([r[c]["rs_s"] for c in range(NCORES)], axis=1).astype(np.float32)
    sk_s = np.concatenate([r[c]["sk_s"].reshape(DEPTH, SB_PER_CORE, 128, SKV, SHD) for c in range(NCORES)],
                          axis=1).astype(np.float32)
    sv_s = np.concatenate([r[c]["sv_s"].reshape(DEPTH, SB_PER_CORE, 128, SKV, SHD) for c in range(NCORES)],
                          axis=1).astype(np.float32)
    return (y_p, y_s, rs_p, sk_p, sv_p, mk_p, mv_p, rs_s, sk_s, sv_s)


def kernel(**inp):
    cfg = dict(nsc=SEQ // 512, nlay=DEPTH, sample=True)
    r = run_cores(inp, cfg, list(range(NCORES)))
    return assemble(r, cfg["nsc"])
```

```python
import numpy as np
import concourse.bass as bass
import concourse.mybir as mybir
from concourse.bass_utils import run_bass_kernel_spmd

F32 = mybir.dt.float32
BF16 = mybir.dt.bfloat16
AF = mybir.ActivationFunctionType
ALU = mybir.AluOpType

D = 1024
DEPTH = 2
SEQ = 4096
BATCH = 4
DEC_BATCH = 128
DEC_SEQ = 8
PAST_LEN = 16384
RH, RDK, RDV = 8, 64, 128
SH, SKV, SHD = 8, 2, 64
MH, MHD, NMEM = 4, 128, 256
DFF = 4096
IN_W = 7424
ALPHA = (2 * DEPTH) ** 0.25
LN_EPS = 1e-5
GN_EPS = 1e-5
NEG = -2000.0
NCORES = 8
SB_PER_CORE = DEC_BATCH // NCORES


def _dsize(dt):
    return mybir.dt.size(dt)


class Tracker:
    def __init__(self, nc):
        self.nc = nc
        self.ops = []
        self.psum_last = {}
        self.hist = {}
        self.phase = 0
        self.eng_objs = {"pe": nc.tensor, "dve": nc.vector, "act": nc.scalar,
                         "pool": nc.gpsimd, "sp": nc.sync}

    @staticmethod
    def box(ap):
        t = ap.tensor
        name = t.name
        dims = ap.ap
        esz = _dsize(ap.dtype)
        space = str(ap.space)
        if "DRAM" in space.upper() or "HBM" in space.upper() or "dram" in space:
            lo = ap.offset
            hi = lo + sum((c - 1) * s for s, c in dims if c > 0)
            return name, (0, 0, lo * esz, (hi + 1) * esz), True
        fstride = dims[0][0]
        p0 = ap.offset // fstride
        f0 = ap.offset % fstride
        p1 = p0 + dims[0][1] - 1
        f1 = f0 + sum((c - 1) * s for s, c in dims[1:])
        return name, (p0, p1, f0 * esz, (f1 + 1) * esz), False

    @staticmethod
    def overlap(a, b):
        return not (a[1] < b[0] or b[1] < a[0] or a[3] <= b[2] or b[3] <= a[2])

    @staticmethod
    def contains(a, b):
        return a[0] <= b[0] and a[1] >= b[1] and a[2] <= b[2] and a[3] >= b[3]

    def add(self, eng, fn, w=(), r=(), dma_key=None, untracked=(), inc=None):
        opid = len(self.ops)
        deps = set()
        is_dma = dma_key is not None
        rboxes = [self.box(a) for a in r]
        wboxes = [self.box(a) for a in w]
        psum_names = set()
        for name, bx, isdram in rboxes + wboxes:
            if name.startswith("PB") or name.startswith("PT"):
                psum_names.add(name)
        for name in psum_names:
            last = self.psum_last.setdefault(name, {})
            for e2, o2 in last.items():
                if e2 != eng:
                    deps.add(o2)
            last[eng] = opid
        rboxes = [b for b in rboxes if b[0] not in psum_names]
        wboxes = [b for b in wboxes if b[0] not in psum_names]
        for name, bx, isdram in rboxes:
            if name in untracked:
                continue
            h = self.hist.setdefault(name, [])
            for (b2, o2, isw, e2, d2) in h:
                if isw and self.overlap(bx, b2):
                    deps.add(o2)
        for name, bx, isdram in wboxes:
            if name in untracked:
                continue
            h = self.hist.setdefault(name, [])
            for (b2, o2, isw, e2, d2) in h:
                if self.overlap(bx, b2):
                    deps.add(o2)
        for name, bx, isdram in rboxes:
            if name in untracked:
                continue
            h = self.hist[name]
            if not is_dma:
                h[:] = [e for e in h if not ((not e[2]) and e[3] == eng and (not e[4])
                                             and self.contains(bx, e[0]))]
            h.append((bx, opid, False, eng, is_dma))
        for name, bx, isdram in wboxes:
            if name in untracked:
                continue
            h = self.hist[name]
            h[:] = [e for e in h if not self.contains(bx, e[0])]
            h.append((bx, opid, True, eng, is_dma))
        deps.discard(opid)
        self.ops.append(dict(eng=eng, fn=fn, deps=deps, dma_key=dma_key, phase=self.phase,
                             signal=False, inc_override=inc))
        return opid

    def emit(self, final_wait_ops):
        nc = self.nc
        ops = self.ops
        for o in ops:
            if o["eng"] == "pe" and o["dma_key"] is None:
                o["deps"] = {d for d in o["deps"]
                             if not (ops[d]["eng"] == "pe" and ops[d]["dma_key"] is None)}
        for o in ops:
            for d in o["deps"]:
                ops[d]["signal"] = True
        for d in final_wait_ops:
            ops[d]["signal"] = True
        for o in ops:
            if o["dma_key"] is not None:
                o["signal"] = True
        sem_keys = []
        counts = {}
        for o in ops:
            if not o["signal"]:
                continue
            if o["dma_key"] is not None:
                k = ("dma", o["dma_key"])
                inc = 16 if o["inc_override"] is None else o["inc_override"]
            else:
                k = (o["eng"], o["phase"])
                inc = 1
            if k not in counts:
                counts[k] = 0
                sem_keys.append(k)
            counts[k] += inc
            o["sem"] = k
            o["val"] = counts[k]
            o["inc"] = inc
        rng = bass.get_kernel_semaphore_range()
        assert len(sem_keys) <= len(rng) - 2, f"too many semaphores: {len(sem_keys)}"
        sems = {}
        self._sem_cms = []
        for k in sem_keys:
            cm = nc.semaphore(("s_%s_%s" % k).replace("-", "_"))
            s = cm.__enter__()
            self._sem_cms.append(cm)
            sems[k] = s
        waited = {}
        nwait = 0
        issued = {}
        for o in ops:
            eobj = self.eng_objs[o["eng"]]
            need = {}
            for d in o["deps"]:
                od = ops[d]
                k = od["sem"]
                if od["dma_key"] is not None:
                    need[k] = max(need.get(k, 0), issued[k])
                else:
                    need[k] = max(need.get(k, 0), od["val"])
            for k, v in need.items():
                wk = (o["eng"], k)
                if waited.get(wk, 0) >= v:
                    continue
                eobj.wait_ge(sems[k], v)
                waited[wk] = v
                nwait += 1
            ins = o["fn"]()
            if o["signal"]:
                ins.then_inc(sems[o["sem"]], o["inc"])
                if o["dma_key"] is not None:
                    issued[o["sem"]] = o["val"]
        need = {}
        for d in final_wait_ops:
            od = ops[d]
            need[od["sem"]] = max(need.get(od["sem"], 0), od["val"])
        for k, v in need.items():
            nc.sync.wait_ge(sems[k], v)
        self.stats = dict(nops=len(ops), nwait=nwait, nsem=len(sem_keys),
                          maxcount=max(counts.values()) if counts else 0)

    def close(self):
        for cm in reversed(self._sem_cms):
            cm.__exit__(None, None, None)


def ret_gammas():
    return (1.0 - np.exp2(-5.0 - np.arange(RH, dtype=np.float64)))


def host_consts():
    c = {}
    c["ident"] = np.eye(128, dtype=np.float32)
    half = 32
    inv = np.power(np.float32(10000.0), -np.arange(half, dtype=np.float32) / np.float32(half)).astype(np.float32)
    pos = np.arange(SEQ, dtype=np.float32)
    ang = (pos[:, None] * inv[None, :]).astype(np.float32)
    cosp = np.cos(ang).astype(np.float32).reshape(SEQ // 128, 128, half).transpose(1, 0, 2)
    sinp = np.sin(ang).astype(np.float32).reshape(SEQ // 128, 128, half).transpose(1, 0, 2)
    c["cosp"] = np.ascontiguousarray(cosp)
    c["sinp"] = np.ascontiguousarray(sinp)
    c["nsinp"] = np.ascontiguousarray(-sinp)
    poss = (PAST_LEN + (np.arange(128) % DEC_SEQ)).astype(np.float32)
    angs = (poss[:, None] * inv[None, :]).astype(np.float32)
    c["coss"] = np.cos(angs).astype(np.float32).reshape(128, 1, half)
    c["sins"] = np.sin(angs).astype(np.float32).reshape(128, 1, half)
    c["nsins"] = (-c["sins"]).astype(np.float32)
    g = ret_gammas()
    lg = np.log(g)
    j = np.arange(128)[:, None]
    i = np.arange(128)[None, :]
    dec = np.zeros((128, RH, 128), np.float64)
    decs = np.zeros((128, RH, 128), np.float64)
    for h in range(RH):
        dec[:, h, :] = np.where(i >= j, np.exp(lg[h] * np.maximum(i - j, 0)), 0.0) * RDK ** -0.5
        same = (i // DEC_SEQ) == (j // DEC_SEQ)
        decs[:, h, :] = np.where((i >= j) & same, np.exp(lg[h] * np.maximum(i - j, 0)), 0.0) * RDK ** -0.5
    c["decp"] = dec.astype(np.float32)
    c["decs"] = decs.astype(np.float32)
    gq = np.zeros((128, 4, 128), np.float64)
    gqs = np.zeros((128, 4, 128), np.float64)
    gl = np.zeros((128, 4), np.float64)
    gls = np.zeros((128, 4), np.float64)
    ii = np.arange(128)
    for pr in range(4):
        for hh in range(2):
            h = 2 * pr + hh
            gq[hh * 64:(hh + 1) * 64, pr, :] = np.exp(lg[h] * (ii + 1.0))[None, :]
            gqs[hh * 64:(hh + 1) * 64, pr, :] = np.exp(lg[h] * ((ii % DEC_SEQ) + 1.0))[None, :]
            gl[hh * 64:(hh + 1) * 64, pr] = np.exp(lg[h] * 128.0)
            gls[hh * 64:(hh + 1) * 64, pr] = np.exp(lg[h] * float(DEC_SEQ))
    c["gqp"] = gq.astype(np.float32)
    c["gqs"] = gqs.astype(np.float32)
    c["glp"] = gl.astype(np.float32)
    c["gls"] = gls.astype(np.float32)
    wend = np.exp(lg[None, :] * (127.0 - ii[:, None])) * RDK ** -0.5
    wends = np.exp(lg[None, :] * (DEC_SEQ - 1.0 - (ii[:, None] % DEC_SEQ))) * RDK ** -0.5
    c["wendp"] = wend.astype(np.float32)
    c["wends"] = wends.astype(np.float32)
    own = np.where(j <= i, 0.0, NEG)
    prev = np.where(j > i, 0.0, NEG)
    c["mown"] = np.tile(own, (1, 4)).astype(np.float32)
    c["mprev"] = np.tile(prev, (1, 4)).astype(np.float32)
    same = (i // DEC_SEQ) == (j // DEC_SEQ)
    news = np.where(same & (j <= i), 0.0, NEG)
    c["mnews"] = np.tile(news, (1, 4)).astype(np.float32)
    kk = np.arange(128)[:, None]
    t8 = np.arange(DEC_SEQ)[None, :]
    mc = np.where(kk >= t8 + 1, 0.0, NEG)
    c["mcache"] = np.tile(mc, (1, 64)).astype(np.float32)
    bm = (np.arange(128)[:, None] // DEC_SEQ == np.arange(SB_PER_CORE)[None, :]).astype(np.float32)
    c["bmrow"] = bm
    return c


CONST_SHAPES = None


class Builder:
    def __init__(self, cfg):
        self.cfg = cfg
        self.nc = bass.Bass("TRN2", target_bir_lowering=False)
        self.T = Tracker(self.nc)
        self.cms = []
        self.final_ops = []

    def sb(self, name, shape, dt):
        cm = self.nc.sbuf_tensor(name, list(shape), dt)
        t = cm.__enter__()
        self.cms.append(cm)
        return t

    def ps(self, name, shape, dt):
        cm = self.nc.psum_tensor(name, list(shape), dt)
        t = cm.__enter__()
        self.cms.append(cm)
        return t

    def din(self, name, shape, dt=F32):
        return self.nc.dram_tensor(name, list(shape), dt, kind="ExternalInput").ap()

    def dout(self, name, shape, dt=F32):
        return self.nc.dram_tensor(name, list(shape), dt, kind="ExternalOutput").ap()

    def mm(self, out, lhsT, rhs, start, stop, skip=False):
        nc = self.nc
        if skip:
            return self.T.add("pe", lambda: nc.tensor.matmul(out, lhsT=lhsT, rhs=rhs, start=start, stop=stop,
                                                             skip_group_check=True),
                              w=[out], r=[lhsT, rhs, out])
        return self.T.add("pe", lambda: nc.tensor.matmul(out, lhsT=lhsT, rhs=rhs, start=start, stop=stop),
                          w=[out], r=[lhsT, rhs] + ([] if start else [out]))

    def trf(self, out, in_, identf):
        nc = self.nc
        return self.T.add("pe", lambda: nc.tensor.transpose(out=out, in_=in_, identity=identf),
                          w=[out], r=[in_, identf])

    def tr(self, out, in_):
        nc = self.nc
        ident = self.identb[:]
        return self.T.add("pe", lambda: nc.tensor.transpose(out=out, in_=in_, identity=ident),
                          w=[out], r=[in_, ident])

    def act(self, out, in_, func, scale=1.0, bias=None):
        nc = self.nc
        r = [in_]
        kw = {}
        if not isinstance(scale, float):
            r.append(scale)
        if bias is not None:
            kw["bias"] = bias
            if not isinstance(bias, float):
                r.append(bias)
        return self.T.add("act", lambda: nc.scalar.activation(out=out, in_=in_, func=func, scale=scale, **kw),
                          w=[out], r=r)

    def tt(self, out, in0, in1, op, eng="dve"):
        nc = self.nc
        e = nc.vector if eng == "dve" else nc.gpsimd
        return self.T.add(eng, lambda: e.tensor_tensor(out=out, in0=in0, in1=in1, op=op), w=[out], r=[in0, in1])

    def ts(self, out, in0, s1, op0, s2=None, op1=None, eng="dve"):
        nc = self.nc
        e = nc.vector if eng == "dve" else nc.gpsimd
        r = [in0] + [s for s in (s1, s2) if s is not None and not isinstance(s, float)]
        if op1 is None:
            return self.T.add(eng, lambda: e.tensor_scalar(out=out, in0=in0, scalar1=s1, scalar2=None, op0=op0),
                              w=[out], r=r)
        return self.T.add(eng, lambda: e.tensor_scalar(out=out, in0=in0, scalar1=s1, scalar2=s2, op0=op0, op1=op1),
                          w=[out], r=r)

    def stt(self, out, in0, scalar, in1, op0, op1):
        nc = self.nc
        r = [in0, in1] + ([] if isinstance(scalar, float) else [scalar])
        return self.T.add("dve", lambda: nc.vector.scalar_tensor_tensor(out=out, in0=in0, scalar=scalar, in1=in1,
                                                                         op0=op0, op1=op1), w=[out], r=r)

    def cp(self, out, in_, eng="dve"):
        nc = self.nc
        if eng == "act":
            return self.T.add("act", lambda: nc.scalar.copy(out=out, in_=in_), w=[out], r=[in_])
        e = nc.vector if eng == "dve" else nc.gpsimd
        return self.T.add(eng, lambda: e.tensor_copy(out=out, in_=in_), w=[out], r=[in_])

    def memset(self, ap, val, eng="dve"):
        nc = self.nc
        e = nc.vector if eng == "dve" else nc.gpsimd
        return self.T.add(eng, lambda: e.memset(ap, val), w=[ap])

    def dma(self, out, in_, key, q="sp", final=False):
        nc = self.nc
        e = {"sp": nc.sync, "pool": nc.gpsimd, "act": nc.scalar}[q]
        if key is None or not key.startswith("!"):
            on, inn = out.tensor.name, in_.tensor.name
            key = ("st_" + inn) if on in self.dram_names else ("ld_" + on)
        op = self.T.add(q, lambda: e.dma_start(out=out, in_=in_), w=[out], r=[in_], dma_key=key,
                        untracked=self.untracked)
        if final:
            self.final_ops.append(op)
        return op

    def bnstats(self, out, in_):
        nc = self.nc
        return self.T.add("dve", lambda: nc.vector.bn_stats(out=out, in_=in_), w=[out], r=[in_])

    def bnaggr(self, out, in_):
        nc = self.nc
        return self.T.add("dve", lambda: nc.vector.bn_aggr(out=out, in_=in_), w=[out], r=[in_])

    def recip(self, out, in_):
        nc = self.nc
        return self.T.add("dve", lambda: nc.vector.reciprocal(out=out, in_=in_), w=[out], r=[in_])


def V(t, off, dims):
    f = 1
    for s in t.shape[1:]:
        f *= s
    return bass.AP(t, off, [[f, 128]] + [list(d) for d in dims])


def VP(t, p0, pn, off, dims):
    f = 1
    for s in t.shape[1:]:
        f *= s
    return bass.AP(t, p0 * f + off, [[f, pn]] + [list(d) for d in dims])


def AV(view, off, dims, p0=0, pn=128):
    f = view.ap[0][0]
    base = view.offset
    return bass.AP(view.tensor, base + p0 * f + off, [[f, pn]] + [list(d) for d in dims])


def build(cfg):
    nsc = cfg["nsc"]
    nlay = cfg["nlay"]
    stage = cfg.get("stage", 99)
    sstage = cfg.get("sstage", 99)
    dbgname = cfg.get("dbg", None)
    do_sample = cfg["sample"]
    B = Builder(cfg)
    nc = B.nc
    T = B.T
    NTOK = nsc * 512
    x_p = B.din("x_p", [NTOK, D])
    mem_p = B.din("mem_p", [NMEM, D])
    x_s = B.din("x_s", [128, D])
    st_ret = B.din("st_ret", [DEPTH, SB_PER_CORE, RH, RDK, RDV])
    c_sk = B.din("c_sk", [DEPTH, SB_PER_CORE, 128, 128])
    c_sv = B.din("c_sv", [DEPTH, SB_PER_CORE, 128, 128])
    c_mk = B.din("c_mk", [DEPTH, SB_PER_CORE, NMEM, 512])
    c_mv = B.din("c_mv", [DEPTH, SB_PER_CORE, NMEM, 512])
    w_in = B.din("w_in", [DEPTH, D, IN_W])
    w_br_ret = B.din("w_br_ret", [DEPTH, 1024, D])
    w_br_swa = B.din("w_br_swa", [DEPTH, 512, D])
    w_br_mem = B.din("w_br_mem", [DEPTH, 512, D])
    w_out = B.din("w_out", [DEPTH, D, D])
    w_mem_kv = B.din("w_mem_kv", [DEPTH, D, 1024])
    sinks = B.din("sinks", [DEPTH, SH])
    gn_g = B.din("gn_g", [DEPTH, 1024])
    ln1_g = B.din("ln1_g", [DEPTH, D])
    ln1_b = B.din("ln1_b", [DEPTH, D])
    w_up = B.din("w_up", [DEPTH, D, DFF])
    w_down = B.din("w_down", [DEPTH, DFF, D])
    ln2_g = B.din("ln2_g", [DEPTH, D])
    ln2_b = B.din("ln2_b", [DEPTH, D])
    own = dict(w_in=B.din("wo_in", [D, IN_W]), w_br_ret=B.din("wo_br_ret", [1024, D]),
               w_br_swa=B.din("wo_br_swa", [512, D]), w_br_mem=B.din("wo_br_mem", [512, D]),
               w_out=B.din("wo_out", [D, D]), w_mem_kv=B.din("wo_mem_kv", [D, 1024]),
               sinks=B.din("o_sinks", [1, SH]), gn_g=B.din("o_gn_g", [1024]), ln1_g=B.din("o_ln1_g", [D]),
               ln1_b=B.din("o_ln1_b", [D]), w_up=B.din("wo_up", [D, DFF]), w_down=B.din("wo_down", [DFF, D]),
               ln2_g=B.din("o_ln2_g", [D]), ln2_b=B.din("o_ln2_b", [D]))
    own_names = [v.tensor.name for v in own.values()]
    sinks_full = sinks
    w_in = [w_in[0], w_in[1], own["w_in"]]
    w_br_ret = [w_br_ret[0], w_br_ret[1], own["w_br_ret"]]
    w_br_swa = [w_br_swa[0], w_br_swa[1], own["w_br_swa"]]
    w_br_mem = [w_br_mem[0], w_br_mem[1], own["w_br_mem"]]
    w_out = [w_out[0], w_out[1], own["w_out"]]
    w_mem_kv = [w_mem_kv[0], w_mem_kv[1], own["w_mem_kv"]]
    w_up = [w_up[0], w_up[1], own["w_up"]]
    w_down = [w_down[0], w_down[1], own["w_down"]]
    gn_g = [gn_g[0], gn_g[1], own["gn_g"]]
    ln1_g = [ln1_g[0], ln1_g[1], own["ln1_g"]]
    ln1_b = [ln1_b[0], ln1_b[1], own["ln1_b"]]
    ln2_g = [ln2_g[0], ln2_g[1], own["ln2_g"]]
    ln2_b = [ln2_b[0], ln2_b[1], own["ln2_b"]]
    NSLOT = nsc + 2
    sel_d = B.din("sel", [128, 2])
    flag_d = B.din("flagneg", [128, 512])
    r_cos = B.din("r_cos", [128, NSLOT * 4, 32])
    r_sin = B.din("r_sin", [128, NSLOT * 4, 32])
    r_nsin = B.din("r_nsin", [128, NSLOT * 4, 32])
    cc_src = [nc.dram_tensor("cc_src%d" % i, [512, D], F32, kind="Internal").ap() for i in range(2)]
    cc_dst = [nc.dram_tensor("cc_dst%d" % i, [2 * 512, D], F32, kind="Internal").ap() for i in range(2)]
    hc = host_consts()
    cd = {k: B.din("c_" + k, list(v.shape)) for k, v in hc.items()}
    y_p = B.dout("y_p", [NSLOT * 512, D])
    y_s = B.dout("y_s", [128, D])
    rs_p = B.dout("rs_p", [2, RH, RDK, RDV])
    sk_p = B.dout("sk_p", [2, 128, 128])
    sv_p = B.dout("sv_p", [2, 128, 128])
    mk_p = B.dout("mk_p", [NMEM, 512])
    mv_p = B.dout("mv_p", [NMEM, 512])
    rs_s = B.dout("rs_s", [DEPTH, SB_PER_CORE, RH, RDK, RDV])
    sk_s = B.dout("sk_s", [DEPTH, SB_PER_CORE, 128, 128])
    sv_s = B.dout("sv_s", [DEPTH, SB_PER_CORE, 128, 128])
    B.dram_names = set(t.tensor.name for t in [y_p, y_s, rs_p, sk_p, sv_p, mk_p, mv_p, rs_s, sk_s, sv_s])
    B.untracked = set(n.tensor.name for n in
                      [x_p, mem_p, x_s, st_ret, c_sk, c_sv, c_mk, c_mv, w_in[0], w_br_ret[0], w_br_swa[0], w_br_mem[0],
                       w_out[0],
                       w_mem_kv[0], sinks, gn_g[0], ln1_g[0], ln1_b[0], w_up[0], w_down[0], ln2_g[0], ln2_b[0],
                       sel_d, flag_d, r_cos, r_sin, r_nsin]
                      + list(cd.values())) | set(own_names)

    sb = B.sb
    B.identb = sb("identb", [128, 128], BF16)
    onesb = sb("onesb", [128, 128], BF16)
    X = sb("X", [128, 4, D], F32)
    XT = sb("XT", [128, 8, 512], BF16)
    Xb = sb("Xb", [128, D], BF16)
    ARENA = sb("ARENA", [128, 32 * 512], BF16)
    HT = ARENA[:].rearrange("p (a b) -> p a b", a=32)
    GT = ARENA[:, 0:24 * 512].rearrange("p (a b) -> p a b", a=24)
    MQT = ARENA[:, 24 * 512:28 * 512].rearrange("p (a b) -> p a b", a=4)
    RQT = ARENA[:, 28 * 512:32 * 512].rearrange("p (a b) -> p a b", a=4)
    PV_ = dict(HT=HT, GT=GT, MQT=MQT, RQT=RQT)
    SV_ = dict(HT=ARENA[:, 0:32 * 128].rearrange("p (a b) -> p a b", a=32),
               GT=ARENA[:, 0:24 * 128].rearrange("p (a b) -> p a b", a=24),
               MQT=ARENA[:, 24 * 128:28 * 128].rearrange("p (a b) -> p a b", a=4),
               RQT=ARENA[:, 28 * 128:32 * 128].rearrange("p (a b) -> p a b", a=4))
    a0 = 32 * 128
    RKwX = ARENA[:, a0:a0 + 4096].rearrange("p (r c) -> p r c", r=8)
    S0bg = ARENA[:, a0 + 4096:a0 + 6144].rearrange("p (b r e) -> p b r e", b=4, r=4)
    Ec = ARENA[:, a0 + 6144:a0 + 7168].rearrange("p (b g c) -> p b g c", b=16, g=2)
    KcT = ARENA[:, a0 + 7168:a0 + 9216].rearrange("p (b k) -> p b k", b=16)
    Kraw = ARENA[:, a0 + 9216:a0 + 11264].rearrange("p (b k) -> p b k", b=16)
    SQTs = ARENA[:, a0 + 11264:a0 + 12288].rearrange("p (g c) -> p g c", g=2)
    S0g = X[:, 1:3, :].rearrange("p a (b e) -> p (a b) e", e=128).rearrange("p (b r) e -> p b r e", b=4)
    Vcp = X[:, 3, :].bitcast(BF16).rearrange("p (b g h c) -> p b g h c", b=4, g=2, h=2)
    identF = sb("identF", [128, 128], F32)
    BMR = sb("BMR", [128, 16], F32)
    RKm = sb("RKm", [128, 4, 8, 128], BF16)
    SQT = sb("SQT", [128, 4, 512], BF16)
    SKm = sb("SKm", [128, 5, 2, 128], BF16)
    RKw = sb("RKw", [128, 4, 512], BF16)
    Vb = sb("Vb", [128, 4, 1024], BF16)
    MT = Vb[:].rearrange("p a (k t) -> p (a k) t", k=2)
    SRG = sb("SRG", [128, 4, 1024], BF16)
    SVp = sb("SVp", [128, 5, 2, 2, 128], BF16)
    ONp = sb("ONp", [128, 2, 128], BF16)
    GRT = sb("GRT", [128, 8, 512], BF16)
    SWOT = sb("SWOT", [128, 4, 512], BF16)
    MOT = sb("MOT", [128, 4, 512], BF16)
    T1 = sb("T1", [128, 512], F32)
    T2 = sb("T2", [128, 512], F32)
    RF = sb("RF", [128, 512], F32)
    RQb = sb("RQb", [128, 512], BF16)
    ST = sb("ST", [128, 8, 128], BF16)
    RQs2 = sb("RQs2", [128, 8, 128], BF16)
    Y = sb("Y", [128, 8, 128], F32)
    GR = sb("GR", [128, 1024], BF16)
    STAT = sb("STAT", [128, 8, 6], F32)
    MV = sb("MV", [128, 8, 2], F32)
    E = sb("E", [128, 4, 512], BF16)
    MEMT = E[:].rearrange("p a (k m) -> p (a k) m", k=2)
    identf = T2[:, 0:128]
    MSKf = T1
    DEN = sb("DEN", [128, 512], F32)
    LNB = sb("LNB", [128, 2, D], F32)
    GNG = sb("GNG", [128, 1024], F32)
    NW = 3
    WR = [sb("WR%d" % i, [128, 8, 512], BF16) for i in range(NW)]
    ROPE = sb("ROPE", [128, 3, 4, 32], F32)
    DEC = sb("DEC", [128, 8, 128], F32)
    GQ = sb("GQ", [128, 4, 128], F32)
    WEND = sb("WEND", [128, 8], F32)
    GL = sb("GL", [128, 4], F32)
    MOWN = sb("MOWN", [128, 512], BF16)
    MPREV = sb("MPREV", [128, 512], BF16)
    SINKE = sb("SINKE", [128, 3, 4], F32)
    Sst = [sb("Sst", [128, 4, 128], F32)] * 3
    Sbf = [sb("Sbf", [128, 4, 128], BF16)] * 3
    SKTprev = [sb("SKTp", [128, 2, 128], BF16)] * 3
    SVprev = [sb("SVpv", [128, 2, 2, 128], BF16)] * 3
    MKT = [sb("MKT", [128, 4, 256], BF16)] * 3
    MVb = [sb("MVb", [128, 2, 512], BF16)] * 3
    SEL = sb("SEL", [128, 2], F32)
    XbN = sb("XbN", [128, 4, D], BF16)
    XbS = Xb[:].bitcast(F32)
    FLAGN = sb("FLAGN", [128, 512], BF16)
    SKf = sb("SKf", [128, 256], F32)
    PB = [B.ps("PB%d" % i, [128, 512], F32) for i in range(6)]
    PT = [B.ps("PT%d" % i, [128, 1024], BF16) for i in range(2)]

    B.dma(identf, cd["ident"], "c0")
    B.cp(B.identb[:], identf)
    B.memset(onesb[:], 1.0)
    B.memset(ONp[:], 0.0)
    B.memset(ONp[:, 0, 0:64], 1.0)
    B.memset(ONp[:, 1, 64:128], 1.0)
    B.memset(SVp[:], 0.0)
    B.memset(RKm[:], 0.0)
    B.memset(SKm[:], 0.0)
    B.memset(RQs2[:], 0.0)
    for l in range(1):
        B.memset(SVprev[l][:], 0.0)
        B.memset(SKTprev[l][:], 0.0)
        B.memset(Sst[l][:], 0.0)
        B.memset(Sbf[l][:], 0.0)
    B.dma(SEL[:], sel_d, None)
    B.dma(MSKf[:], flag_d, None)
    B.cp(FLAGN[:], MSKf[:])
    B.dma(MSKf[:], cd["mown"], "c0")
    B.cp(MOWN[:], MSKf[:])
    B.dma(MSKf[:], cd["mprev"], "c0")
    B.cp(MPREV[:], MSKf[:])
    SKRAW = sb("SKRAW", [128, 3, SH], F32)
    B.dma(SKRAW[:, 0:2, :], bass.AP(sinks_full.tensor, 0, [[0, 128], [SH, DEPTH], [1, SH]]), "c0")
    B.dma(SKRAW[:, 2, :], bass.AP(own["sinks"].tensor, 0, [[0, 128], [1, SH]]), "c0")
    for hh in range(2):
        B.act(SINKE[hh * 64:(hh + 1) * 64, :, :],
              AV(SKRAW[:], hh, [[SH, 3], [2, 4]], p0=hh * 64, pn=64), AF.Exp)

    def load_prompt_consts():
        B.dma(DEC[:], cd["decp"], "c0")
        B.dma(GQ[:], cd["gqp"], "c0")
        B.dma(WEND[:], cd["wendp"], "c0")
        B.dma(GL[:], cd["glp"], "c0")

    wstate = dict(idx=0, plan=[])

    def wplan_for_pass(l, first):
        pl = []
        if first:
            pl.append((w_mem_kv[l][:, 0:512], 8, 512))
            pl.append((w_mem_kv[l][:, 512:1024], 8, 512))
        for j in range(7):
            pl.append((w_in[l][:, j * 512:(j + 1) * 512], 8, 512))
        pl.append((w_in[l][:, 3584:3840], 8, 256))
        for jj in range(7):
            pl.append((w_in[l][:, 3840 + jj * 512:3840 + (jj + 1) * 512], 8, 512))
        for cb in range(2):
            pl.append((w_br_ret[l][:, cb * 512:(cb + 1) * 512], 8, 512))
            pl.append(((w_br_swa[l][:, cb * 512:(cb + 1) * 512], w_br_mem[l][:, cb * 512:(cb + 1) * 512]), 8, 512))
        for cb in range(2):
            pl.append((w_out[l][:, cb * 512:(cb + 1) * 512], 8, 512))
        for jb in range(8):
            pl.append((w_up[l][:, jb * 512:(jb + 1) * 512], 8, 512))
        for cb in range(2):
            for rb in range(4):
                pl.append((w_down[l][rb * 1024:(rb + 1) * 1024, cb * 512:(cb + 1) * 512], 8, 512))
        return pl

    def w_issue(i):
        src, nkt, ncols = wstate["plan"][i]
        slot = WR[i % NW]
        if isinstance(src, tuple):
            for half, s in enumerate(src):
                B.dma(slot[:, half * 4:(half + 1) * 4, 0:ncols], s.rearrange("(kt p) c -> p kt c", p=128),
                      "w%d" % (i % NW), q="pool")
        else:
            B.dma(slot[:, 0:nkt, 0:ncols], src.rearrange("(kt p) c -> p kt c", p=128), "w%d" % (i % NW), q="pool")

    PF = 1

    def w_next():
        i = wstate["idx"]
        if i == 0:
            for k in range(min(PF, len(wstate["plan"]))):
                w_issue(k)
        if i + PF < len(wstate["plan"]):
            w_issue(i + PF)
        wstate["idx"] = i + 1
        return WR[i % NW]

    def transposes_to(dst_view_fn, src_fn, n, ptile, evac_eng="dve"):
        for k in range(n):
            B.tr(ptile[:, k * 128:(k + 1) * 128], src_fn(k))
        B.cp(dst_view_fn, ptile[:, 0:n * 128].rearrange("p (a b) -> p a b", a=n), eng=evac_eng)

    def rope_block(ps, nheads, tabidx, out_ap_f32=None, out_ap_bf=None, sample=False, perm=False):
        w = nheads * 64
        cosb = AV(ROPE[:], (0 * 4 + tabidx) * 32, [[0, nheads], [0, 2], [1, 32]])
        sinb = AV(ROPE[:], (1 * 4 + tabidx) * 32, [[0, nheads], [1, 32]])
        nsinb = AV(ROPE[:], (2 * 4 + tabidx) * 32, [[0, nheads], [1, 32]])
        psv = ps.rearrange("p (h t f) -> p h t f", h=nheads, t=2)
        t1v = T1[:, 0:w].rearrange("p (h t f) -> p h t f", h=nheads, t=2)
        t2v = T2[:, 0:w].rearrange("p (h t f) -> p h t f", h=nheads, t=2)
        B.tt(t1v, psv, cosb, ALU.mult)
        B.tt(t2v[:, :, 0, :], psv[:, :, 1, :], nsinb, ALU.mult)
        B.tt(t2v[:, :, 1, :], psv[:, :, 0, :], sinb, ALU.mult)
        if out_ap_f32 is not None:
            B.tt(out_ap_f32, T1[:, 0:w], T2[:, 0:w], ALU.add)
            if out_ap_bf is not None:
                B.cp(out_ap_bf, out_ap_f32, eng="act")
        elif perm:
            B.tt(out_ap_bf, T1[:, 0:w].rearrange("p (s t d) -> p s t d", s=2, t=4),
                 T2[:, 0:w].rearrange("p (s t d) -> p s t d", s=2, t=4), ALU.add)
        else:
            B.tt(out_ap_bf, T1[:, 0:w], T2[:, 0:w], ALU.add)

    RSTD = sb("RSTD", [128, 4], F32)

    def layer_norm_chunks(nch):
        for c in range(nch):
            for a in range(2):
                B.bnstats(STAT[:, 2 * c + a, :], X[:, c, a * 512:(a + 1) * 512])
            B.bnaggr(MV[:, c, :], STAT[:, 2 * c:2 * c + 2, :].rearrange("p a b -> p (a b)"))
        B.ts(RSTD[:, 0:nch], MV[:, 0:nch, 1], LN_EPS, ALU.add)
        B.act(RSTD[:, 0:nch], RSTD[:, 0:nch], AF.Ln)
        B.act(RSTD[:, 0:nch], RSTD[:, 0:nch], AF.Exp, scale=-0.5)
        for c in range(nch):
            xc = X[:, c, :]
            B.ts(xc, xc, MV[:, c, 0:1], ALU.subtract, RSTD[:, c:c + 1], ALU.mult)
            B.tt(xc, xc, LNB[:, 0, :], ALU.mult)
            B.tt(xc, xc, LNB[:, 1, :], ALU.add)

    def gn_part1(ops):
        for h in range(8):
            B.bnstats(STAT[:, h, :], ops[h // 4][:, (h % 4) * 128:(h % 4 + 1) * 128])
        for h in range(8):
            B.bnaggr(MV[:, h, :], STAT[:, h, :])
        B.ts(MV[:, :, 1], MV[:, :, 1], GN_EPS, ALU.add)
        B.act(MV[:, :, 1], MV[:, :, 1], AF.Ln)
        B.act(MV[:, :, 1], MV[:, :, 1], AF.Exp, scale=-0.5)
        for half in range(2):
            pv = ops[half][:].rearrange("p (h e) -> p h e", h=4)
            B.tt(Y[:, half * 4:(half + 1) * 4, :], pv,
                 AV(MV[:], half * 8, [[2, 4], [0, 128]]), ALU.subtract)

    def gn_part2(c):
        B.tt(Y[:], Y[:], AV(MV[:], 1, [[2, 8], [0, 128]]), ALU.mult)
        B.tt(Y[:], Y[:], GNG[:].rearrange("p (h e) -> p h e", h=8), ALU.mult)
        B.tt(GR[:], Y[:].rearrange("p h e -> p (h e)"), SRG[:, c, :], ALU.mult)
        transposes_to(GRT[:, :, c * 128:(c + 1) * 128], lambda k: GR[:, k * 128:(k + 1) * 128], 8, PT[0],
                      evac_eng="act")

    def group_norm_gate(ops, c, nch_tok):
        gn_part1(ops)
        gn_part2(c)

    EM = sb("EM", [128, 2, 512], BF16)

    M = dict(PV_)

    def bcast_row(dram_ap_row, n):
        return bass.AP(dram_ap_row.tensor, dram_ap_row.offset, [[0, 128], [1, n]])

    def compute_memT():
        for mb in range(2):
            for half in range(2):
                B.dma(T1[:], mem_p[mb * 128:(mb + 1) * 128, half * 512:(half + 1) * 512], "x")
                B.cp(Xb[:, 0:512], T1[:], eng="act")
                transposes_to(MEMT[:, half * 4:(half + 1) * 4, mb * 128:(mb + 1) * 128],
                              lambda k: Xb[:, k * 128:(k + 1) * 128], 4, PT[half])

    def mem_kv(l):
        import os
        SUB = int(os.environ.get("SUB", "99"))
        compute_memT()
        Wk = w_next()
        for mb in range(2):
            ps = PB[mb]
            for kt in range(8):
                B.mm(ps[:], MEMT[:, kt, mb * 128:(mb + 1) * 128], Wk[:, kt, :], kt == 0, kt == 7)
            if SUB < 1:
                continue
            B.cp(T1[:], ps[:], eng="act")
            if SUB < 2:
                continue
            B.dma(mk_p[mb * 128:(mb + 1) * 128, :], T1[:], "o_mk", final=True)
            if SUB < 3:
                continue
            B.cp(RQb[:], ps[:])
            transposes_to(MKT[l][:, :, mb * 128:(mb + 1) * 128], lambda k: RQb[:, k * 128:(k + 1) * 128], 4, PT[mb])
        if SUB < 4:
            return
        Wv = w_next()
        for mb in range(2):
            ps = PB[2 + mb]
            for kt in range(8):
                B.mm(ps[:], MEMT[:, kt, mb * 128:(mb + 1) * 128], Wv[:, kt, :], kt == 0, kt == 7)
            B.cp(T2[:], ps[:], eng="act")
            B.dma(mv_p[mb * 128:(mb + 1) * 128, :], T2[:], "o_mv", final=True)
            B.cp(MVb[l][:, mb, :], ps[:])

    def load_layer_params(l, with_gn=True):
        if with_gn:
            B.dma(GNG[:], bcast_row(gn_g[l], 1024), "prm")
        B.dma(LNB[:, 0, :], bcast_row(ln1_g[l], D), "prm")
        B.dma(LNB[:, 1, :], bcast_row(ln1_b[l], D), "prm")

    def xT_phase(nch):
        for c in range(nch):
            B.cp(Xb[:], X[:, c, :], eng="act")
            transposes_to(XT[:, :, c * 128:(c + 1) * 128], lambda k: Xb[:, k * 128:(k + 1) * 128], 8, PT[c % 2])

    RQbB = sb("RQbB", [128, 512], BF16)

    def in_proj(l, nch, sample, last_chunk_out=None, mid_hook=None):
        NTk = nch * 128
        pend = [None]
        par = [0]

        def defer(fn):
            if pend[0] is not None:
                pend[0]()
            pend[0] = fn

        for j in range(8):
            W = w_next()
            ncols = 512 if j < 7 else 256
            for c in range(nch):
                ps = PB[(j * nch + c) % 6]
                for kt in range(8):
                    B.mm(ps[:, 0:ncols], XT[:, kt, c * 128:(c + 1) * 128], W[:, kt, 0:ncols], kt == 0, kt == 7)
                tab = 0 if sample else c
                if j in (0, 1, 6, 7):
                    par[0] ^= 1
                    RQ = (RQb, RQbB)[par[0]]
                    ptb = PT[par[0]]
                if j == 0:
                    rope_block(ps[:], 8, tab, out_ap_bf=RQ[:])

                    def post(RQ=RQ, ptb=ptb, c=c):
                        transposes_to(M["RQT"][:, :, c * 128:(c + 1) * 128], lambda k: RQ[:, k * 128:(k + 1) * 128], 4, ptb)
                    defer(post)
                elif j == 1:
                    rope_block(ps[:], 8, tab, out_ap_f32=RF[:], out_ap_bf=RQ[:])
                    B.tt(RKw[:, c, :].rearrange("p (h d) -> p h d", h=8), RF[:].rearrange("p (h d) -> p h d", h=8),
                         AV(WEND[:], 0, [[1, 8], [0, 64]]), ALU.mult)

                    def post(RQ=RQ, ptb=ptb, c=c):
                        for k in range(4):
                            B.tr(ptb[:, k * 128:(k + 1) * 128], RQ[:, k * 128:(k + 1) * 128])
                        for hh in range(2):
                            B.cp(AV(RKm[:], (c * 8 + hh) * 128, [[256, 4], [1, 128]], p0=hh * 64, pn=64),
                                 ptb[hh * 64:(hh + 1) * 64, 0:512].rearrange("p (a b) -> p a b", a=4))
                    defer(post)
                elif j in (2, 3):
                    B.cp(Vb[:, c, (j - 2) * 512:(j - 1) * 512], ps[:], eng="act")
                elif j in (4, 5):
                    B.act(SRG[:, c, (j - 4) * 512:(j - 3) * 512], ps[:], AF.Silu)
                elif j == 6:
                    rope_block(ps[:], 8, tab, out_ap_bf=AV(RQ[:], 0, [[64, 2], [128, 4], [1, 64]]), perm=True)

                    def post(RQ=RQ, ptb=ptb, c=c):
                        transposes_to(SQT[:, c, :].rearrange("p (a b) -> p a b", a=4),
                                      lambda k: RQ[:, k * 128:(k + 1) * 128], 4, ptb)
                    defer(post)
                else:
                    rope_block(ps[:, 0:128], 2, tab, out_ap_f32=SKf[:, 0:128], out_ap_bf=RQ[:, 0:128])
                    B.cp(SKf[:, 128:256], ps[:, 128:256], eng="act")
                    for g in range(2):
                        for hh in range(2):
                            B.cp(SVp[:, 1 + c, g, hh, hh * 64:(hh + 1) * 64], SKf[:, 128 + g * 64:128 + (g + 1) * 64])
                    if last_chunk_out is not None and c == nch - 1:
                        last_chunk_out()

                    def post(RQ=RQ, ptb=ptb, c=c):
                        B.tr(ptb[:, 0:128], RQ[:, 0:128])
                        for g in range(2):
                            B.cp(SKm[g * 64:(g + 1) * 64, 1 + c, g, :], ptb[g * 64:(g + 1) * 64, 0:128])
                    defer(post)
        defer(None)
        for jj in range(7):
            if jj == 1 and mid_hook is not None:
                mid_hook()
            W = w_next()
            for t in range(4):
                ti = jj * 4 + t
                ps = PB[ti % 6]
                for kt in range(8):
                    B.mm(ps[:, 0:NTk], W[:, kt, t * 128:(t + 1) * 128], XT[:, kt, 0:NTk], kt == 0, kt == 7)
                if ti < 4:
                    B.cp(M["MQT"][:, ti, 0:NTk], ps[:, 0:NTk], eng="act")
                else:
                    B.act(M["GT"][:, ti - 4, 0:NTk], ps[:, 0:NTk], AF.Sigmoid)

    def swa_pv(l, c, has_prev, slot_prev, slot_own):
        for pr in range(4):
            g = pr // 2
            for which in range(2):
                dst = (PB[5], PB[0])[which][:, pr * 128:(pr + 1) * 128]
                first = True
                for hh in range(2):
                    t = 2 * (pr % 2) + hh
                    for blk in range(2):
                        if blk == 0 and not has_prev:
                            continue
                        slot = slot_prev if blk == 0 else slot_own
                        lhs = SVp[:, slot, g, hh, :] if which == 0 else ONp[:, hh, :]
                        B.mm(dst, lhs, E[:, g * 2 + blk, t * 128:(t + 1) * 128], first, hh == 1 and blk == 1)
                        first = False

    def swa_norm(l, c):
        for pr in range(4):
            B.ts(DEN[:, pr * 128:(pr + 1) * 128], PB[0][:, pr * 128:(pr + 1) * 128], SINKE[:, l, pr:pr + 1], ALU.add)
        B.recip(DEN[:], DEN[:])
        B.tt(SWOT[:, :, c * 128:(c + 1) * 128], PB[5][:].rearrange("p (a i) -> p a i", a=4),
             DEN[:].rearrange("p (a i) -> p a i", a=4), ALU.mult)

    def swa_pv_norm(l, c, has_prev, slot_prev, slot_own):
        swa_pv(l, c, has_prev, slot_prev, slot_own)
        swa_norm(l, c)

    def branches_out_mlp(l, nch, mid_hook=None, tail_hook=None):
        NTk = nch * 128
        for cb in range(2):
            Wret = w_next()
            Wsm = w_next()
            for o4 in range(4):
                ot = cb * 4 + o4
                bk = (ot % 2) * 3
                pr_, psw, pme = PB[bk], PB[bk + 1], PB[bk + 2]
                for kt in range(8):
                    B.mm(pr_[:, 0:NTk], Wret[:, kt, o4 * 128:(o4 + 1) * 128], GRT[:, kt, 0:NTk], kt == 0, kt == 7)
                for kt in range(4):
                    B.mm(psw[:, 0:NTk], Wsm[:, kt, o4 * 128:(o4 + 1) * 128], SWOT[:, kt, 0:NTk], kt == 0, kt == 3)
                for kt in range(4):
                    B.mm(pme[:, 0:NTk], Wsm[:, 4 + kt, o4 * 128:(o4 + 1) * 128], MOT[:, kt, 0:NTk], kt == 0, kt == 3)
                B.tt(T1[:, 0:NTk], pr_[:, 0:NTk], M["GT"][:, ot, 0:NTk], ALU.mult)
                B.tt(T2[:, 0:NTk], psw[:, 0:NTk], M["GT"][:, 8 + ot, 0:NTk], ALU.mult)
                B.tt(RF[:, 0:NTk], pme[:, 0:NTk], M["GT"][:, 16 + ot, 0:NTk], ALU.mult)
                B.tt(T1[:, 0:NTk], T1[:, 0:NTk], T2[:, 0:NTk], ALU.add)
                B.tt(MT[:, ot, 0:NTk], T1[:, 0:NTk], RF[:, 0:NTk], ALU.add)
        for cb in range(2):
            Wo = w_next()
            for c in range(nch):
                ps = PB[(cb * nch + c) % 6]
                for kt in range(8):
                    B.mm(ps[:], MT[:, kt, c * 128:(c + 1) * 128], Wo[:, kt, :], kt == 0, kt == 7)
                xs = X[:, c, cb * 512:(cb + 1) * 512]
                B.stt(xs, xs, ALPHA, ps[:], ALU.mult, ALU.add)
        layer_norm_chunks(nch)
        B.dma(LNB[:, 0, :], bcast_row(ln2_g[l], D), "prm")
        B.dma(LNB[:, 1, :], bcast_row(ln2_b[l], D), "prm")
        xT_phase(nch)
        for jb in range(8):
            W = w_next()
            for t in range(4):
                ft = jb * 4 + t
                ps = PB[ft % 6]
                R = (T1, T2)[ft % 2]
                for kt in range(8):
                    B.mm(ps[:, 0:NTk], W[:, kt, t * 128:(t + 1) * 128], XT[:, kt, 0:NTk], kt == 0, kt == 7)
                B.act(R[:, 0:NTk], ps[:, 0:NTk], AF.Relu)
                B.tt(M["HT"][:, ft, 0:NTk], R[:, 0:NTk], R[:, 0:NTk], ALU.mult)
        for cb in range(2):
            for rb in range(4):
                W = w_next()
                for c in range(nch):
                    for k8 in range(8):
                        B.mm(PB[c][:], M["HT"][:, rb * 8 + k8, c * 128:(c + 1) * 128], W[:, k8, :],
                             rb == 0 and k8 == 0, rb == 3 and k8 == 7)
            if cb == 1 and tail_hook is not None:
                tail_hook()
            for c in range(nch):
                xs = X[:, c, cb * 512:(cb + 1) * 512]
                B.stt(xs, xs, ALPHA, PB[c][:], ALU.mult, ALU.add)
            if cb == 0 and mid_hook is not None:
                mid_hook()
        layer_norm_chunks(nch)

    def load_rope(t):
        chunk0 = t * 4
        B.dma(ROPE[:, 0, :, :], r_cos[:, chunk0:chunk0 + 4, :], "rope")
        B.dma(ROPE[:, 1, :, :], r_sin[:, chunk0:chunk0 + 4, :], "rope")
        B.dma(ROPE[:, 2, :, :], r_nsin[:, chunk0:chunk0 + 4, :], "rope")

    def prefetch_input_bf16(t):
        for c in range(4):
            for half in range(2):
                dst = XbN[:, c, half * 512:(half + 1) * 512]
                if t < nsc:
                    B.dma(RF[:], x_p[t * 512 + c * 128:t * 512 + (c + 1) * 128, half * 512:(half + 1) * 512], None)
                if t >= 2:
                    B.dma(XbS, cc_dst[t % 2][c * 128:(c + 1) * 128, half * 512:(half + 1) * 512], None)
                if t < 2:
                    B.ts(dst, RF[:], SEL[:, 0:1], ALU.mult)
                elif t < nsc:
                    B.ts(RF[:], RF[:], SEL[:, 0:1], ALU.mult)
                    B.stt(dst, XbS, SEL[:, 1:2], RF[:], ALU.mult, ALU.add)
                else:
                    B.ts(dst, XbS, SEL[:, 1:2], ALU.mult)

    def xt_from_prefetch():
        for c in range(4):
            transposes_to(XT[:, :, c * 128:(c + 1) * 128], lambda k: XbN[:, c, k * 128:(k + 1) * 128], 8, PT[c % 2],
                          evac_eng="act")

    def load_x_fp32(sc):
        if sc < nsc:
            B.dma(X[:], x_p[sc * 512:(sc + 1) * 512, :].rearrange("(c p) d -> p c d", p=128), "x")
            Xf = X[:].rearrange("p c d -> p (c d)")
            B.ts(Xf, Xf, SEL[:, 0:1], ALU.mult)
        else:
            B.memset(X[:], 0.0)
        if sc >= 2:
            gsrc = cc_dst[sc % 2]
            for c in range(4):
                for half in range(2):
                    stg = (T1, T2)[(c * 2 + half) % 2]
                    B.dma(stg[:], gsrc[c * 128:(c + 1) * 128, half * 512:(half + 1) * 512], None)
                    xs = X[:, c, half * 512:(half + 1) * 512]
                    B.stt(xs, stg[:], SEL[:, 1:2], xs, ALU.mult, ALU.add)

    def prompt_pass(sc, l):
        if sc == 0:
            load_rope(0)
        load_layer_params(l, with_gn=(sc == 0))
        if sc == 0:
            mem_kv(l)
        if sc == 0:
            prefetch_input_bf16(0)
            xt_from_prefetch()
        B.cp(SKm[:, 0], SKTprev[l][:])
        B.cp(SVp[:, 0], SVprev[l][:])
        ver = {nsc - 1: 0, nsc + 1: 1}.get(sc, None)
        last = ver is not None

        def last_out():
            B.dma(sk_p[ver], SKf[:, 0:128], "o_sk", final=True)
            B.dma(sv_p[ver], SKf[:, 128:256], "o_sk", final=True)
        in_proj(l, 4, False, last_out if last else None, mid_hook=lambda: load_x_fp32(sc))
        S, Sb_ = Sst[l], Sbf[l]
        DENm = sb("DENm%d" % sc, [128, 512], F32) if False else DEN
        for c in range(4):
            cs = slice(c * 128, (c + 1) * 128)
            has_prev = not (sc == 0 and c == 0)
            obanks = (PB[5], PB[0])
            for h in range(8):
                pr = h // 2
                ps = PB[3 + h // 4]
                B.mm(ps[:, (h % 4) * 128:(h % 4 + 1) * 128], RKm[:, c, h, :], M["RQT"][:, pr, cs], True, True)
            for half in range(2):
                B.tt(ST[:, half * 4:(half + 1) * 4, :], PB[3 + half][:].rearrange("p (h i) -> p h i", h=4),
                     DEC[:, half * 4:(half + 1) * 4, :], ALU.mult)
            for hh in range(2):
                B.tt(AV(RQs2[:], hh * 128, [[256, 4], [1, 128]], p0=hh * 64, pn=64),
                     M["RQT"][hh * 64:(hh + 1) * 64, :, cs], GQ[hh * 64:(hh + 1) * 64, :, :], ALU.mult)
            banks = {(0, 0): PB[1], (0, 1): PB[2], (1, 0): PB[3], (1, 1): PB[4]}
            for g in range(2):
                qv = SQT[:, c, :]
                for blk in range(2):
                    if blk == 0 and not has_prev:
                        continue
                    ps = banks[(g, blk)]
                    B.mm(ps[:], SKm[:, c + blk, g, :], qv, True, False)
                    if blk == 0 and sc == 2 and c == 0:
                        B.mm(ps[:], B.identb[:], FLAGN[:], False, False)
                    B.mm(ps[:], B.identb[:], (MPREV if blk == 0 else MOWN)[:], False, True)
                    B.act(E[:, g * 2 + blk, :], ps[:], AF.Exp, scale=SHD ** -0.5)
            for h in range(8):
                pr = h // 2
                po = obanks[h // 4][:, (h % 4) * 128:(h % 4 + 1) * 128]
                B.mm(po, ST[:, h, :], Vb[:, c, h * 128:(h + 1) * 128], True, False)
                B.mm(po, RQs2[:, h, :], Sb_[:, pr, :], False, True)
            for blk in range(2):
                ps = PB[1 + blk]
                for h in range(4):
                    B.mm(ps[:, h * 128:(h + 1) * 128], MKT[l][:, h, blk * 128:(blk + 1) * 128], M["MQT"][:, h, cs], True, True)
                B.act(EM[:, blk, :], ps[:], AF.Exp, scale=MHD ** -0.5)
            gn_part1(obanks)
            swa_pv(l, c, has_prev, c, c + 1)
            for h in range(4):
                for blk in range(2):
                    B.mm(PB[3][:, h * 128:(h + 1) * 128], MVb[l][:, blk, h * 128:(h + 1) * 128],
                         EM[:, blk, h * 128:(h + 1) * 128], blk == 0, blk == 1)
                for blk in range(2):
                    B.mm(PB[4][:, h * 128:(h + 1) * 128], onesb[:], EM[:, blk, h * 128:(h + 1) * 128], blk == 0, blk == 1)
            gn_part2(c)
            swa_norm(l, c)
            B.recip(DEN[:], PB[4][:])
            B.tt(MOT[:, :, cs], PB[3][:].rearrange("p (a i) -> p a i", a=4),
                 DEN[:].rearrange("p (a i) -> p a i", a=4), ALU.mult)
            for pr in range(4):
                ps = PB[1 + pr % 2]
                B.mm(ps[:, 0:256], RKw[:, c, pr * 128:(pr + 1) * 128], Vb[:, c, pr * 256:(pr + 1) * 256], True, True)
                for hh in range(2):
                    sv = S[hh * 64:(hh + 1) * 64, pr, :]
                    B.stt(sv, sv, GL[hh * 64:(hh + 1) * 64, pr:pr + 1],
                          ps[hh * 64:(hh + 1) * 64, hh * 128:(hh + 1) * 128], ALU.mult, ALU.add)
            B.cp(Sb_[:], S[:], eng="act")
            if last and c == 3:
                B.dma(bass.AP(rs_p.tensor, ver * RH * RDK * RDV, [[128, 128], [2 * RDK * RDV, 4], [1, 128]]), S[:],
                      "o_rs", final=True)
        B.cp(SKTprev[l][:], SKm[:, 4])
        B.cp(SVprev[l][:], SVp[:, 4])
        if stage < 8:
            return
        nxt = sc + 1 < NSLOT
        def mid():
            load_rope(sc + 1)
            prefetch_input_bf16(sc + 1)
        branches_out_mlp(l, 4, mid_hook=mid if nxt else None,
                         tail_hook=xt_from_prefetch if nxt else None)
        B.dma(y_p[sc * 512:(sc + 1) * 512, :].rearrange("(c p) d -> p c d", p=128), X[:], "o_y", final=True)
        if sc < nsc:
            k = sc % 2
            B.dma(cc_src[k].rearrange("(c p) d -> p c d", p=128), X[:], "!ccs%d" % k)
            src_ap, dst_ap = cc_src[k], cc_dst[k]
            if not cfg.get("nocc", False):
              B.T.add("pool", lambda: nc.gpsimd.collective_compute(
                "AllGather", ALU.bypass, replica_groups=[[i, i + 4] for i in range(4)], ins=[src_ap], outs=[dst_ap]),
                w=[dst_ap], r=[src_ap], dma_key="!cc%d" % sc, inc=1)

    def load_sample_consts():
        B.dma(DEC[:], cd["decs"], None)
        B.dma(GQ[:], cd["gqs"], None)
        B.dma(WEND[:], cd["wends"], None)
        B.dma(GL[:], cd["gls"], None)
        B.dma(ROPE[:, 0, 0:1, :], cd["coss"], None)
        B.dma(ROPE[:, 1, 0:1, :], cd["sins"], None)
        B.dma(ROPE[:, 2, 0:1, :], cd["nsins"], None)
        B.dma(BMR[:], cd["bmrow"], None)
        B.dma(identF[:], cd["ident"], None)
        B.dma(T1[:], cd["mnews"], None)
        B.cp(MOWN[:], T1[:])
        B.dma(T1[:], cd["mcache"], None)
        B.cp(MPREV[:], T1[:])

    def reload_prompt_masks():
        B.dma(T1[:], cd["mown"], None)
        B.cp(MOWN[:], T1[:])
        B.dma(T1[:], cd["mprev"], None)
        B.cp(MPREV[:], T1[:])

    def sample_pass(l):
        NB = SB_PER_CORE
        load_layer_params(l)
        if l == 0:
            B.dma(X[:, 0, :], x_s, None)
        B.dma(sk_s[l][:, 0:120, :], c_sk[l][:, 8:128, :], "!cck", final=True)
        B.dma(sv_s[l][:, 0:120, :], c_sv[l][:, 8:128, :], "!ccv", final=True)
        xT_phase(1)

        def new_rows_out():
            for b in range(NB):
                B.dma(sk_s[l, b, 120:128, :], SKf[8 * b:8 * b + 8, 0:128], None, final=True)
                B.dma(sv_s[l, b, 120:128, :], SKf[8 * b:8 * b + 8, 128:256], None, final=True)
        in_proj(l, 1, True, new_rows_out)
        RQT_, MQT_ = M["RQT"], M["MQT"]
        if sstage < 2:
            return
        cs = slice(0, 128)
        for h in range(8):
            pr = h // 2
            ps = PB[3 + h // 4]
            B.mm(ps[:, (h % 4) * 128:(h % 4 + 1) * 128], RKm[:, 0, h, :], RQT_[:, pr, cs], True, True)
        for half in range(2):
            B.tt(ST[:, half * 4:(half + 1) * 4, :], PB[3 + half][:].rearrange("p (h i) -> p h i", h=4),
                 DEC[:, half * 4:(half + 1) * 4, :], ALU.mult)
        for hh in range(2):
            B.tt(AV(RQs2[:], hh * 128, [[256, 4], [1, 128]], p0=hh * 64, pn=64),
                 RQT_[hh * 64:(hh + 1) * 64, :, cs], GQ[hh * 64:(hh + 1) * 64, :, :], ALU.mult)
        ot = (PB[5], PB[0])
        B.memset(ot[0][:], 0.0)
        B.memset(ot[1][:], 0.0)
        for h in range(8):
            B.mm(ot[h // 4][:, (h % 4) * 128:(h % 4 + 1) * 128], Vb[:, 0, h * 128:(h + 1) * 128], ST[:, h, :],
                 False, False, skip=True)
        for grp in range(4):
            b0 = grp * 4
            st_src = bass.AP(st_ret.tensor, (l * NB + b0) * RH * RDK * RDV,
                             [[128, 128], [RH * RDK * RDV, 4], [2 * RDK * RDV, 4], [1, 128]])
            B.dma(S0g, st_src, None)
            B.dma(S0bg, st_src, None, q="pool")
            if grp % 2 == 0:
                rnd = grp // 2
                B.tt(RKwX, AV(RKw[:], 0, [[0, 8], [1, 512]]),
                     AV(BMR[:], rnd * 8, [[1, 8], [0, 512]]), ALU.mult)
            for bl in range(4):
                b = b0 + bl
                for h in range(8):
                    pr = h // 2
                    B.mm(ot[h // 4][:, (h % 4) * 128 + 8 * b:(h % 4) * 128 + 8 * b + 8], S0bg[:, bl, pr, :],
                         RQs2[:, h, 8 * b:8 * b + 8], False, False, skip=True)
                for pr in range(4):
                    ps = PB[1 + pr // 2]
                    B.mm(ps[:, (pr % 2) * 256:(pr % 2 + 1) * 256], RKwX[:, b % 8, pr * 128:(pr + 1) * 128],
                         Vb[:, 0, pr * 256:(pr + 1) * 256], True, True)
                for pr in range(4):
                    ps = PB[1 + pr // 2]
                    for hh in range(2):
                        sv = S0g[hh * 64:(hh + 1) * 64, bl, pr, :]
                        c0 = (pr % 2) * 256 + hh * 128
                        B.stt(sv, sv, GL[hh * 64:(hh + 1) * 64, pr:pr + 1],
                              ps[hh * 64:(hh + 1) * 64, c0:c0 + 128], ALU.mult, ALU.add)
            B.dma(bass.AP(rs_s.tensor, (l * NB + b0) * RH * RDK * RDV,
                          [[128, 128], [RH * RDK * RDV, 4], [2 * RDK * RDV, 4], [1, 128]]), S0g, None, final=True)
        for half in range(2):
            B.cp(Y[:, half * 4:(half + 1) * 4, :], ot[half][:].rearrange("p (h i) -> p h i", h=4), eng="act")
        for h in range(8):
            B.trf(PB[3 + h // 4][:, (h % 4) * 128:(h % 4 + 1) * 128], Y[:, h, :], identF[:])
        group_norm_gate((PB[3], PB[4]), 0, 1)
        if sstage < 3:
            return
        for g in range(2):
            ps = PB[1 + g]
            B.mm(ps[:], SKm[:, 1, g, :], SQT[:, 0, :], True, False)
            B.mm(ps[:], B.identb[:], MOWN[:], False, True)
            B.act(E[:, g * 2 + 1, :], ps[:], AF.Exp, scale=SHD ** -0.5)
        B.memset(SQTs, 0.0)
        for g in range(2):
            B.cp(AV(SQTs, g * 512, [[32, 16], [8, 4], [1, 8]], p0=g * 64, pn=64),
                 AV(SQT[:], 0, [[8, 16], [128, 4], [1, 8]], p0=g * 64, pn=64))
        B.dma(Kraw, c_sk[l].rearrange("b k c -> k b c"), None, q="pool")
        for half in range(2):
            for k in range(8):
                B.tr(PT[half][:, k * 128:(k + 1) * 128], Kraw[:, half * 8 + k, :])
            B.cp(KcT[:, half * 8:(half + 1) * 8, :], PT[half][:].rearrange("p (a b) -> p a b", a=8))
        for half in range(2):
            ps = PB[3 + half]
            B.memset(ps[:], 0.0)
            for b8 in range(8):
                b = half * 8 + b8
                for g in range(2):
                    B.mm(ps[:, (b8 * 2 + g) * 32:(b8 * 2 + g + 1) * 32], KcT[:, b, :],
                         SQTs[:, g, b * 32:(b + 1) * 32], False, False, skip=True)
            B.mm(ps[:], B.identb[:], MPREV[:], False, False, skip=True)
            B.act(Ec[:, half * 8:(half + 1) * 8, :, :].rearrange("p b g c -> p (b g c)"), ps[:], AF.Exp,
                  scale=SHD ** -0.5)
        B.dma(Kraw, c_sv[l].rearrange("b k c -> k b c"), None, q="pool")
        num, den = PB[5], PB[0]
        B.memset(num[:], 0.0)
        B.memset(den[:], 0.0)
        for pr in range(4):
            g = pr // 2
            for hh in range(2):
                t = 2 * (pr % 2) + hh
                B.mm(num[:, pr * 128:(pr + 1) * 128], SVp[:, 1, g, hh, :], E[:, g * 2 + 1, t * 128:(t + 1) * 128],
                     False, False, skip=True)
                B.mm(den[:, pr * 128:(pr + 1) * 128], ONp[:, hh, :], E[:, g * 2 + 1, t * 128:(t + 1) * 128],
                     False, False, skip=True)
        for grp in range(4):
            b0 = grp * 4
            B.memset(Vcp, 0.0)
            for hh in range(2):
                B.cp(AV(Vcp, hh * 128 + hh * 64, [[512, 4], [256, 2], [1, 64]]),
                     AV(Kraw, b0 * 128, [[128, 4], [64, 2], [1, 64]]))
            for bl in range(4):
                b = b0 + bl
                for pr in range(4):
                    g = pr // 2
                    for hh in range(2):
                        t = 2 * (pr % 2) + hh
                        rhs = Ec[:, b, g, t * 8:(t + 1) * 8]
                        B.mm(num[:, pr * 128 + 8 * b:pr * 128 + 8 * b + 8], Vcp[:, bl, g, hh, :], rhs,
                             False, False, skip=True)
                        B.mm(den[:, pr * 128 + 8 * b:pr * 128 + 8 * b + 8], ONp[:, hh, :], rhs,
                             False, False, skip=True)
        for pr in range(4):
            B.ts(DEN[:, pr * 128:(pr + 1) * 128], den[:, pr * 128:(pr + 1) * 128], SINKE[:, l, pr:pr + 1], ALU.add)
        B.recip(DEN[:], DEN[:])
        B.tt(SWOT[:, :, cs], num[:].rearrange("p (a i) -> p a i", a=4),
             DEN[:].rearrange("p (a i) -> p a i", a=4), ALU.mult)
        if sstage < 4:
            return
        Kmraw = Vb[:, 1:3, :].rearrange("p a (k c) -> p (a k) c", k=2)
        KmT = SRG[:, 1:3, :].rearrange("p a (h m) -> p a h m", h=4)
        Vm = RKm[:, 1:3, :, :].rearrange("p a h c -> p (a h c)").rearrange("p (k c) -> p k c", k=4)
        Em = RKw[:, 1, 0:128].rearrange("p (b k h i) -> p b k h i", b=2, k=2, h=4)
        mnum, mden = PB[3], PB[4]
        B.memset(mnum[:], 0.0)
        B.memset(mden[:], 0.0)
        for grp in range(NB // 2):
            b0 = grp * 2
            B.dma(Kmraw, c_mk[l][b0:b0 + 2].rearrange("b (k m) c -> m (b k) c", k=2), None, q="pool")
            B.dma(Vm, c_mv[l][b0:b0 + 2].rearrange("b (k m) c -> m (b k) c", k=2), None, q="pool")
            sc_ps = PB[1 + grp % 2]
            B.memset(sc_ps[:, 0:128], 0.0)
            for bl in range(2):
                b = b0 + bl
                for blk in range(2):
                    for h in range(4):
                        B.tr(PT[bl][:, (h * 2 + blk) * 128:(h * 2 + blk + 1) * 128],
                             Kmraw[:, bl * 2 + blk, h * 128:(h + 1) * 128])
                B.cp(KmT[:, bl, :, :].rearrange("p h m -> p (h m)"), PT[bl][:])
                for blk in range(2):
                    for h in range(4):
                        c0 = ((bl * 2 + blk) * 4 + h) * 8
                        B.mm(sc_ps[:, c0:c0 + 8], KmT[:, bl, h, blk * 128:(blk + 1) * 128],
                             MQT_[:, h, 8 * b:8 * b + 8], False, False, skip=True)
            B.act(Em.rearrange("p b k h i -> p (b k h i)"), sc_ps[:, 0:128], AF.Exp, scale=MHD ** -0.5)
            for bl in range(2):
                b = b0 + bl
                for blk in range(2):
                    for h in range(4):
                        rhs = Em[:, bl, blk, h, :]
                        B.mm(mnum[:, h * 128 + 8 * b:h * 128 + 8 * b + 8], Vm[:, bl * 2 + blk, h * 128:(h + 1) * 128],
                             rhs, False, False, skip=True)
                        B.mm(mden[:, h * 128 + 8 * b:h * 128 + 8 * b + 8], onesb[:], rhs, False, False, skip=True)
        B.recip(DEN[:], mden[:])
        B.tt(MOT[:, :, cs], mnum[:].rearrange("p (a i) -> p a i", a=4),
             DEN[:].rearrange("p (a i) -> p a i", a=4), ALU.mult)
        if sstage < 5:
            return
        branches_out_mlp(l, 1)
        if l == nlay - 1:
            B.dma(y_s, X[:, 0, :], None, final=True)

    passes = []
    if do_sample:
        for l in range(nlay):
            passes.append(("s", 0, l))
    for sc in range(NSLOT):
        passes.append(("p", sc, 2))
    for kind, sc, l in passes:
        wstate["plan"] += wplan_for_pass(l, kind == "p" and sc == 0)
    if do_sample:
        load_sample_consts()
    else:
        load_prompt_consts()
    for pi, (kind, sc, l) in enumerate(passes):
        T.phase = pi
        if kind == "s":
            M.update(SV_)
            sample_pass(l)
            if l == nlay - 1:
                M.update(PV_)
                B.memset(RKm[:], 0.0)
                load_prompt_consts()
                reload_prompt_masks()
        elif not cfg.get("noprompt", False):
            prompt_pass(sc, l)
    if stage >= 99 and sstage >= 99 and not cfg.get("noprompt", False):
        assert wstate["idx"] == len(wstate["plan"]), (wstate["idx"], len(wstate["plan"]))
    if dbgname is not None:
        dbg = B.dout("dbg", [128, 4096])
        DBG = X[:].rearrange("p a b -> p (a b)")
        srcs = dict(XT=XT[:], RQT=RQT, RKm=RKm[:, 0], SQT=SQT[:], SKm=SKm[:], Vb=Vb[:, :, :], SRG=SRG[:], MQT=MQT, GT=GT[:, 0:8, :],
                    GRT=GRT[:], SWOT=SWOT[:], MOT=MOT[:], MT=MT, X=X[:], RKw=RKw[:], MKT=MKT[0][:], MVb=MVb[0][:],
                    MEMT=MEMT, HT=HT[:, 0:8, :], S=Sst[0][:], E=E[:], EM=EM[:])[dbgname]
        n = 1
        for s_ in srcs.shape[1:]:
            n *= s_
        dims = "abcd"[:len(srcs.shape) - 1]
        flat = srcs.rearrange("p %s -> p (%s)" % (" ".join(dims), " ".join(dims))) if len(dims) > 1 else srcs
        if dbgname != "X":
            B.cp(DBG[:, 0:n], flat)
        B.dma(dbg[:, 0:n], DBG[:, 0:n], "o_dbg", final=True)
    T.emit(B.final_ops)
    return B


_CACHE = {}


def _core_inputs(inp, core, nsc, hc):
    f = lambda a: np.ascontiguousarray(np.asarray(a, dtype=np.float32))
    b = core % BATCH
    L = core // BATCH
    sb0 = core * SB_PER_CORE
    sl = slice(sb0, sb0 + SB_PER_CORE)
    nslot = nsc + 2
    m = {
        "x_p": f(inp["x_prompt"][b, :nsc * 512]),
        "mem_p": f(inp["mem_prompt"][b]),
        "x_s": f(np.asarray(inp["x_sample"])[sl].reshape(SB_PER_CORE * DEC_SEQ, D)),
        "st_ret": f(np.asarray(inp["state_ret"])[:, sl]),
        "c_sk": f(np.asarray(inp["cache_swa_k"])[:, sl].reshape(DEPTH, SB_PER_CORE, 128, 128)),
        "c_sv": f(np.asarray(inp["cache_swa_v"])[:, sl].reshape(DEPTH, SB_PER_CORE, 128, 128)),
        "c_mk": f(np.asarray(inp["cache_mem_k"])[:, sl].reshape(DEPTH, SB_PER_CORE, NMEM, 512)),
        "c_mv": f(np.asarray(inp["cache_mem_v"])[:, sl].reshape(DEPTH, SB_PER_CORE, NMEM, 512)),
        "sinks": f(inp["attn_sinks"]),
        "gn_g": f(np.asarray(inp["ret_gn_g"]).reshape(DEPTH, 1024)),
    }
    chunk_of = []
    for s in range(nslot):
        for c in range(4):
            ch = (4 * s + c) if L == 0 else (4 * (s - 2) + c)
            if ch < 0 or ch >= nsc * 4:
                ch = 0
            chunk_of.append(ch)
    for nm, key in (("r_cos", "cosp"), ("r_sin", "sinp"), ("r_nsin", "nsinp")):
        m[nm] = np.ascontiguousarray(hc[key][:, chunk_of, :])
    sel = np.zeros((128, 2), np.float32)
    sel[:, L] = 1.0
    m["sel"] = sel
    m["flagneg"] = np.full((128, 512), NEG if L == 1 else 0.0, np.float32)
    return m


_OWN = (("wo_in", "w_in"), ("wo_br_ret", "w_br_ret"), ("wo_br_swa", "w_br_swa"), ("wo_br_mem", "w_br_mem"),
        ("wo_out", "w_out"), ("wo_mem_kv", "w_mem_kv"), ("wo_up", "w_up"), ("wo_down", "w_down"),
        ("o_ln1_g", "ln1_g"), ("o_ln1_b", "ln1_b"), ("o_ln2_g", "ln2_g"), ("o_ln2_b", "ln2_b"))
_FULL = ("w_in", "w_br_ret", "w_br_swa", "w_br_mem", "w_out", "w_mem_kv", "ln1_g", "ln1_b", "w_up", "w_down",
         "ln2_g", "ln2_b")


def run_cores(inp, cfg, cores):
    key = tuple(sorted(cfg.items()))
    if key not in _CACHE:
        _CACHE[key] = build(cfg)
    B = _CACHE[key]
    hc = host_consts()
    f = lambda a: np.ascontiguousarray(np.asarray(a, dtype=np.float32))
    full = {k: f(inp[k]) for k in _FULL}
    ownl = []
    for L in range(DEPTH):
        d = {dst: np.ascontiguousarray(full[srck][L]) for dst, srck in _OWN}
        d["o_sinks"] = f(inp["attn_sinks"])[L:L + 1]
        d["o_gn_g"] = f(np.asarray(inp["ret_gn_g"]).reshape(DEPTH, 1024))[L]
        ownl.append(d)
    in_maps = []
    for core in cores:
        m = _core_inputs(inp, core, cfg["nsc"], hc)
        m.update(full)
        m.update(ownl[core // BATCH])
        for k, v in hc.items():
            m["c_" + k] = v
        in_maps.append(m)
    res = run_bass_kernel_spmd(B.nc, in_maps, core_ids=list(range(len(cores))))
    return res.results


def assemble(r, nsc):
    S = nsc * 512
    y_p = np.stack([r[BATCH + b]["y_p"][2 * 512:2 * 512 + S] for b in range(BATCH)]).astype(np.float32)
    y_s = np.concatenate([r[c]["y_s"].reshape(SB_PER_CORE, DEC_SEQ, D) for c in range(NCORES)]).astype(np.float32)

    def per_layer(name, shape, versioned):
        out = []
        for L in range(DEPTH):
            row = []
            for b in range(BATCH):
                a = r[L * BATCH + b][name]
                a = a[L] if versioned else a
                row.append(np.asarray(a).reshape(shape))
            out.append(np.stack(row))
        return np.stack(out).astype(np.float32)
    rs_p = per_layer("rs_p", (RH, RDK, RDV), True)
    sk_p = per_layer("sk_p", (128, SKV, SHD), True)
    sv_p = per_layer("sv_p", (128, SKV, SHD), True)
    mk_p = per_layer("mk_p", (NMEM, MH, MHD), False)
    mv_p = per_layer("mv_p", (NMEM, MH, MHD), False)
    rs_s = np.concatenate([r[c]["rs_s"] for c in range(NCORES)], axis=1).astype(np.float32)
    sk_s = np.concatenate([r[c]["sk_s"].reshape(DEPTH, SB_PER_CORE, 128, SKV, SHD) for c in range(NCORES)],
                          axis=1).astype(np.float32)
    sv_s = np.concatenate([r[c]["sv_s"].reshape(DEPTH, SB_PER_CORE, 128, SKV, SHD) for c in range(NCORES)],
                          axis=1).astype(np.float32)
    return (y_p, y_s, rs_p, sk_p, sv_p, mk_p, mv_p, rs_s, sk_s, sv_s)


def kernel(**inp):
    cfg = dict(nsc=SEQ // 512, nlay=DEPTH, sample=True)
    r = run_cores(inp, cfg, list(range(NCORES)))
    return assemble(r, cfg["nsc"])
```
